# Optimizing a Trainium2 kernel written in Bass

```python
import jax, jax.numpy as jnp
from jax import lax
import numpy as np

D_MODEL = 1024
BATCH = 2
SEQ = 16384
DEPTH = 1
DEC_BATCH = 32
DEC_SEQ = 64
PAST_LEN = 4096

CHUNK = 64
GMLP_CHUNK = 128
GMLP_GROUPS = 8
GMLP_WIDTH = 1024
GMLP_GROUP_DIM = GMLP_WIDTH // GMLP_GROUPS
MLA_HEADS = 8
Q_LORA = 512
KV_LORA = 512
NOPE_DIM = 128
ROPE_DIM = 64
V_DIM = 128
QK_DIM = NOPE_DIM + ROPE_DIM
ROPE_BASE = 10000.0
ATTN_SCALE = QK_DIM ** -0.5
Q_BLOCK = 128
NEG_INF = -1e30
D_FF = 2816
CONV_W = 3
EPS = 1e-6
OFF_Q = 2 * GMLP_WIDTH
OFF_KV = OFF_Q + Q_LORA
OFF_GATE = OFF_KV + KV_LORA + ROPE_DIM
IN_COLS = OFF_GATE + 2 * D_MODEL

kernel_name = "hybrid_gmlp_mla_convffn_stream_step"


def _rmsnorm(x, g):
    xf = x.astype(jnp.float32)
    y = xf * lax.rsqrt(jnp.mean(xf * xf, axis=-1, keepdims=True) + EPS)
    return (y * g.astype(jnp.float32)).astype(x.dtype)


def _layernorm(x, g, b):
    xf = x.astype(jnp.float32)
    xc = xf - jnp.mean(xf, axis=-1, keepdims=True)
    var = jnp.mean(xc * xc, axis=-1, keepdims=True)
    return (xc * lax.rsqrt(var + EPS) * g.astype(jnp.float32) + b.astype(jnp.float32)).astype(x.dtype)


def _rope(x, pos):
    half = ROPE_DIM // 2
    inv_freq = ROPE_BASE ** (-jnp.arange(half, dtype=jnp.float32) / half)
    ang = pos.astype(jnp.float32)[:, None] * inv_freq[None, :]
    ang = ang.reshape(ang.shape[:1] + (1,) * (x.ndim - 3) + (half,))
    cos, sin = jnp.cos(ang), jnp.sin(ang)
    xf = x.astype(jnp.float32)
    x1, x2 = xf[..., :half], xf[..., half:]
    return jnp.concatenate([x1 * cos - x2 * sin, x1 * sin + x2 * cos], axis=-1).astype(x.dtype)


def _gmlp_branch(uv, ln_g, ln_b, w_s, b_s):
    u, v = jnp.split(jax.nn.gelu(uv), 2, axis=-1)
    v = _layernorm(v, ln_g, ln_b)
    bsz, L, _ = v.shape
    n = min(L, GMLP_CHUNK)
    causal = jnp.tril(jnp.ones((n, n), dtype=bool))
    w = jnp.where(causal, w_s[:, :n, :n], 0)
    vc = v.reshape(bsz, L // n, n, GMLP_GROUPS, GMLP_GROUP_DIM)
    s = jnp.einsum('gts,bnsgc->bntgc', w, vc) + b_s[:, :n].T[:, :, None]
    return u * s.reshape(bsz, L, GMLP_WIDTH), v


def _mla_qkv(q_lat, kv_lat, pos, q_norm_g, w_uq, kv_norm_g):
    q = jnp.einsum('bsr,rhd->bshd', _rmsnorm(q_lat, q_norm_g), w_uq)
    q = jnp.concatenate([q[..., :NOPE_DIM], _rope(q[..., NOPE_DIM:], pos)], axis=-1)
    c_kv = _rmsnorm(kv_lat[..., :KV_LORA], kv_norm_g)
    k_rope = _rope(kv_lat[..., KV_LORA:], pos)
    return q, c_kv, k_rope


def _expand_kv(c_kv, k_rope, w_uk, w_uv):
    k_nope = jnp.einsum('bsc,chd->bshd', c_kv, w_uk)
    k_r = jnp.broadcast_to(k_rope[:, :, None, :], k_nope.shape[:3] + (ROPE_DIM,))
    v = jnp.einsum('bsc,chd->bshd', c_kv, w_uv)
    return jnp.concatenate([k_nope, k_r], axis=-1), v


def _attend(q, q_pos, k, v, k_pos):
    s = jnp.einsum('bqhd,bkhd->bhqk', q, k).astype(jnp.float32) * ATTN_SCALE
    visible = (k_pos[None, :] // CHUNK) <= (q_pos[:, None] // CHUNK)
    s = jnp.where(visible, s, NEG_INF)
    p = jax.nn.softmax(s, axis=-1).astype(v.dtype)
    return jnp.einsum('bhqk,bkhd->bqhd', p, v)


def _conv_ffn(h, prev, w_up, conv_w, conv_b, w_down):
    up = h @ w_up
    L = up.shape[1]
    full = jnp.concatenate([prev, up], axis=1)
    c = conv_b + sum(full[:, i:i + L] * conv_w[i] for i in range(CONV_W))
    val, gate = jnp.split(c, 2, axis=-1)
    return (jax.nn.silu(gate) * val) @ w_down, full[:, L:]


def _layer(x, pos, past_ckv, past_krope, past_conv,
           norm_mix_g, w_in, gmlp_ln_g, gmlp_ln_b, gmlp_w_s, gmlp_b_s,
           mla_q_norm_g, mla_w_uq, mla_kv_norm_g, mla_w_uk, mla_w_uv,
           w_proj_a, w_proj_b, w_out, norm_ffn_g, ffn_w_up, ffn_conv_w, ffn_conv_b, ffn_w_down):
    bsz, L, _ = x.shape
    z = _rmsnorm(x, norm_mix_g) @ w_in
    gates = jax.nn.sigmoid(z[..., OFF_GATE:])
    a_out, v_rows = _gmlp_branch(z[..., :OFF_Q], gmlp_ln_g, gmlp_ln_b, gmlp_w_s, gmlp_b_s)
    q, c_kv, k_rope = _mla_qkv(z[..., OFF_Q:OFF_KV], z[..., OFF_KV:OFF_GATE], pos,
                               mla_q_norm_g, mla_w_uq, mla_kv_norm_g)
    if past_ckv is None:
        k, v = _expand_kv(c_kv, k_rope, mla_w_uk, mla_w_uv)
        nb = L // Q_BLOCK
        qb = jnp.moveaxis(q.reshape(bsz, nb, Q_BLOCK, MLA_HEADS, QK_DIM), 1, 0)
        pb = pos.reshape(nb, Q_BLOCK)
        ob = lax.map(lambda qp: _attend(qp[0], qp[1], k, v, pos), (qb, pb))
        attn = jnp.moveaxis(ob, 0, 1).reshape(bsz, L, MLA_HEADS * V_DIM)
    else:
        k, v = _expand_kv(jnp.concatenate([past_ckv, c_kv], axis=1),
                          jnp.concatenate([past_krope, k_rope], axis=1), mla_w_uk, mla_w_uv)
        k_pos = jnp.arange(past_ckv.shape[1] + L, dtype=jnp.int32)
        attn = _attend(q, pos, k, v, k_pos).reshape(bsz, L, MLA_HEADS * V_DIM)
    merged = gates[..., :D_MODEL] * (a_out @ w_proj_a) + gates[..., D_MODEL:] * (attn @ w_proj_b)
    h = x + merged @ w_out
    ffn_out, conv_rows = _conv_ffn(_rmsnorm(h, norm_ffn_g), past_conv,
                                   ffn_w_up, ffn_conv_w, ffn_conv_b, ffn_w_down)
    return h + ffn_out, c_kv, k_rope, conv_rows, v_rows


def setup_inputs(seed: int = 0) -> dict:
    key = jax.random.key(seed)
    ks = jax.random.split(key, 32)

    def nrm(k, shape, scale=1.0):
        return jax.random.normal(k, shape, jnp.float32) * scale

    return {
        "x_prompt": nrm(ks[0], (BATCH, SEQ, D_MODEL)),
        "x_sample": nrm(ks[1], (DEC_BATCH, DEC_SEQ, D_MODEL)),
        "cache_mla_ckv": nrm(ks[2], (DEPTH, DEC_BATCH, PAST_LEN, KV_LORA)),
        "cache_mla_krope": nrm(ks[3], (DEPTH, DEC_BATCH, PAST_LEN, ROPE_DIM)),
        "state_ffn_conv": nrm(ks[4], (DEPTH, DEC_BATCH, CONV_W - 1, 2 * D_FF)),
        "norm_mix_g": 1.0 + nrm(ks[5], (DEPTH, D_MODEL), 0.02),
        "w_in": nrm(ks[6], (DEPTH, D_MODEL, IN_COLS), D_MODEL ** -0.5),
        "gmlp_ln_g": 1.0 + nrm(ks[7], (DEPTH, GMLP_WIDTH), 0.02),
        "gmlp_ln_b": nrm(ks[8], (DEPTH, GMLP_WIDTH), 0.02),
        "gmlp_w_s": nrm(ks[9], (DEPTH, GMLP_GROUPS, GMLP_CHUNK, GMLP_CHUNK), GMLP_CHUNK ** -0.5),
        "gmlp_b_s": 1.0 + nrm(ks[10], (DEPTH, GMLP_GROUPS, GMLP_CHUNK), 0.02),
        "mla_q_norm_g": 1.0 + nrm(ks[11], (DEPTH, Q_LORA), 0.02),
        "mla_w_uq": nrm(ks[12], (DEPTH, Q_LORA, MLA_HEADS, QK_DIM), Q_LORA ** -0.5),
        "mla_kv_norm_g": 1.0 + nrm(ks[13], (DEPTH, KV_LORA), 0.02),
        "mla_w_uk": nrm(ks[14], (DEPTH, KV_LORA, MLA_HEADS, NOPE_DIM), KV_LORA ** -0.5),
        "mla_w_uv": nrm(ks[15], (DEPTH, KV_LORA, MLA_HEADS, V_DIM), KV_LORA ** -0.5),
        "w_proj_a": nrm(ks[16], (DEPTH, GMLP_WIDTH, D_MODEL), GMLP_WIDTH ** -0.5),
        "w_proj_b": nrm(ks[17], (DEPTH, MLA_HEADS * V_DIM, D_MODEL), (MLA_HEADS * V_DIM) ** -0.5),
        "w_out": nrm(ks[18], (DEPTH, D_MODEL, D_MODEL), D_MODEL ** -0.5),
        "norm_ffn_g": 1.0 + nrm(ks[19], (DEPTH, D_MODEL), 0.02),
        "ffn_w_up": nrm(ks[20], (DEPTH, D_MODEL, 2 * D_FF), D_MODEL ** -0.5),
        "ffn_conv_w": nrm(ks[21], (DEPTH, CONV_W, 2 * D_FF), CONV_W ** -0.5),
        "ffn_conv_b": nrm(ks[22], (DEPTH, 2 * D_FF), 0.02),
        "ffn_w_down": nrm(ks[23], (DEPTH, D_FF, D_MODEL), D_FF ** -0.5),
        "final_norm_g": 1.0 + nrm(ks[24], (D_MODEL,), 0.02),
    }


def reference(x_prompt, x_sample, cache_mla_ckv, cache_mla_krope, state_ffn_conv,
              norm_mix_g, w_in, gmlp_ln_g, gmlp_ln_b, gmlp_w_s, gmlp_b_s,
              mla_q_norm_g, mla_w_uq, mla_kv_norm_g, mla_w_uk, mla_w_uv,
              w_proj_a, w_proj_b, w_out, norm_ffn_g, ffn_w_up, ffn_conv_w, ffn_conv_b, ffn_w_down,
              final_norm_g):
    seq = x_prompt.shape[1]
    dec_seq = x_sample.shape[1]
    past_len = cache_mla_ckv.shape[2]
    pos_p = jnp.arange(seq, dtype=jnp.int32)
    pos_s = past_len + jnp.arange(dec_seq, dtype=jnp.int32)
    conv_zero = jnp.zeros((x_prompt.shape[0], CONV_W - 1, 2 * D_FF), x_prompt.dtype)

    y_p, y_s = x_prompt, x_sample
    ckv_p, kr_p, cv_p, ckv_s, kr_s, cv_s, gv_s = [], [], [], [], [], [], []
    for l in range(DEPTH):
        w = (norm_mix_g[l], w_in[l], gmlp_ln_g[l], gmlp_ln_b[l], gmlp_w_s[l], gmlp_b_s[l],
             mla_q_norm_g[l], mla_w_uq[l], mla_kv_norm_g[l], mla_w_uk[l], mla_w_uv[l],
             w_proj_a[l], w_proj_b[l], w_out[l], norm_ffn_g[l], ffn_w_up[l], ffn_conv_w[l],
             ffn_conv_b[l], ffn_w_down[l])
        y_p, c1, k1, v1, _ = _layer(y_p, pos_p, None, None, conv_zero, *w)
        y_s, c2, k2, v2, g2 = _layer(y_s, pos_s, cache_mla_ckv[l], cache_mla_krope[l],
                                     state_ffn_conv[l], *w)
        ckv_p.append(c1); kr_p.append(k1); cv_p.append(v1)
        ckv_s.append(c2); kr_s.append(k2); cv_s.append(v2); gv_s.append(g2)

    y_prompt = _rmsnorm(y_p, final_norm_g)
    y_sample = _rmsnorm(y_s, final_norm_g)
    return (y_prompt, y_sample,
            jnp.stack(ckv_p), jnp.stack(kr_p), jnp.stack(cv_p),
            jnp.stack(ckv_s), jnp.stack(kr_s), jnp.stack(cv_s), jnp.stack(gv_s))
```

```python
import contextlib
import math
import numpy as np
import ml_dtypes
import concourse.bass as bass
import concourse.mybir as mybir
from concourse.bass_utils import run_bass_kernel_spmd

F32 = mybir.dt.float32
BF16 = mybir.dt.bfloat16
AF = mybir.ActivationFunctionType
ALU = mybir.AluOpType

D = 1024
GW = 1024
QL = 512
KVL = 512
ROPE = 64
NOPE = 128
NH = 8
DFF = 2816
NFC = 44
NPAIR = 22
EPS = 1e-6
ATT_SCALE = (NOPE + ROPE) ** -0.5
BIG = 30000.0
PW = 8192
KVC = 640


class Cfg:
    def __init__(s, SEQ=16384, W=512, NJ=4, PAST=4096, NSB=4, DS=64, NBATCH=2, DEC_BATCH=32):
        s.SEQ, s.W, s.NJ, s.PAST, s.NSB, s.DS = SEQ, W, NJ, PAST, NSB, DS
        s.NBATCH, s.DEC_BATCH = NBATCH, DEC_BATCH
        s.B = 2 * W
        s.B8 = s.B // 128
        s.W8 = W // 128
        assert 4 * NJ * s.B == SEQ
        s.NBP = SEQ // 128
        s.NBC = PAST // 128
        s.WS = NSB * DS
        assert s.WS % 128 == 0
        s.NBS_NEW = s.WS // 128
        s.NBLK = s.NBP + NSB * s.NBC + s.NBS_NEW
        s.OWN = NJ * (128 + s.B)
        s.NMM = 3 * s.B8 + s.W8
        assert s.NBP % 4 == 0 and s.NBC % 4 == 0 and s.NBS_NEW <= 4
        s.NGP = s.NBP // 4
        s.NGC = s.NBC // 4
        s.GRP_NEW = s.NGP + NSB * s.NGC
        s.NGRP = s.GRP_NEW + 1

    def tiles(s):
        out = []
        for J in range(s.NJ):
            base = J * (128 + s.B)
            nk = s.B8 * (4 * J + 3)
            out.append(dict(kind="halo", J=J, i=-1, W=128, tok0=base, nkb=nk,
                            mlo=max(0, s.B8 * 4 * J - 1)))
            for i in range(2):
                nk2 = nk + s.W8 * (i + 1)
                out.append(dict(kind="tile", J=J, i=i, W=s.W, tok0=base + 128 + i * s.W, nkb=nk2,
                                mlo=nk2 - s.NMM))
        return out


class Op:
    __slots__ = ("id", "eng", "fn", "deps", "sem", "inc", "has_dep", "cnt", "waits", "ph")


ENGS = ("pe", "act", "dve", "pool", "sp")


class Sched:
    def __init__(self, nc):
        self.nc = nc
        self.ops = []
        self.last_w = {}
        self.readers = {}
        self.floor = None
        self.last_eng = {}
        self.dma_since = []
        self.ph = "init"
        self.scopes = False
        self.soft = None
        self.alias_names = set()
        self.last_dma = {}

    def op(self, eng, fn, reads=(), writes=(), dma_sem=None, extra=()):
        o = Op()
        o.id = len(self.ops)
        o.eng = eng
        o.fn = fn
        o.sem = dma_sem
        o.has_dep = False
        o.ph = self.ph
        deps = {}
        is_dma = dma_sem is not None

        def add(d, raw):
            if d is None:
                return
            p = self.ops[d]
            if (not raw) and (not is_dma) and p.sem is None and p.eng == eng and eng == "pe":
                return
            deps[d] = True

        for b in reads:
            for d in self.last_w.get(b, {}).values():
                add(d, True)
        for b in writes:
            for d in self.last_w.get(b, {}).values():
                add(d, False)
            for d in self.readers.get(b, {}).values():
                add(d, False)
        for d in extra:
            deps[d] = True
        if self.soft is not None:
            for b in writes:
                nm = b if isinstance(b, str) else b[0]
                if nm in self.alias_names:
                    for d in self.soft:
                        deps[d] = True
                    break
        if self.floor is not None:
            deps[self.floor] = True
        deps.pop(o.id, None)
        o.deps = list(deps)
        who = dma_sem if is_dma else eng
        for b in reads:
            self.readers.setdefault(b, {})[who] = o.id
        for b in writes:
            self.last_w.setdefault(b, {})[who] = o.id
        self.ops.append(o)
        self.last_eng[eng] = o.id
        if is_dma:
            self.dma_since.append(o.id)
            self.last_dma[dma_sem] = o.id
        return o.id

    def soft_barrier(self):
        self.soft = list(self.last_eng.values()) + list(self.last_dma.values())

    def barrier(self):
        extra = list(self.last_eng.values()) + list(self.dma_since)
        self.floor = None
        bid = self.op("sp", lambda e: e.nop(), extra=extra)
        self.floor = bid
        self.dma_since = []
        self.last_w = {}
        self.readers = {}
        self.soft = None
        return bid

    def emit(self, sems):
        ops = self.ops
        for o in ops:
            for d in o.deps:
                ops[d].has_dep = True
        cnt = {}
        waited = {e: {} for e in ENGS}
        per_eng = {e: [] for e in ENGS}
        n_wait = 0
        for o in ops:
            need = {}
            for d in o.deps:
                p = ops[d]
                is_dma = p.sem is not None
                if (not is_dma) and p.eng == "pe" and o.eng == "pe":
                    continue
                key = p.sem if is_dma else sems[p.eng]
                v = p.cnt
                if need.get(key, 0) < v:
                    need[key] = v
            o.waits = []
            for key, v in need.items():
                if waited[o.eng].get(key, 0) >= v:
                    continue
                o.waits.append((key, v))
                waited[o.eng][key] = v
                n_wait += 1
            if o.sem is not None:
                c = cnt.get(o.sem, 0) + 16
                cnt[o.sem] = c
                o.cnt = c
                o.inc = (o.sem, 16)
            elif o.has_dep:
                key = sems[o.eng]
                c = cnt.get(key, 0) + 1
                cnt[key] = c
                o.cnt = c
                o.inc = (key, 1)
            else:
                o.cnt = None
                o.inc = None
            per_eng[o.eng].append(o)

        def run1(h, o):
            for key, v in o.waits:
                h.wait_ge(key, v)
            ins = o.fn(h)
            if o.inc is not None:
                ins.then_inc(o.inc[0], o.inc[1])

        def run(h, lst):
            if not self.scopes:
                for o in lst:
                    run1(h, o)
                return
            i = 0
            while i < len(lst):
                j = i
                while j < len(lst) and lst[j].ph == lst[i].ph:
                    j += 1
                with self.nc.named_scope(lst[i].ph):
                    for o in lst[i:j]:
                        run1(h, o)
                i = j

        with self.nc.Block() as block:
            @block.tensor
            def _(e):
                run(e, per_eng["pe"])

            @block.scalar
            def _(e):
                run(e, per_eng["act"])

            @block.vector
            def _(e):
                run(e, per_eng["dve"])

            @block.gpsimd
            def _(e):
                run(e, per_eng["pool"])

            @block.sync
            def _(e):
                run(e, per_eng["sp"])
        return n_wait


class Arena:
    def __init__(self, tile, nwords):
        self.t = tile
        self.n = nwords
        self.tops = {"": 0}
        self.parent = {}
        self.hi = 0

    def phase(self, name, parent=""):
        self.parent[name] = parent
        self.tops[name] = None

    def _top(self, ph):
        if self.tops[ph] is None:
            self.tops[ph] = self._top(self.parent[ph])
        return self.tops[ph]

    def alloc(self, ph, shape, dt):
        nel = 1
        for d in shape[1:]:
            nel *= d
        nbytes = nel * (4 if dt == F32 else 2)
        nw = (nbytes + 3) // 4
        nw = (nw + 7) // 8 * 8
        off = self._top(ph)
        assert off + nw <= self.n, ("SBUF arena overflow", ph, shape, off, nw, self.n)
        self.tops[ph] = off + nw
        self.hi = max(self.hi, off + nw)
        ap = self.t[:, off:off + nw]
        if dt != F32:
            ap = ap.bitcast(dt)
        ap = ap[:, 0:nel]
        if len(shape) == 3:
            ap = ap.rearrange("p (a b) -> p a b", a=shape[1])
        elif len(shape) == 4:
            ap = ap.rearrange("p (a b c) -> p a b c", a=shape[1], b=shape[2])
        if shape[0] != 128:
            ap = ap[0:shape[0]]
        return ap


def _bf16_bits(a):
    return np.ascontiguousarray(np.asarray(a, np.float32).astype(ml_dtypes.bfloat16)).view(np.uint16)


def _piece(arr):
    K, nc_ = arr.shape
    nkc = K // 128
    a = arr.reshape(nkc, 128, nc_).transpose(1, 0, 2).reshape(128, nkc * nc_)
    out = np.zeros((128, PW), np.float32)
    out[:, :nkc * nc_] = a
    return out, nkc, nc_


def _gtab(g, nkc):
    t = np.ones((128, 12), np.float32)
    if g is not None:
        t[:, :nkc] = np.asarray(g, np.float32).reshape(nkc, 128).T
    return t


def weight_pieces(inp):
    w_in = inp["w_in"][0]
    mix = inp["norm_mix_g"][0]
    qn = inp["mla_q_norm_g"][0]
    ffn = inp["norm_ffn_g"][0]
    w_uq = inp["mla_w_uq"][0]
    w_up = inp["ffn_w_up"][0]
    w_dn = inp["ffn_w_down"][0]
    perm = (np.arange(64) + 32) % 64
    rope = w_uq[:, :, 128:192]
    ropep = rope[:, :, perm]
    specs = [
        (w_in[:, 2560:3136], mix),
        (inp["mla_w_uk"][0].reshape(512, 1024), None),
        (inp["mla_w_uv"][0].reshape(512, 1024), None),
        (w_in[:, 0:1024], mix),
        (w_in[:, 1024:2048], mix),
        (w_in[:, 2048:2560], mix),
        (w_in[:, 3136:4160], mix),
        (w_in[:, 4160:5184], mix),
        (w_uq[:, :, 0:128].reshape(512, 1024), qn),
        (np.concatenate([rope, ropep, ropep, rope], axis=2).reshape(512, 2048), qn),
        (np.concatenate([inp["w_proj_a"][0][:, 0:512], inp["w_proj_b"][0][:, 0:512]], axis=1), None),
        (np.concatenate([inp["w_proj_a"][0][:, 512:1024], inp["w_proj_b"][0][:, 512:1024]], axis=1), None),
        (inp["w_out"][0], None),
    ]
    for t in range(6):
        cols = []
        for p in range(4 * t, min(4 * t + 4, NPAIR)):
            cols.append(w_up[:, p * 128:(p + 1) * 128])
            cols.append(w_up[:, DFF + p * 128:DFF + (p + 1) * 128])
        specs.append((np.concatenate(cols, axis=1), ffn))
    for half in range(2):
        for part in range(2):
            specs.append((w_dn[part * 1408:(part + 1) * 1408, half * 512:(half + 1) * 512], None))
    wraw = np.zeros((len(specs), 128, PW), np.float32)
    wgt = np.ones((128, len(specs), 12), np.float32)
    meta = []
    for i, (a, g) in enumerate(specs):
        wraw[i], nkc, nc_ = _piece(np.ascontiguousarray(a, dtype=np.float32))
        wgt[:, i, :] = _gtab(g, nkc)
        meta.append((nkc, nc_))
    return wraw, wgt, meta


PIECE_META = [(8, 576), (4, 1024), (4, 1024), (8, 1024), (8, 1024), (8, 512), (8, 1024), (8, 1024),
              (4, 1024), (4, 2048), (8, 1024), (8, 1024), (8, 1024),
              (8, 1024), (8, 1024), (8, 1024), (8, 1024), (8, 1024), (8, 512),
              (11, 512), (11, 512), (11, 512), (11, 512)]
NPW = len(PIECE_META)
P_KV, P_UK, P_UV, P_U, P_V, P_Q, P_G0, P_G1, P_UQN, P_UQR, P_AB0, P_AB1, P_O, P_UP0, P_D0 = \
    0, 1, 2, 3, 4, 5, 6, 7, 8, 9, 10, 11, 12, 13, 19
TILE_PIECES = list(range(3, NPW))


def _cs_tables(pos):
    half = ROPE // 2
    inv = (np.float32(10000.0) ** (-np.arange(half, dtype=np.float32) / np.float32(half))).astype(np.float32)
    ang = pos.astype(np.float32)[:, None] * inv[None, :]
    return np.cos(ang).astype(np.float32), np.sin(ang).astype(np.float32)


def _mask_rows(cfg, qstart, W, blocks, fake=False):
    out = np.zeros((8, cfg.NMM * 128), np.float32)
    if fake:
        return out
    for u, n in enumerate(blocks):
        kc = (n * 128 + np.arange(128)) // 64
        for j in range(W // 64):
            qc = qstart // 64 + j
            out[j, u * 128:(u + 1) * 128] = np.where(kc > qc, -BIG, 0.0)
    return out


def host_prep(inp, cfg):
    wraw, wgt, meta = weight_pieces(inp)
    assert meta == PIECE_META, meta
    xp = np.asarray(inp["x_prompt"], np.float32)
    xs = np.asarray(inp["x_sample"], np.float32)
    ckv_c = np.asarray(inp["cache_mla_ckv"], np.float32)[0]
    kr_c = np.asarray(inp["cache_mla_krope"], np.float32)[0]
    cst = np.asarray(inp["state_ffn_conv"], np.float32)[0]
    tl = cfg.tiles()
    shared = {}
    shared["wraw"] = wraw
    shared["wgt"] = wgt
    shared["ident"] = _bf16_bits(np.eye(128))
    shared["onesb"] = _bf16_bits(np.ones((128, 128)))
    ok = np.zeros((128, 128), np.float32)
    ok[0, :] = 1.0
    ok[32, :] = 1.0
    shared["onesk"] = _bf16_bits(ok)
    w_s = np.asarray(inp["gmlp_w_s"], np.float32)[0]
    shared["wst"] = np.ascontiguousarray(w_s.transpose(2, 0, 1))
    tri = (np.arange(128)[:, None] <= np.arange(128)[None, :]).astype(np.float32)
    shared["tri"] = np.ascontiguousarray(np.broadcast_to(tri[:, None, :], (128, 8, 128)))
    wss = np.zeros((128, 8, 128), np.float32)
    w64 = w_s[:, :64, :64].transpose(2, 0, 1)
    wss[0:64, :, 0:64] = w64
    wss[64:128, :, 64:128] = w64
    shared["wss"] = wss
    tri2 = np.zeros((128, 128), np.float32)
    tri2[0:64, 0:64] = tri[0:64, 0:64]
    tri2[64:128, 64:128] = tri[0:64, 0:64]
    shared["tri2"] = np.ascontiguousarray(np.broadcast_to(tri2[:, None, :], (128, 8, 128)))
    b_s = np.asarray(inp["gmlp_b_s"], np.float32)[0]
    bsr = np.zeros((128, 2, 8, 128), np.float32)
    bsr[0, 0] = b_s
    bsr[32, 0] = b_s
    bs64 = np.concatenate([b_s[:, :64], b_s[:, :64]], axis=1)
    bsr[0, 1] = bs64
    bsr[32, 1] = bs64
    shared["bsr"] = bsr.reshape(128, 2 * 8 * 128)
    bc = lambda v, n: np.ascontiguousarray(np.broadcast_to(np.asarray(v, np.float32).reshape(1, n), (128, n)))
    shared["lng"] = bc(inp["gmlp_ln_g"][0], GW)
    shared["lnb"] = bc(inp["gmlp_ln_b"][0], GW)
    shared["gfin"] = bc(inp["final_norm_g"], D)
    shared["gkv"] = bc(inp["mla_kv_norm_g"][0], KVL)
    cw = np.asarray(inp["ffn_conv_w"], np.float32)[0]
    cb = np.asarray(inp["ffn_conv_b"], np.float32)[0]
    cwt = np.zeros((128, NFC, 4), np.float32)
    for k in range(3):
        cwt[:, :, k] = cw[k].reshape(NFC, 128).T
    cwt[:, :, 3] = cb.reshape(NFC, 128).T
    shared["convw"] = cwt.reshape(128, NFC * 4)
    bp = np.zeros((8, 512), np.float32)
    for j in range(8):
        bp[j, j * 64:(j + 1) * 64] = 1.0
    shared["brow_p"] = _bf16_bits(bp)
    bsm = np.zeros((8, 512), np.float32)
    bsm[0, :] = 1.0
    shared["brow_s"] = _bf16_bits(bsm)
    pos_s = cfg.PAST + np.arange(cfg.DS)
    cs_, sn_ = _cs_tables(pos_s)
    cs_tok_s = np.tile(np.concatenate([cs_, sn_], axis=1), (cfg.NSB, 1))
    shared["cs_tok_s"] = np.ascontiguousarray(cs_tok_s, dtype=np.float32)
    c2 = np.concatenate([cs_, cs_], axis=1).T
    s2 = np.concatenate([-sn_, sn_], axis=1).T
    csq_s = np.stack([np.tile(c2, (1, cfg.NSB)), np.tile(s2, (1, cfg.NSB))], axis=0)
    shared["csq_s"] = np.ascontiguousarray(csq_s.transpose(1, 0, 2), dtype=np.float32)
    cs_p, sn_p = _cs_tables(np.arange(cfg.SEQ))
    shared["cs_tok"] = np.ascontiguousarray(np.concatenate([cs_p, sn_p], axis=1), dtype=np.float32)

    in_maps = []
    for c in range(8):
        b, r = c // 4, c % 4
        m = dict(shared)
        m["x_seq"] = np.ascontiguousarray(xp[b])
        xo = np.zeros((cfg.OWN, D), np.float32)
        pos_own = np.zeros((cfg.OWN,), np.int64)
        hs = np.ones((128, cfg.NJ), np.float32)
        mk = np.zeros((len(tl) + 1, 8, cfg.NMM * 128), np.float32)
        for ti, t in enumerate(tl):
            g = 4 * t["J"] + r
            if t["kind"] == "halo":
                q0 = g * cfg.B - 128
                fake = q0 < 0
                if fake:
                    hs[:, t["J"]] = 0.0
                else:
                    xo[t["tok0"]:t["tok0"] + 128] = xp[b, q0:q0 + 128]
                    pos_own[t["tok0"]:t["tok0"] + 128] = np.arange(q0, q0 + 128)
            else:
                q0 = g * cfg.B + t["i"] * cfg.W
                fake = False
                xo[t["tok0"]:t["tok0"] + cfg.W] = xp[b, q0:q0 + cfg.W]
                pos_own[t["tok0"]:t["tok0"] + cfg.W] = np.arange(q0, q0 + cfg.W)
            blocks = list(range(t["mlo"], t["nkb"]))
            mk[ti] = _mask_rows(cfg, max(q0, 0), t["W"], blocks, fake=fake)
        smk = np.zeros((8, cfg.NMM * 128), np.float32)
        for gidx in range(cfg.NSB):
            lo = 64 if gidx % 2 == 0 else 0
            smk[0, gidx * 128 + lo:gidx * 128 + lo + 64] = -BIG
        mk[len(tl)] = smk
        m["maskt"] = _bf16_bits(mk)
        m["x_own"] = xo
        m["hscale"] = hs
        co, so = _cs_tables(pos_own)
        c2o = np.concatenate([co, co], axis=1).T
        s2o = np.concatenate([-so, so], axis=1).T
        m["csq"] = np.ascontiguousarray(np.stack([c2o, s2o], axis=1), dtype=np.float32)
        sb0 = c * cfg.NSB
        m["x_smp"] = np.ascontiguousarray(xs[sb0:sb0 + cfg.NSB].reshape(cfg.WS, D))
        m["ckv_cache"] = np.ascontiguousarray(ckv_c[sb0:sb0 + cfg.NSB].reshape(cfg.NSB * cfg.PAST, KVL))
        m["kr_cache"] = np.ascontiguousarray(kr_c[sb0:sb0 + cfg.NSB].reshape(cfg.NSB * cfg.PAST, ROPE))
        st = cst[sb0:sb0 + cfg.NSB]
        m["conv_state"] = np.ascontiguousarray(st.reshape(cfg.NSB, 2, NFC, 128).transpose(3, 2, 0, 1)).reshape(128, NFC * cfg.NSB * 2)
        in_maps.append(m)
    return in_maps


class Prog:
    def __init__(self, cfg):
        self.cfg = cfg
        self.nc = bass.Bass("TRN2", target_bir_lowering=False)
        self.es = contextlib.ExitStack()
        self.S = Sched(self.nc)
        self.dsems = {}
        self.bank_rr = 0
        self.alt = 0
        self.wpos = 0
        self.wsched = []
        self.wloaded = 0

    def din(self, name, shape, dt=F32):
        return self.nc.dram_tensor(name, list(shape), dt, kind="ExternalInput").ap()

    def dout(self, name, shape, dt=F32):
        return self.nc.dram_tensor(name, list(shape), dt, kind="ExternalOutput").ap()

    def dint(self, name, shape, dt):
        return self.nc.dram_tensor(name, list(shape), dt, kind="Internal").ap()

    def dsem(self, key):
        if key not in self.dsems:
            self.dsems[key] = self.es.enter_context(self.nc.semaphore("d%d" % len(self.dsems)))
        return self.dsems[key]

    def dma(self, out, in_, reads, writes, semkey, slow=False):
        sem = self.dsem(semkey)
        if slow:
            fn = lambda e: e.dma_start(out=out, in_=in_, allow_slow_non_contiguous=True)
        else:
            fn = lambda e: e.dma_start(out=out, in_=in_)
        return self.S.op("sp", fn, reads=reads, writes=writes, dma_sem=sem)

    def mm(self, out, lhsT, rhs, start, stop, reads, writes):
        return self.S.op("pe", lambda e: e.matmul(out, lhsT=lhsT, rhs=rhs, start=start, stop=stop),
                         reads=reads, writes=writes)

    def tr(self, out, in_, reads, writes):
        ident = self.identb
        return self.S.op("pe", lambda e: e.transpose(out=out, in_=in_, identity=ident),
                         reads=list(reads) + ["ident"], writes=writes)

    def act(self, out, in_, func, reads, writes, scale=None, bias=None, accum=None):
        kw = {}
        if scale is not None:
            kw["scale"] = scale
        if bias is not None:
            kw["bias"] = bias
        if accum is not None:
            kw["accum_out"] = accum
        return self.S.op("act", lambda e: e.activation(out=out, in_=in_, func=func, **kw),
                         reads=reads, writes=writes)

    def ts(self, eng, out, in0, s1, s2, op0, op1, reads, writes):
        if op1 is None:
            fn = lambda e: e.tensor_scalar(out=out, in0=in0, scalar1=s1, scalar2=None, op0=op0)
        else:
            fn = lambda e: e.tensor_scalar(out=out, in0=in0, scalar1=s1, scalar2=s2, op0=op0, op1=op1)
        return self.S.op(eng, fn, reads=reads, writes=writes)

    def tt(self, eng, out, in0, in1, op, reads, writes):
        return self.S.op(eng, lambda e: e.tensor_tensor(out=out, in0=in0, in1=in1, op=op),
                         reads=reads, writes=writes)

    def stt(self, out, in0, scalar, in1, op0, op1, reads, writes):
        return self.S.op("dve", lambda e: e.scalar_tensor_tensor(out=out, in0=in0, scalar=scalar, in1=in1,
                                                                  op0=op0, op1=op1),
                         reads=reads, writes=writes)

    def cp(self, eng, out, in_, reads, writes):
        if eng == "act":
            return self.S.op("act", lambda e: e.copy(out=out, in_=in_), reads=reads, writes=writes)
        return self.S.op(eng, lambda e: e.tensor_copy(out=out, in_=in_), reads=reads, writes=writes)

    def memset(self, eng, ap, val, writes):
        return self.S.op(eng, lambda e: e.memset(ap, val), writes=writes)

    def evac_eng(self):
        self.alt ^= 1
        return "act" if self.alt else "dve"

    def bank(self):
        b = self.banks_free[self.bank_rr % len(self.banks_free)]
        self.bank_rr += 1
        return b

    def tbank(self):
        bk = self.bank()
        return self.F[bk][:, 0:256].bitcast(BF16), ("F", bk)

    def rstd(self, ssq, out, n, inv_n, key_in, key_out):
        tmp = self.sttmp[:, 0:n]
        self.ts("dve", tmp, ssq, inv_n, EPS, ALU.mult, ALU.add, reads=[key_in], writes=["sttmp"])
        nh = self.neghalf[:, 0:n]
        self.tt("pool", out, tmp, nh, ALU.pow, reads=["sttmp", "neghalf"], writes=[key_out])

    def w_issue(self, k):
        if k >= len(self.wsched) or k < self.wloaded:
            return
        assert k == self.wloaded
        pid = self.wsched[k]
        nkc, ncol = PIECE_META[pid]
        slot = k % 3
        dst = self.wslot[slot][:, 0:nkc * ncol]
        src = self.wstream[pid, :, 0:nkc * ncol]
        self.dma(dst, src, reads=[("wst", pid)], writes=[("ws", slot)], semkey=("ws", slot))
        self.wloaded = k + 1

    def w_get(self, pid):
        k = self.wpos
        assert self.wsched[k] == pid, (k, self.wsched[k], pid)
        for kk in range(self.wloaded, k + 3):
            self.w_issue(kk)
        self.wpos += 1
        nkc, ncol = PIECE_META[pid]
        slot = k % 3
        ap = self.wslot[slot][:, 0:nkc * ncol].rearrange("p (a b) -> p a b", a=nkc)
        return ap, ("ws", slot)

    def declare(self):
        cfg = self.cfg
        nc = self.nc
        es = self.es
        I = {}
        I["wraw"] = self.din("wraw", [NPW, 128, PW])
        I["wgt"] = self.din("wgt", [128, NPW, 12])
        for n in ("ident", "onesb", "onesk"):
            I[n] = self.din(n, [128, 128], mybir.dt.uint16)
        I["wst"] = self.din("wst", [128, 8, 128])
        I["tri"] = self.din("tri", [128, 8, 128])
        I["wss"] = self.din("wss", [128, 8, 128])
        I["tri2"] = self.din("tri2", [128, 8, 128])
        I["bsr"] = self.din("bsr", [128, 2048])
        I["lng"] = self.din("lng", [128, GW])
        I["lnb"] = self.din("lnb", [128, GW])
        I["gfin"] = self.din("gfin", [128, D])
        I["gkv"] = self.din("gkv", [128, KVL])
        I["convw"] = self.din("convw", [128, NFC * 4])
        I["brow_p"] = self.din("brow_p", [8, 512], mybir.dt.uint16)
        I["brow_s"] = self.din("brow_s", [8, 512], mybir.dt.uint16)
        I["cs_tok_s"] = self.din("cs_tok_s", [cfg.WS, 64])
        I["csq_s"] = self.din("csq_s", [64, 2, cfg.WS])
        I["cs_tok"] = self.din("cs_tok", [cfg.SEQ, 64])
        I["x_seq"] = self.din("x_seq", [cfg.SEQ, D])
        I["x_own"] = self.din("x_own", [cfg.OWN, D])
        I["hscale"] = self.din("hscale", [128, cfg.NJ])
        self.ntiles = len(cfg.tiles())
        I["maskt"] = self.din("maskt", [self.ntiles + 1, 8, cfg.NMM * 128], mybir.dt.uint16)
        I["csq"] = self.din("csq", [64, 2, cfg.OWN])
        I["x_smp"] = self.din("x_smp", [cfg.WS, D])
        I["ckv_cache"] = self.din("ckv_cache", [cfg.NSB * cfg.PAST, KVL])
        I["kr_cache"] = self.din("kr_cache", [cfg.NSB * cfg.PAST, ROPE])
        I["conv_state"] = self.din("conv_state", [128, NFC * cfg.NSB * 2])
        self.I = I
        O = {}
        O["y_own"] = self.dout("y_own", [cfg.NJ * cfg.B, D])
        O["ckv_seq"] = self.dout("ckv_seq", [cfg.SEQ, KVL])
        O["kr_seq"] = self.dout("kr_seq", [cfg.SEQ, ROPE])
        O["conv_last"] = self.dout("conv_last", [2, 2 * DFF])
        O["y_smp"] = self.dout("y_smp", [cfg.WS, D])
        O["ckv_smp"] = self.dout("ckv_smp", [cfg.WS, KVL])
        O["kr_smp"] = self.dout("kr_smp", [cfg.WS, ROPE])
        O["conv_smp"] = self.dout("conv_smp", [cfg.NSB, 2, 2 * DFF])
        O["gv_smp"] = self.dout("gv_smp", [cfg.WS, GW])
        self.O = O
        self.wstream = self.dint("wstream", [NPW, 128, PW], BF16)
        self.kvscr = self.dint("kvscr", [4, cfg.NGRP, 128, 4, KVC], BF16)

        NW = 50 * 1024
        big = es.enter_context(nc.sbuf_tensor("arena", [128, NW], F32))
        A = Arena(big, NW)
        self.A = A
        al = A.alloc
        self.identb = al("", [128, 128], BF16)
        self.onesb = al("", [128, 128], BF16)
        self.onesk = al("", [128, 128], BF16)
        self.ones32 = al("", [128, 128], F32)
        self.wstm = al("", [128, 8, 128], BF16)
        self.wssm = al("", [128, 8, 128], BF16)
        self.bsrb = al("", [128, 2, 1024], BF16)
        self.lng = al("", [128, GW], F32)
        self.lnb = al("", [128, GW], F32)
        self.gfin = al("", [128, D], F32)
        self.gkv = al("", [128, KVL], F32)
        self.convw = al("", [128, NFC, 4], F32)
        self.hscale = al("", [128, cfg.NJ], F32)
        self.neghalf = al("", [128, 8], F32)
        self.sttmp = al("", [128, 8], F32)
        self.stats = al("", [128, 96], F32)
        self.prevsave = al("", [128, NFC, cfg.NSB, 2], F32)
        self.wslot = [al("", [128, PW], BF16) for _ in range(3)]
        self.wgt = al("", [128, NPW, 12], F32)
        A.phase("wprep", "")
        self.wp_in = [al("wprep", [128, PW], F32) for _ in range(2)]
        self.wp_out = [al("wprep", [128, PW], BF16) for _ in range(2)]
        self.wp_f = [al("wprep", [128, 1024], F32) for _ in range(4)]
        A.phase("pre", "")
        G = 4
        self.XP = [al("pre", [128, G, D], F32) for _ in range(2)]
        self.p_xnb = [[al("pre", [128, D], BF16) for _ in range(G)] for _ in range(2)]
        self.p_xnT = al("pre", [128, 8, G * 128], BF16)
        self.p_ckv32 = [al("pre", [128, G, KVL], F32) for _ in range(2)]
        self.p_ckvb = [al("pre", [128, KVL], BF16) for _ in range(G)]
        self.p_krraw = al("pre", [128, G, ROPE], F32)
        self.p_kr32 = [al("pre", [128, G, ROPE], F32) for _ in range(2)]
        self.p_krb = al("pre", [128, G, ROPE], BF16)
        self.p_cs = [al("pre", [128, G, 64], F32) for _ in range(2)]
        self.p_t = [al("pre", [128, G, 32], F32) for _ in range(4)]
        self.p_ckvT = al("pre", [128, 4, G * 128], BF16)
        self.p_stg = [al("pre", [128, G, 4, KVC], BF16) for _ in range(1)]
        self.p_junk = al("pre", [128, D], BF16)
        self.p_st = al("pre", [128, 2, 32], F32)
        A.phase("main", "")
        W = max(cfg.W, cfg.WS)
        self.Wmax = W
        NS = W // 128
        self.X = al("main", [128, NS, D], F32)
        self.bufA = al("main", [128, 8, W], BF16)
        self.uT = al("main", [128, 8, W], BF16)
        self.gT = al("main", [128, 16, W], BF16)
        self.qnT = al("main", [128, 4, W], BF16)
        self.junk = al("main", [128, D], BF16)
        A.phase("front", "main")
        self.xnb = [al("front", [128, D], BF16) for _ in range(2)]
        self.vg = [al("front", [128, GW], F32) for _ in range(2)]
        self.vb = al("front", [128, NS, GW], BF16)
        self.qnb = [al("front", [128, QL], BF16) for _ in range(NS)]
        self.bst = al("front", [128, 2, 6], F32)
        A.phase("attn", "main")
        self.qnopeT = al("attn", [128, 8, W], BF16)
        self.QR = al("attn", [128, 8, W], BF16)
        self.attnT = al("attn", [128, 8, W], BF16)
        self.MK = al("attn", [128, cfg.NMM, 128], BF16)
        self.ropet = [al("attn", [128, W], F32) for _ in range(2)]
        self.csq = al("attn", [128, 2, W], F32)
        self.PT = [al("attn", [128, W], BF16) for _ in range(4)]
        self.NKV = 3
        self.KV = [al("attn", [128, 4, KVC], BF16) for _ in range(self.NKV)]
        self.rec = [al("attn", [128, W], F32) for _ in range(2)]
        self.sacc = [al("attn", [128, W], F32) for _ in range(2)]
        self.tmpm = self.sacc
        A.phase("ffn", "main")
        self.hnb = [al("ffn", [128, D], BF16) for _ in range(2)]
        self.actT = al("ffn", [128, NPAIR, W], BF16)
        self.U = [[al("ffn", [128, W + 2 * cfg.NSB], F32) for _ in range(2)] for _ in range(2)]
        self.cv = [[al("ffn", [128, W], F32) for _ in range(2)] for _ in range(2)]
        self.sg = [al("ffn", [128, W], F32) for _ in range(2)]
        self.Y = [al("ffn", [128, D], F32) for _ in range(2)]
        self.F = [es.enter_context(nc.psum_tensor("F%d" % i, [128, 512], F32)) for i in range(8)]
        self.banks_free = list(range(8))
        self.sems = {e: es.enter_context(nc.semaphore("s_" + e)) for e in ENGS}

    def setup(self):
        I = self.I
        self.S.ph = "setup"
        bf = lambda ap: ap.bitcast(BF16)
        self.dma(self.identb, bf(I["ident"]), [], ["ident"], "c0")
        self.dma(self.onesb, bf(I["onesb"]), [], ["onesb"], "c1")
        self.dma(self.onesk, bf(I["onesk"]), [], ["onesk"], "c2")
        self.dma(self.lng, I["lng"], [], ["lng"], "c3")
        self.dma(self.lnb, I["lnb"], [], ["lnb"], "c4")
        self.dma(self.gfin, I["gfin"], [], ["gfin"], "c5")
        self.dma(self.gkv, I["gkv"], [], ["gkv"], "c6")
        self.dma(self.convw.rearrange("p a b -> p (a b)"), I["convw"], [], ["convw"], "c7")
        self.dma(self.hscale, I["hscale"], [], ["hscale"], "c8")
        self.dma(self.wgt.rearrange("p a b -> p (a b)"), I["wgt"].rearrange("p a b -> p (a b)"), [], ["wgt"], "c9")
        self.memset("pool", self.neghalf, -0.5, ["neghalf"])
        self.memset("pool", self.ones32, 1.0, ["ones32"])
        self.memset("pool", self.prevsave.rearrange("p a b c -> p (a b c)"), 0.0, ["prevsave"])
        self.memset("pool", self.stats, 1.0, ["onescol"])
        f = self.wp_f
        flat = lambda ap: ap.rearrange("p a b -> p (a b)")
        self.dma(f[0], flat(I["wst"]), [], ["f0"], "f0")
        self.dma(f[1], flat(I["tri"]), [], ["f1"], "f1")
        self.dma(f[2], flat(I["wss"]), [], ["f2"], "f2")
        self.dma(f[3], flat(I["tri2"]), [], ["f3"], "f3")
        self.tt("dve", flat(self.wstm), f[0], f[1], ALU.mult, ["f0", "f1"], ["wstm"])
        self.tt("dve", flat(self.wssm), f[2], f[3], ALU.mult, ["f2", "f3"], ["wssm"])
        src = self.wp_in[0][:, 0:2048]
        tmpb = self.wp_out[0][:, 0:2048]
        bs = self.bsrb.rearrange("p a b -> p (a b)")
        self.dma(src, I["bsr"], [], ["bsrc"], "bsrc")
        self.memset("pool", bs, 0.0, ["bsrb"])
        self.cp("dve", bs[0:1, :], src[0:1, :], ["bsrc"], ["bsrb"])
        self.cp("dve", tmpb[32:33, :], src[32:33, :], ["bsrc"], ["btmp"])
        self.tt("dve", src[32:33, :], src[32:33, :], tmpb[32:33, :], ALU.subtract, ["bsrc", "btmp"], ["bsrc2"])
        self.cp("dve", bs[32:33, :], src[32:33, :], ["bsrc2"], ["bsrb"])
        self.S.barrier()
        self.S.ph = "wprep"
        engs = ["dve", "act"]
        k = 0
        for pi in range(NPW):
            nkc, ncol = PIECE_META[pi]
            b = pi % 2
            n = nkc * ncol
            self.dma(self.wp_in[b][:, 0:n], I["wraw"][pi, :, 0:n], [], [("wpi", b)], ("wpi", b))
            for kc in range(nkc):
                eng = engs[k % 2]
                k += 1
                o = self.wp_out[b][:, kc * ncol:(kc + 1) * ncol]
                i_ = self.wp_in[b][:, kc * ncol:(kc + 1) * ncol]
                sc = self.wgt[:, pi, kc:kc + 1]
                if eng == "act":
                    self.act(o, i_, AF.Copy, [("wpi", b), "wgt"], [("wpo", b)], scale=sc)
                else:
                    self.ts(eng, o, i_, sc, None, ALU.mult, None, [("wpi", b), "wgt"], [("wpo", b)])
            self.dma(self.wstream[pi, :, 0:n], self.wp_out[b][:, 0:n], [("wpo", b)], [("wst", pi)], ("wpo", b))
        self.S.barrier()

    def pre_setup(self):
        for i, pid in enumerate((P_KV, P_UK, P_UV)):
            nkc, ncol = PIECE_META[pid]
            self.dma(self.wslot[i][:, 0:nkc * ncol], self.wstream[pid, :, 0:nkc * ncol], [], [("pw", i)], ("pw", i))
        stg = self.p_stg[0]
        self.memset("pool", stg.rearrange("p a b c -> p (a b c)"), 0.0, ["stg_all"])
        for gi in range(4):
            self.memset("pool", stg[64:65, gi, :, 256:384], 1.0, ["stg_all"])
        self.Wkv = self.wslot[0][:, 0:8 * 576].rearrange("p (a b) -> p a b", a=8)
        self.Wuk = self.wslot[1][:, 0:4 * 1024].rearrange("p (a b) -> p a b", a=4)
        self.Wuv = self.wslot[2][:, 0:4 * 1024].rearrange("p (a b) -> p a b", a=4)
        self.pre_first = True

    def kv_stage_a(self, kind, G, par, x_src=None, cs_src=None, ckv_src=None, kr_src=None):
        st = self.p_st
        if kind == "x":
            self.dma(self.p_cs[par][:, 0:G, :], cs_src.rearrange("(g p) c -> p g c", p=128), [], [("pcs", par)], ("pcs", par))
            for gi in range(G):
                self.act(self.p_junk, self.XP[par][:, gi, :], AF.Square, [("XP", par)], ["pjunk", ("pssq", par)],
                         accum=st[:, par, gi:gi + 1])
            self.ts("dve", st[:, par, 4:4 + G], st[:, par, 0:G], 1.0 / D, EPS, ALU.mult, ALU.add,
                    [("pssq", par)], [("pms", par)])
            self.tt("pool", st[:, par, 8:8 + G], st[:, par, 4:4 + G], self.neghalf[:, 0:G], ALU.pow,
                    [("pms", par), "neghalf"], [("prs", par)])
            for gi in range(G):
                self.ts("dve", self.p_xnb[par][gi], self.XP[par][:, gi, :], st[:, par, 8 + gi:9 + gi], None, ALU.mult, None,
                        [("XP", par), ("prs", par)], [("pxnb", par, gi)])
        else:
            self.dma(self.p_ckv32[par][:, 0:G, :], ckv_src.rearrange("(g p) c -> p g c", p=128), [], [("pckv32", par)], ("pckv32", par))
            self.dma(self.p_kr32[par][:, 0:G, :], kr_src.rearrange("(g p) c -> p g c", p=128), [], [("pkr32", par)], ("pkr32", par))

    def kv_a_load(self, kind, G, par, x_src=None, cs_src=None, ckv_src=None, kr_src=None):
        if kind == "x":
            self.dma(self.XP[par][:, 0:G, :], x_src.rearrange("(g p) d -> p g d", p=128), [], [("XP", par)], ("XP", par))

    def kv_b(self, kind, G, par, blks, ckv_out=None, kr_out=None):
        if kind == "x":
            for gi in range(G):
                rows = slice(gi * 128, (gi + 1) * 128)
                for hf in range(2):
                    Tv, tk = self.tbank()
                    for j in range(4):
                        kc = 4 * hf + j
                        self.tr(Tv[:, j * 128:(j + 1) * 128],
                                self.p_xnb[par][gi][:, kc * 128:(kc + 1) * 128], [("pxnb", par, gi)], [tk])
                    self.cp(self.evac_eng(), self.p_xnT[:, 4 * hf:4 * hf + 4, rows],
                            Tv.rearrange("p (a b) -> p a b", a=4), [tk], [("pxnT", gi, hf)])

    def kv_c(self, kind, G, par, blks, ckv_out=None, kr_out=None):
        F = self.F
        st = self.p_st
        kr32 = self.p_kr32[par]
        if kind == "x":
            banks = []
            for gi in range(G):
                rows = slice(gi * 128, (gi + 1) * 128)
                ba = self.bank()
                for kc in range(8):
                    self.mm(F[ba][:, 0:512], self.p_xnT[:, kc, rows], self.Wkv[:, kc, 0:512], kc == 0, kc == 7,
                            [("pxnT", gi, 0), ("pxnT", gi, 1), ("pw", 0)], [("F", ba)])
                banks.append(ba)
            bb = self.bank()
            for gi in range(G):
                rows = slice(gi * 128, (gi + 1) * 128)
                for kc in range(8):
                    self.mm(F[bb][:, gi * 64:(gi + 1) * 64], self.p_xnT[:, kc, rows], self.Wkv[:, kc, 512:576],
                            kc == 0, kc == 7, [("pxnT", gi, 0), ("pxnT", gi, 1), ("pw", 0)], [("F", bb)])
            for gi in range(G):
                self.act(self.p_junk[:, 0:512], F[banks[gi]][:, 0:512], AF.Square, [("F", banks[gi])],
                         ["pjunk", ("pssc", par)], accum=st[:, par, 12 + gi:13 + gi])
            self.cp("act", self.p_krraw[:, 0:G, :], F[bb][:, 0:G * 64].rearrange("p (a b) -> p a b", a=G),
                    [("F", bb)], ["pkraw"])
            self.ts("dve", st[:, par, 16:16 + G], st[:, par, 12:12 + G], 1.0 / KVL, EPS, ALU.mult, ALU.add,
                    [("pssc", par)], [("pmc", par)])
            self.tt("pool", st[:, par, 20:20 + G], st[:, par, 16:16 + G], self.neghalf[:, 0:G], ALU.pow,
                    [("pmc", par), "neghalf"], [("prc", par)])
            for gi in range(G):
                self.stt(self.p_ckv32[par][:, gi, :], F[banks[gi]][:, 0:512], st[:, par, 20 + gi:21 + gi], self.gkv,
                         ALU.mult, ALU.mult, [("F", banks[gi]), ("prc", par), "gkv"], [("pckv32", par)])
            self.dma(ckv_out.rearrange("(g p) c -> p g c", p=128), self.p_ckv32[par][:, 0:G, :], [("pckv32", par)], [],
                     ("pckv32", par))
            raw, cs, t = self.p_krraw, self.p_cs[par], self.p_t
            rk = ["pkraw", ("pcs", par)]
            g_ = slice(0, G)
            self.tt("dve", t[0][:, g_, :], raw[:, g_, 0:32], cs[:, g_, 0:32], ALU.mult, rk, ["pt0"])
            self.tt("dve", t[1][:, g_, :], raw[:, g_, 32:64], cs[:, g_, 32:64], ALU.mult, rk, ["pt1"])
            self.tt("dve", t[2][:, g_, :], raw[:, g_, 0:32], cs[:, g_, 32:64], ALU.mult, rk, ["pt2"])
            self.tt("dve", t[3][:, g_, :], raw[:, g_, 32:64], cs[:, g_, 0:32], ALU.mult, rk, ["pt3"])
            self.tt("dve", kr32[:, g_, 0:32], t[0][:, g_, :], t[1][:, g_, :], ALU.subtract, ["pt0", "pt1"], [("pkr32", par)])
            self.tt("dve", kr32[:, g_, 32:64], t[2][:, g_, :], t[3][:, g_, :], ALU.add, ["pt2", "pt3"], [("pkr32", par)])
            self.dma(kr_out.rearrange("(g p) c -> p g c", p=128), kr32[:, 0:G, :], [("pkr32", par)], [], ("pkr32", par))

    def kv_d(self, kind, G, par, blks, ckv_out=None, kr_out=None):
        F = self.F
        stg = self.p_stg[0]
        sdep = ["stg_all"]
        kr32 = self.p_kr32[par]
        for gi in range(G):
            self.cp("act", self.p_ckvb[gi], self.p_ckv32[par][:, gi, :], [("pckv32", par)], [("pckvb", gi)])
        self.cp("dve", self.p_krb[:, 0:G, :], kr32[:, 0:G, :], [("pkr32", par)], ["pkrb"])
        for gi in range(G):
            rows = slice(gi * 128, (gi + 1) * 128)
            Tv, tk = self.tbank()
            for kc in range(4):
                self.tr(Tv[:, kc * 128:(kc + 1) * 128], self.p_ckvb[gi][:, kc * 128:(kc + 1) * 128],
                        [("pckvb", gi)], [tk])
            self.cp(self.evac_eng(), self.p_ckvT[:, 0:4, rows],
                    Tv.rearrange("p (a b) -> p a b", a=4), [tk], [("pckvT", gi)])
        Tv2, tk2 = self.tbank()
        for gi in range(G):
            self.tr(Tv2[0:64, gi * 128:(gi + 1) * 128], self.p_krb[:, gi, :], ["pkrb"], [tk2])
        for hp in range(4):
            self.cp("dve" if hp % 2 else "act", stg[0:64, 0:G, hp, 256:384],
                    Tv2[0:64, 0:G * 128].rearrange("p (a b) -> p a b", a=G),
                    [tk2] + sdep, [("stg", gi) for gi in range(G)])

    def kv_e1(self, kind, G, par, blks, ckv_out=None, kr_out=None):
        F = self.F
        stg = self.p_stg[0]
        sdep = ["stg_all"]
        GW_ = G * 128
        for h in range(NH):
            bk = self.bank()
            for kc in range(4):
                self.mm(F[bk][:, 0:GW_], self.Wuk[:, kc, h * 128:(h + 1) * 128], self.p_ckvT[:, kc, 0:GW_],
                        kc == 0, kc == 3, [("pckvT", gi) for gi in range(G)] + [("pw", 1)], [("F", bk)])
            self.cp(self.evac_eng(), stg[:, 0:G, h // 2, (h % 2) * 128:(h % 2 + 1) * 128],
                    F[bk][:, 0:GW_].rearrange("p (a b) -> p a b", a=G), [("F", bk)] + sdep,
                    [("stg", gi) for gi in range(G)])

    def kv_e2(self, kind, G, par, blks, ckv_out=None, kr_out=None):
        F = self.F
        stg = self.p_stg[0]
        sdep = ["stg_all"]
        for gi in range(G):
            rows = slice(gi * 128, (gi + 1) * 128)
            for half in range(2):
                bk = self.bank()
                for kc in range(4):
                    self.mm(F[bk][:, 0:512], self.p_ckvT[:, kc, rows], self.Wuv[:, kc, half * 512:(half + 1) * 512],
                            kc == 0, kc == 3, [("pckvT", gi), ("pw", 2)], [("F", bk)])
                self.cp(self.evac_eng(), stg[:, gi, 2 * half:2 * half + 2, 384:640],
                        F[bk][:, 0:512].rearrange("p (a b) -> p a b", a=2), [("F", bk)] + sdep, [("stg", gi)])
        for hp in range(4):
            self.dma(self.kvscr[hp, blks], stg[:, :, hp, :], [("stg", gi) for gi in range(4)], [("kvs", blks, hp)],
                     ("stgd", hp))

    def prepass(self):
        cfg = self.cfg
        self.S.ph = "prepass"
        I, O = self.I, self.O
        self.pre_setup()
        G = 4
        jobs = []
        for g0 in range(0, cfg.NBP, G):
            r = slice(g0 * 128, (g0 + G) * 128)
            jobs.append(dict(kind="x", G=G, blks=g0 // 4,
                             a=dict(x_src=I["x_seq"][r, :], cs_src=I["cs_tok"][r, :]),
                             r=dict(ckv_out=O["ckv_seq"][r, :], kr_out=O["kr_seq"][r, :])))
        Gs = cfg.NBS_NEW
        jobs.append(dict(kind="x", G=Gs, blks=cfg.GRP_NEW,
                         a=dict(x_src=I["x_smp"], cs_src=I["cs_tok_s"]),
                         r=dict(ckv_out=O["ckv_smp"], kr_out=O["kr_smp"])))
        for b in range(cfg.NSB):
            for g0 in range(0, cfg.NBC, 4):
                r = slice(b * cfg.PAST + g0 * 128, b * cfg.PAST + (g0 + 4) * 128)
                jobs.append(dict(kind="cache", G=4, blks=cfg.NGP + b * cfg.NGC + g0 // 4,
                                 a=dict(ckv_src=I["ckv_cache"][r, :], kr_src=I["kr_cache"][r, :]), r={}))
        n = len(jobs)

        def call(fn, i, stage_a=0):
            if i >= n:
                return
            jb = jobs[i]
            if stage_a == 1:
                self.kv_stage_a(jb["kind"], jb["G"], i % 2, **jb["a"])
            elif stage_a == 2:
                self.kv_a_load(jb["kind"], jb["G"], i % 2, **jb["a"])
            else:
                fn(jb["kind"], jb["G"], i % 2, jb["blks"], **jb["r"])

        call(None, 0, 2)
        call(None, 1, 2)
        call(None, 0, 1)
        call(None, 1, 1)
        call(None, 2, 2)
        call(self.kv_b, 0)
        call(self.kv_c, 0)
        call(self.kv_b, 1)
        for i in range(n):
            call(self.kv_d, i)
            call(None, i + 2, 1)
            call(None, i + 3, 2)
            call(self.kv_e1, i)
            call(self.kv_c, i + 1)
            call(self.kv_e2, i)
            call(self.kv_b, i + 2)
        self.S.barrier()

    def tile(self, W, x_src, csq_src, brow_src, mask_idx, groups, nb, hs_col, is_halo, sample,
             y_out=None, gv_out=None, conv_out=None):
        cfg = self.cfg
        S = self.S
        F = self.F
        I = self.I
        NS = W // 128
        wb = W // nb
        bufA, uT, gT, qnT, X = self.bufA, self.uT, self.gT, self.qnT, self.X
        st = self.stats
        S.soft_barrier()
        kind_ = ("smp" if sample else ("halo" if is_halo else "tile")) + str(mask_idx)
        S.ph = kind_ + ".front"
        for s in range(NS):
            self.dma(X[:, s, :], x_src[s * 128:(s + 1) * 128, :], [], [("X", s)], ("X", s))
        for s in range(NS):
            self.act(self.junk, X[:, s, :], AF.Square, [("X", s)], ["junk", ("ssq", s)], accum=st[:, s:s + 1])
            self.rstd(st[:, s:s + 1], st[:, 8 + s:9 + s], 1, 1.0 / D, ("ssq", s), ("rs", s))
            if s % 2 == 0:
                self.act(self.xnb[s % 2], X[:, s, :], AF.Copy, [("X", s), ("rs", s)], [("xnb", s % 2)], scale=st[:, 8 + s:9 + s])
            else:
                self.ts("dve", self.xnb[s % 2], X[:, s, :], st[:, 8 + s:9 + s], None, ALU.mult, None,
                        [("X", s), ("rs", s)], [("xnb", s % 2)])
            for hf in range(2):
                Tv, tk = self.tbank()
                for j in range(4):
                    kc = 4 * hf + j
                    self.tr(Tv[:, j * 128:(j + 1) * 128],
                            self.xnb[s % 2][:, kc * 128:(kc + 1) * 128], [("xnb", s % 2)], [tk])
                self.cp(self.evac_eng(), bufA[:, 4 * hf:4 * hf + 4, s * 128:(s + 1) * 128],
                        Tv.rearrange("p (a b) -> p a b", a=4), [tk], [("bA", s)])
        bA_all = [("bA", s) for s in range(NS)]
        Wu, wk = self.w_get(P_U)
        for j in range(8):
            bk = self.bank()
            for kc in range(8):
                self.mm(F[bk][:, 0:W], Wu[:, kc, j * 128:(j + 1) * 128], bufA[:, kc, 0:W], kc == 0, kc == 7,
                        bA_all + [wk], [("F", bk)])
            self.act(uT[:, j, 0:W], F[bk][:, 0:W], AF.Gelu_apprx_tanh, [("F", bk)], [("uT", j)])
        Wv, wk = self.w_get(P_V)
        for s in range(NS):
            vg = self.vg[s % 2]
            vk = ("vg", s % 2)
            for half in range(2):
                bk = self.bank()
                for kc in range(8):
                    self.mm(F[bk][:, 0:512], bufA[:, kc, s * 128:(s + 1) * 128], Wv[:, kc, half * 512:(half + 1) * 512],
                            kc == 0, kc == 7, [("bA", s), wk], [("F", bk)])
                self.act(vg[:, half * 512:(half + 1) * 512], F[bk][:, 0:512], AF.Gelu_apprx_tanh, [("F", bk)], [vk])
            for half in range(2):
                S.op("dve", lambda e, half=half, vg=vg: e.bn_stats(out=self.bst[:, half, :], in_=vg[:, half * 512:(half + 1) * 512]),
                     reads=[vk], writes=[("bst", half)])
            S.op("dve", lambda e: e.bn_aggr(out=st[:, 32:34], in_=self.bst.rearrange("p a b -> p (a b)")),
                 reads=[("bst", 0), ("bst", 1)], writes=["lnmv"])
            self.rstd(st[:, 33:34], st[:, 34:35], 1, 1.0, "lnmv", "lnrs")
            self.ts("dve", vg, vg, st[:, 32:33], st[:, 34:35], ALU.subtract, ALU.mult, [vk, "lnmv", "lnrs"], [vk])
            self.tt("dve", vg, vg, self.lng, ALU.mult, [vk, "lng"], [vk])
            self.tt("dve", vg, vg, self.lnb, ALU.add, [vk, "lnb"], [vk])
            if gv_out is not None:
                self.dma(gv_out[s * 128:(s + 1) * 128, :], vg, [vk], [], vk)
            self.cp("act", self.vb[:, s, :], vg, [vk], [("vb", s)])
        Wq, wk = self.w_get(P_Q)
        for s in range(NS):
            bk = self.bank()
            for kc in range(8):
                self.mm(F[bk][:, 0:512], bufA[:, kc, s * 128:(s + 1) * 128], Wq[:, kc, 0:512], kc == 0, kc == 7,
                        [("bA", s), wk], [("F", bk)])
            self.act(self.junk[:, 0:512], F[bk][:, 0:512], AF.Square, [("F", bk)], ["junk", ("qss", s)],
                     accum=st[:, 40 + s:41 + s])
            self.rstd(st[:, 40 + s:41 + s], st[:, 44 + s:45 + s], 1, 1.0 / QL, ("qss", s), ("qrs", s))
            self.act(self.qnb[s], F[bk][:, 0:512], AF.Copy, [("F", bk), ("qrs", s)], [("qnb", s)],
                     scale=st[:, 44 + s:45 + s])
        for gi_, pid in enumerate((P_G0, P_G1)):
            Wg, wk = self.w_get(pid)
            for jj in range(8):
                j = 8 * gi_ + jj
                bk = self.bank()
                for kc in range(8):
                    self.mm(F[bk][:, 0:W], Wg[:, kc, jj * 128:(jj + 1) * 128], bufA[:, kc, 0:W], kc == 0, kc == 7,
                            bA_all + [wk], [("F", bk)])
                self.act(gT[:, j, 0:W], F[bk][:, 0:W], AF.Sigmoid, [("F", bk)], [("gT", j)])
        wsm = self.wssm if sample else self.wstm
        bsel = 1 if sample else 0
        for g in range(8):
            bk = self.bank()
            for s in range(NS):
                self.mm(F[bk][:, s * 128:(s + 1) * 128], self.vb[:, s, g * 128:(g + 1) * 128], wsm[:, g, :], True, False,
                        [("vb", s), "wstm", "wssm"], [("F", bk)])
                self.mm(F[bk][:, s * 128:(s + 1) * 128], self.onesk, self.bsrb[:, bsel, g * 128:(g + 1) * 128], False, True,
                        ["onesk", "bsrb"], [("F", bk)])
            self.tt("dve", uT[:, g, 0:W], F[bk][:, 0:W], uT[:, g, 0:W], ALU.mult, [("F", bk), ("uT", g)], [("uT", g)])
        for s in range(NS):
            Tv, tk = self.tbank()
            for kc in range(4):
                self.tr(Tv[:, kc * 128:(kc + 1) * 128], self.qnb[s][:, kc * 128:(kc + 1) * 128],
                        [("qnb", s)], [tk])
            self.cp("dve", qnT[:, 0:4, s * 128:(s + 1) * 128], Tv.rearrange("p (a b) -> p a b", a=4),
                    [tk], [("qnT", s)])
        S.soft_barrier()
        S.ph = kind_ + ".qhead"
        QR, qnopeT, attnT, MK = self.QR, self.qnopeT, self.attnT, self.MK
        self.memset("pool", QR[64:128, :, :].rearrange("p a b -> p (a b)"), 0.0, ["QRhi"])
        for h in range(NH):
            self.dma(QR[96:104, h, 0:W], brow_src[:, 0:W].bitcast(BF16), ["QRhi"], [("QRb", h)], ("QRb", h))
        self.memset("pool", MK.rearrange("p a b -> p (a b)"), 0.0, ["MK0"])
        self.dma(MK[96:104, :, :].rearrange("p a b -> p (a b)"), I["maskt"][mask_idx].bitcast(BF16), ["MK0"], ["MK"], "MK")
        self.dma(self.csq[0:64, :, 0:W], csq_src, [], ["csq"], "csq")
        qnT_all = [("qnT", s) for s in range(NS)]
        Wn, wk = self.w_get(P_UQN)
        for h in range(NH):
            bk = self.bank()
            for kc in range(4):
                self.mm(F[bk][:, 0:W], Wn[:, kc, h * 128:(h + 1) * 128], qnT[:, kc, 0:W], kc == 0, kc == 3,
                        qnT_all + [wk], [("F", bk)])
            self.cp(self.evac_eng(), qnopeT[:, h, 0:W], F[bk][:, 0:W], [("F", bk)], [("qno", h)])
        Wr, wk = self.w_get(P_UQR)
        for h in range(NH):
            ba, bb = self.bank(), self.bank()
            for kc in range(4):
                self.mm(F[ba][:, 0:W], Wr[:, kc, h * 256:h * 256 + 128], qnT[:, kc, 0:W], kc == 0, kc == 3,
                        qnT_all + [wk], [("F", ba)])
            for kc in range(4):
                self.mm(F[bb][:, 0:W], Wr[:, kc, h * 256 + 128:h * 256 + 256], qnT[:, kc, 0:W], kc == 0, kc == 3,
                        qnT_all + [wk], [("F", bb)])
            r0, r1 = self.ropet[0], self.ropet[1]
            self.tt("dve", r0[0:64, 0:W], F[ba][0:64, 0:W], self.csq[0:64, 0, 0:W], ALU.mult, [("F", ba), "csq"], ["r0"])
            self.tt("dve", r1[0:64, 0:W], F[bb][0:64, 0:W], self.csq[0:64, 1, 0:W], ALU.mult, [("F", bb), "csq"], ["r1"])
            self.tt("pool", QR[0:64, h, 0:W], r0[0:64, 0:W], r1[0:64, 0:W], ALU.add, ["r0", "r1"], [("QRr", h)])
        if is_halo:
            self.memset("pool", attnT.rearrange("p a b -> p (a b)"), 0.0, [("at", h_) for h_ in range(NH)])
        S.ph = kind_ + ".attn"
        self.attention(groups)
        S.ph = kind_ + ".proj"
        for pi_, pid in enumerate((P_AB0, P_AB1)):
            Wab, wk = self.w_get(pid)
            for jj in range(4):
                j = 4 * pi_ + jj
                ba, bb = self.bank(), self.bank()
                for kc in range(8):
                    self.mm(F[ba][:, 0:W], Wab[:, kc, jj * 128:(jj + 1) * 128], uT[:, kc, 0:W], kc == 0, kc == 7,
                            [("uT", k_) for k_ in range(8)] + [wk], [("F", ba)])
                for kc in range(8):
                    self.mm(F[bb][:, 0:W], Wab[:, kc, 512 + jj * 128:512 + (jj + 1) * 128], attnT[:, kc, 0:W], kc == 0, kc == 7,
                            [("at", k_) for k_ in range(8)] + [wk], [("F", bb)])
                t0, t1 = self.tmpm[j % 2], self.rec[j % 2]
                self.tt("dve", t0[:, 0:W], F[ba][:, 0:W], gT[:, j, 0:W], ALU.mult, [("F", ba), ("gT", j)], [("sacc", j % 2)])
                self.tt("dve", t1[:, 0:W], F[bb][:, 0:W], gT[:, 8 + j, 0:W], ALU.mult, [("F", bb), ("gT", 8 + j)], [("rec", j % 2)])
                self.tt("pool", bufA[:, j, 0:W], t0[:, 0:W], t1[:, 0:W], ALU.add, [("sacc", j % 2), ("rec", j % 2)],
                        [("bA", s) for s in range(NS)])
        S.soft_barrier()
        S.ph = kind_ + ".ffn"
        Wo, wk = self.w_get(P_O)
        for s in range(NS):
            for half in range(2):
                bk = self.bank()
                for kc in range(8):
                    self.mm(F[bk][:, 0:512], bufA[:, kc, s * 128:(s + 1) * 128], Wo[:, kc, half * 512:(half + 1) * 512],
                            kc == 0, kc == 7, [("bA", s), wk], [("F", bk)])
                self.tt("dve", X[:, s, half * 512:(half + 1) * 512], F[bk][:, 0:512], X[:, s, half * 512:(half + 1) * 512],
                        ALU.add, [("F", bk), ("X", s)], [("X", s)])
        for s in range(NS):
            self.act(self.junk, X[:, s, :], AF.Square, [("X", s)], ["junk", ("hss", s)], accum=st[:, 48 + s:49 + s])
            self.rstd(st[:, 48 + s:49 + s], st[:, 52 + s:53 + s], 1, 1.0 / D, ("hss", s), ("hrs", s))
            if s % 2 == 0:
                self.act(self.hnb[s % 2], X[:, s, :], AF.Copy, [("X", s), ("hrs", s)], [("hnb", s % 2)], scale=st[:, 52 + s:53 + s])
            else:
                self.ts("dve", self.hnb[s % 2], X[:, s, :], st[:, 52 + s:53 + s], None, ALU.mult, None,
                        [("X", s), ("hrs", s)], [("hnb", s % 2)])
            for hf in range(2):
                Tv, tk = self.tbank()
                for j in range(4):
                    kc = 4 * hf + j
                    self.tr(Tv[:, j * 128:(j + 1) * 128],
                            self.hnb[s % 2][:, kc * 128:(kc + 1) * 128], [("hnb", s % 2)], [tk])
                self.cp(self.evac_eng(), bufA[:, 4 * hf:4 * hf + 4, s * 128:(s + 1) * 128],
                        Tv.rearrange("p (a b) -> p a b", a=4), [tk], [("bA", s)])
        def v3(ap, lo, n):
            return ap[:, 0:nb * (wb + 2)].rearrange("p (a b) -> p a b", a=nb)[:, :, lo:lo + n]
        for t in range(6):
            Wup, wk = self.w_get(P_UP0 + t)
            npair = 4 if t < 5 else 2
            for q in range(npair):
                p = 4 * t + q
                sl = p % 2
                bv, bg = self.bank(), self.bank()
                for kc in range(8):
                    self.mm(F[bv][:, 0:W], Wup[:, kc, q * 256:q * 256 + 128], bufA[:, kc, 0:W], kc == 0, kc == 7,
                            bA_all + [wk], [("F", bv)])
                for kc in range(8):
                    self.mm(F[bg][:, 0:W], Wup[:, kc, q * 256 + 128:q * 256 + 256], bufA[:, kc, 0:W], kc == 0, kc == 7,
                            bA_all + [wk], [("F", bg)])
                for k_, (bk, chunk) in enumerate(((bv, p), (bg, NPAIR + p))):
                    U = self.U[sl][k_]
                    uk = ("U", sl, k_)
                    self.cp("act", v3(U, 2, wb), F[bk][:, 0:W].rearrange("p (a b) -> p a b", a=nb), [("F", bk)], [uk])
                    self.ts("pool", v3(U, 0, 2), self.prevsave[:, chunk, 0:nb, :], hs_col, None, ALU.mult, None,
                            [("prev", chunk), "hscale", "onescol"], [uk])
                    self.cp("pool", self.prevsave[:, chunk, 0:nb, :], v3(U, wb, 2), [uk], [("prev", chunk)])
                    if is_halo:
                        continue
                    c = self.cv[sl][k_][:, 0:W].rearrange("p (a b) -> p a b", a=nb)
                    ck = ("cv", sl, k_)
                    cw = self.convw
                    self.act(c, v3(U, 0, wb), AF.Identity, [uk, "convw"], [ck], scale=cw[:, chunk, 0:1],
                             bias=cw[:, chunk, 3:4])
                    self.stt(c, v3(U, 1, wb), cw[:, chunk, 1:2], c, ALU.mult, ALU.add, [uk, ck, "convw"], [ck])
                    self.stt(c, v3(U, 2, wb), cw[:, chunk, 2:3], c, ALU.mult, ALU.add, [uk, ck, "convw"], [ck])
                if is_halo:
                    continue
                self.act(self.sg[sl][:, 0:W], self.cv[sl][1][:, 0:W], AF.Silu, [("cv", sl, 1)], [("sg", sl)])
                self.tt("dve", self.actT[:, p, 0:W], self.sg[sl][:, 0:W], self.cv[sl][0][:, 0:W], ALU.mult,
                        [("sg", sl), ("cv", sl, 0)], [("aT", p)])
        if is_halo:
            return
        aT_all = [("aT", p) for p in range(NPAIR)]
        for half in range(2):
            for part in range(2):
                Wd, wk = self.w_get(P_D0 + 2 * half + part)
                for s in range(NS):
                    for kc in range(11):
                        self.mm(F[s][:, 0:512], self.actT[:, part * 11 + kc, s * 128:(s + 1) * 128], Wd[:, kc, 0:512],
                                part == 0 and kc == 0, part == 1 and kc == 10, aT_all + [wk], [("F", s)])
            for s in range(NS):
                self.tt("dve", X[:, s, half * 512:(half + 1) * 512], F[s][:, 0:512], X[:, s, half * 512:(half + 1) * 512],
                        ALU.add, [("F", s), ("X", s)], [("X", s)])
        for s in range(NS):
            self.act(self.junk, X[:, s, :], AF.Square, [("X", s)], ["junk", ("yss", s)], accum=st[:, 56 + s:57 + s])
            self.rstd(st[:, 56 + s:57 + s], st[:, 60 + s:61 + s], 1, 1.0 / D, ("yss", s), ("yrs", s))
            self.stt(self.Y[s % 2], X[:, s, :], st[:, 60 + s:61 + s], self.gfin, ALU.mult, ALU.mult,
                     [("X", s), ("yrs", s), "gfin"], [("Y", s % 2)])
            self.dma(y_out[s * 128:(s + 1) * 128, :], self.Y[s % 2], [("Y", s % 2)], [], ("Y", s % 2))
        if conv_out is not None:
            for seg in range(nb):
                for t_ in range(2):
                    self.dma(conv_out[seg][t_].rearrange("(c p) -> p c", p=128), self.prevsave[:, :, seg, t_],
                             [("prev", c_) for c_ in range(NFC)], [], ("convo", seg, t_), slow=True)

    def attention(self, groups):
        F = self.F
        QR, qnopeT, attnT, MK = self.QR, self.qnopeT, self.attnT, self.MK
        kvctr = 0
        uctr = 0
        for (c0, ncol, kvgroups) in groups:
            cols = slice(c0, c0 + ncol)
            nblk = sum(len(js) for _, js in kvgroups)
            for hp in range(4):
                pend = []

                def flush(pend_item):
                    (slot, j, hh, bi, pt) = pend_item
                    kv = self.KV[slot][:, j, :]
                    self.mm(F[hh][:, 0:ncol], kv[:, 384 + hh * 128:384 + (hh + 1) * 128], self.PT[pt][:, 0:ncol],
                            bi == 0, bi == nblk - 1, [("kv", slot), ("PT", pt)], [("F", hh)])
                    sa = self.sacc[hh]
                    if bi == 0:
                        self.cp("dve", sa[:, 0:ncol], self.PT[pt][:, 0:ncol], [("PT", pt)], [("sacc", hh)])
                    else:
                        self.tt("dve", sa[:, 0:ncol], sa[:, 0:ncol], self.PT[pt][:, 0:ncol], ALU.add,
                                [("PT", pt), ("sacc", hh)], [("sacc", hh)])

                bi = -1
                for (grp, js) in kvgroups:
                    slot = kvctr % self.NKV
                    kvctr += 1
                    self.dma(self.KV[slot], self.kvscr[hp, grp], [("kvs", grp, hp)], [("kv", slot)], ("kv", slot))
                    for (j, mu) in js:
                        bi += 1
                        for hh in range(2):
                            h = 2 * hp + hh
                            sb_ = 4 + (uctr % 4)
                            pt = uctr % 4
                            uctr += 1
                            kv = self.KV[slot][:, j, :]
                            masked = mu is not None
                            self.mm(F[sb_][:, 0:ncol], kv[:, hh * 128:(hh + 1) * 128], qnopeT[:, h, cols], True, False,
                                    [("kv", slot), ("qno", h)], [("F", sb_)])
                            self.mm(F[sb_][:, 0:ncol], kv[:, 256:384], QR[:, h, cols], False, not masked,
                                    [("kv", slot), ("QRr", h), ("QRb", h), "QRhi"], [("F", sb_)])
                            if masked:
                                self.mm(F[sb_][:, 0:ncol], MK[:, mu, :], QR[:, h, cols], False, True,
                                        ["MK", ("QRb", h)], [("F", sb_)])
                            if len(pend) >= 2:
                                flush(pend.pop(0))
                            self.act(self.PT[pt][:, 0:ncol], F[sb_][:, 0:ncol], AF.Exp, [("F", sb_)], [("PT", pt)],
                                     scale=ATT_SCALE)
                            pend.append((slot, j, hh, bi, pt))
                while pend:
                    flush(pend.pop(0))
                for hh in range(2):
                    h = 2 * hp + hh
                    rc = self.rec[hh]
                    self.mm(F[2 + hh][:, 0:ncol], self.ones32, self.sacc[hh][:, 0:ncol], True, True,
                            ["ones32", ("sacc", hh)], [("F", 2 + hh)])
                    self.S.op("dve", lambda e, rc=rc, hh=hh, ncol=ncol: e.reciprocal(out=rc[:, 0:ncol], in_=F[2 + hh][:, 0:ncol]),
                              reads=[("F", 2 + hh)], writes=[("rec", hh)])
                    self.tt("dve", attnT[:, h, cols], F[hh][:, 0:ncol], rc[:, 0:ncol], ALU.mult,
                            [("F", hh), ("rec", hh)], [("at", h)])

    def build(self):
        cfg = self.cfg
        self.declare()
        I, O = self.I, self.O
        tl = cfg.tiles()
        base_pieces = [P_U, P_V, P_Q, P_G0, P_G1, P_UQN, P_UQR, P_AB0, P_AB1, P_O] + [P_UP0 + t for t in range(6)]
        dn = [P_D0 + i for i in range(4)]
        for t in tl:
            self.wsched += base_pieces + ([] if t["kind"] == "halo" else dn)
        self.wsched += base_pieces + dn
        self.S.alias_names = {"xnb", "vg", "vb", "qnb", "bst", "qno", "QRr", "QRb", "QRhi", "at", "MK", "MK0", "r0", "r1",
                              "csq", "PT", "kv", "rec", "tm", "sacc", "hnb", "aT", "U", "cv", "sg", "Y"}
        self.setup()
        self.prepass()
        ones_col = self.stats[:, 95:96]
        for ti, t in enumerate(tl):
            W = t["W"]
            blocks = list(range(t["nkb"]))
            mku = {n: n - t["mlo"] for n in range(t["mlo"], t["nkb"])}
            halo = t["kind"] == "halo"
            if halo:
                hs = ones_col
            elif t["i"] == 0:
                hs = self.hscale[:, t["J"]:t["J"] + 1]
            else:
                hs = ones_col
            y_out = None
            conv_out = None
            if not halo:
                r0 = t["J"] * cfg.B + t["i"] * cfg.W
                y_out = O["y_own"][r0:r0 + W, :]
                if ti == len(tl) - 1:
                    conv_out = [O["conv_last"]]
            kvg = [(g_, [(j_, mku.get(4 * g_ + j_)) for j_ in range(4) if 4 * g_ + j_ < t["nkb"]])
                   for g_ in range((t["nkb"] + 3) // 4)]
            grp = [(W - 2, 2, kvg)] if halo else [(0, W, kvg)]
            self.tile(W, I["x_own"][t["tok0"]:t["tok0"] + W, :], I["csq"][:, :, t["tok0"]:t["tok0"] + W],
                      I["brow_p"], ti, grp, 1, hs, halo, False, y_out=y_out, conv_out=conv_out)
        self.dma(self.prevsave.rearrange("p a b c -> p (a b c)"), I["conv_state"], [],
                 [("prev", c_) for c_ in range(NFC)], "prevs")
        groups = []
        for b in range(cfg.NSB):
            kvg = [(cfg.NGP + b * cfg.NGC + g_, [(j_, None) for j_ in range(4)]) for g_ in range(cfg.NGC)]
            kvg.append((cfg.GRP_NEW, [(b // 2, b)]))
            groups.append((b * cfg.DS, cfg.DS, kvg))
        self.tile(cfg.WS, I["x_smp"], I["csq_s"], I["brow_s"], len(tl), groups, cfg.NSB, ones_col, False, True,
                  y_out=O["y_smp"], gv_out=O["gv_smp"], conv_out=[O["conv_smp"][b] for b in range(cfg.NSB)])
        self.S.barrier()
        nw = self.S.emit(self.sems)
        self.stats_info = dict(ops=len(self.S.ops), waits=nw, sbuf_words=self.A.hi, dsems=len(self.dsems))
        return self.nc


_PROG_CACHE = {}


def run_cfg(inputs, cfg):
    in_maps = host_prep(inputs, cfg)
    key = (cfg.SEQ, cfg.W, cfg.NJ, cfg.PAST, cfg.NSB, cfg.DS)
    if key not in _PROG_CACHE:
        p = Prog(cfg)
        p.build()
        _PROG_CACHE[key] = p
    p = _PROG_CACHE[key]
    res = run_bass_kernel_spmd(p.nc, in_maps, core_ids=list(range(8)))
    R = res.results
    nb_, SEQ = 2, cfg.SEQ
    y_p = np.zeros((nb_, SEQ, D), np.float32)
    ckv_p = np.zeros((1, nb_, SEQ, KVL), np.float32)
    kr_p = np.zeros((1, nb_, SEQ, ROPE), np.float32)
    cv_p = np.zeros((1, nb_, 2, 2 * DFF), np.float32)
    nsb = 8 * cfg.NSB
    y_s = np.zeros((nsb, cfg.DS, D), np.float32)
    ckv_s = np.zeros((1, nsb, cfg.DS, KVL), np.float32)
    kr_s = np.zeros((1, nsb, cfg.DS, ROPE), np.float32)
    cv_s = np.zeros((1, nsb, 2, 2 * DFF), np.float32)
    gv_s = np.zeros((1, nsb, cfg.DS, GW), np.float32)
    for c in range(8):
        b, r = c // 4, c % 4
        o = R[c]
        yo = np.asarray(o["y_own"], np.float32)
        for J in range(cfg.NJ):
            g = 4 * J + r
            y_p[b, g * cfg.B:(g + 1) * cfg.B] = yo[J * cfg.B:(J + 1) * cfg.B]
        if r == 0:
            ckv_p[0, b] = np.asarray(o["ckv_seq"], np.float32)
            kr_p[0, b] = np.asarray(o["kr_seq"], np.float32)
        if r == 3:
            cv_p[0, b] = np.asarray(o["conv_last"], np.float32)
        sl = slice(c * cfg.NSB, (c + 1) * cfg.NSB)
        y_s[sl] = np.asarray(o["y_smp"], np.float32).reshape(cfg.NSB, cfg.DS, D)
        ckv_s[0, sl] = np.asarray(o["ckv_smp"], np.float32).reshape(cfg.NSB, cfg.DS, KVL)
        kr_s[0, sl] = np.asarray(o["kr_smp"], np.float32).reshape(cfg.NSB, cfg.DS, ROPE)
        cv_s[0, sl] = np.asarray(o["conv_smp"], np.float32)
        gv_s[0, sl] = np.asarray(o["gv_smp"], np.float32).reshape(cfg.NSB, cfg.DS, GW)
    return (y_p, y_s, ckv_p, kr_p, cv_p, ckv_s, kr_s, cv_s, gv_s)


def kernel(**inputs):
    inputs = {k: np.asarray(v) for k, v in inputs.items()}
    cfg = Cfg(SEQ=inputs["x_prompt"].shape[1], W=512, NJ=inputs["x_prompt"].shape[1] // 4096,
              PAST=inputs["cache_mla_ckv"].shape[2], NSB=inputs["x_sample"].shape[0] // 8,
              DS=inputs["x_sample"].shape[1])
    return run_cfg(inputs, cfg)
```

```python
import contextlib
import math
import numpy as np
import ml_dtypes
import concourse.bass as bass
import concourse.mybir as mybir
from concourse.bass_utils import run_bass_kernel_spmd

F32 = mybir.dt.float32
BF16 = mybir.dt.bfloat16
AF = mybir.ActivationFunctionType
ALU = mybir.AluOpType

D = 1024
GW = 1024
QL = 512
KVL = 512
ROPE = 64
NOPE = 128
NH = 8
DFF = 2816
NFC = 44
NPAIR = 22
EPS = 1e-6
ATT_SCALE = (NOPE + ROPE) ** -0.5
BIG = 30000.0
PW = 8192
KVC = 640


class Cfg:
    def __init__(s, SEQ=16384, W=512, NJ=4, PAST=4096, NSB=4, DS=64, NBATCH=2, DEC_BATCH=32):
        s.SEQ, s.W, s.NJ, s.PAST, s.NSB, s.DS = SEQ, W, NJ, PAST, NSB, DS
        s.NBATCH, s.DEC_BATCH = NBATCH, DEC_BATCH
        s.B = 2 * W
        s.B8 = s.B // 128
        s.W8 = W // 128
        assert 4 * NJ * s.B == SEQ
        s.NBP = SEQ // 128
        s.NBC = PAST // 128
        s.WS = NSB * DS
        assert s.WS % 128 == 0
        s.NBS_NEW = s.WS // 128
        s.NBLK = s.NBP + NSB * s.NBC + s.NBS_NEW
        s.OWN = NJ * (128 + s.B)
        s.NMM = 3 * s.B8 + s.W8
        assert s.NBP % 4 == 0 and s.NBC % 4 == 0 and s.NBS_NEW <= 4
        s.NGP = s.NBP // 4
        s.NGC = s.NBC // 4
        s.GRP_NEW = s.NGP + NSB * s.NGC
        s.NGRP = s.GRP_NEW + 1

    def tiles(s):
        out = []
        for J in range(s.NJ):
            base = J * (128 + s.B)
            nk = s.B8 * (4 * J + 3)
            out.append(dict(kind="halo", J=J, i=-1, W=128, tok0=base, nkb=nk,
                            mlo=max(0, s.B8 * 4 * J - 1)))
            for i in range(2):
                nk2 = nk + s.W8 * (i + 1)
                out.append(dict(kind="tile", J=J, i=i, W=s.W, tok0=base + 128 + i * s.W, nkb=nk2,
                                mlo=nk2 - s.NMM))
        return out


class Op:
    __slots__ = ("id", "eng", "fn", "deps", "sem", "inc", "has_dep", "cnt", "waits", "ph")


ENGS = ("pe", "act", "dve", "pool", "sp")


class Sched:
    def __init__(self, nc):
        self.nc = nc
        self.ops = []
        self.last_w = {}
        self.readers = {}
        self.floor = None
        self.last_eng = {}
        self.dma_since = []
        self.ph = "init"
        self.scopes = False
        self.soft = None
        self.alias_names = set()
        self.last_dma = {}

    def op(self, eng, fn, reads=(), writes=(), dma_sem=None, extra=()):
        o = Op()
        o.id = len(self.ops)
        o.eng = eng
        o.fn = fn
        o.sem = dma_sem
        o.has_dep = False
        o.ph = self.ph
        deps = {}
        is_dma = dma_sem is not None

        def add(d, raw):
            if d is None:
                return
            p = self.ops[d]
            if (not raw) and (not is_dma) and p.sem is None and p.eng == eng and eng == "pe":
                return
            deps[d] = True

        for b in reads:
            for d in self.last_w.get(b, {}).values():
                add(d, True)
        for b in writes:
            for d in self.last_w.get(b, {}).values():
                add(d, False)
            for d in self.readers.get(b, {}).values():
                add(d, False)
        for d in extra:
            deps[d] = True
        if self.soft is not None:
            for b in writes:
                nm = b if isinstance(b, str) else b[0]
                if nm in self.alias_names:
                    for d in self.soft:
                        deps[d] = True
                    break
        if self.floor is not None:
            deps[self.floor] = True
        deps.pop(o.id, None)
        o.deps = list(deps)
        who = dma_sem if is_dma else eng
        for b in reads:
            self.readers.setdefault(b, {})[who] = o.id
        for b in writes:
            self.last_w.setdefault(b, {})[who] = o.id
        self.ops.append(o)
        self.last_eng[eng] = o.id
        if is_dma:
            self.dma_since.append(o.id)
            self.last_dma[dma_sem] = o.id
        return o.id

    def soft_barrier(self):
        self.soft = list(self.last_eng.values()) + list(self.last_dma.values())

    def barrier(self):
        extra = list(self.last_eng.values()) + list(self.dma_since)
        self.floor = None
        bid = self.op("sp", lambda e: e.nop(), extra=extra)
        self.floor = bid
        self.dma_since = []
        self.last_w = {}
        self.readers = {}
        self.soft = None
        return bid

    def emit(self, sems):
        ops = self.ops
        for o in ops:
            for d in o.deps:
                ops[d].has_dep = True
        cnt = {}
        waited = {e: {} for e in ENGS}
        per_eng = {e: [] for e in ENGS}
        n_wait = 0
        for o in ops:
            need = {}
            for d in o.deps:
                p = ops[d]
                is_dma = p.sem is not None
                if (not is_dma) and p.eng == "pe" and o.eng == "pe":
                    continue
                key = p.sem if is_dma else sems[p.eng]
                v = p.cnt
                if need.get(key, 0) < v:
                    need[key] = v
            o.waits = []
            for key, v in need.items():
                if waited[o.eng].get(key, 0) >= v:
                    continue
                o.waits.append((key, v))
                waited[o.eng][key] = v
                n_wait += 1
            if o.sem is not None:
                c = cnt.get(o.sem, 0) + 16
                cnt[o.sem] = c
                o.cnt = c
                o.inc = (o.sem, 16)
            elif o.has_dep:
                key = sems[o.eng]
                c = cnt.get(key, 0) + 1
                cnt[key] = c
                o.cnt = c
                o.inc = (key, 1)
            else:
                o.cnt = None
                o.inc = None
            per_eng[o.eng].append(o)

        def run1(h, o):
            for key, v in o.waits:
                h.wait_ge(key, v)
            ins = o.fn(h)
            if o.inc is not None:
                ins.then_inc(o.inc[0], o.inc[1])

        def run(h, lst):
            if not self.scopes:
                for o in lst:
                    run1(h, o)
                return
            i = 0
            while i < len(lst):
                j = i
                while j < len(lst) and lst[j].ph == lst[i].ph:
                    j += 1
                with self.nc.named_scope(lst[i].ph):
                    for o in lst[i:j]:
                        run1(h, o)
                i = j

        with self.nc.Block() as block:
            @block.tensor
            def _(e):
                run(e, per_eng["pe"])

            @block.scalar
            def _(e):
                run(e, per_eng["act"])

            @block.vector
            def _(e):
                run(e, per_eng["dve"])

            @block.gpsimd
            def _(e):
                run(e, per_eng["pool"])

            @block.sync
            def _(e):
                run(e, per_eng["sp"])
        return n_wait


class Arena:
    def __init__(self, tile, nwords):
        self.t = tile
        self.n = nwords
        self.tops = {"": 0}
        self.parent = {}
        self.hi = 0

    def phase(self, name, parent=""):
        self.parent[name] = parent
        self.tops[name] = None

    def _top(self, ph):
        if self.tops[ph] is None:
            self.tops[ph] = self._top(self.parent[ph])
        return self.tops[ph]

    def alloc(self, ph, shape, dt):
        nel = 1
        for d in shape[1:]:
            nel *= d
        nbytes = nel * (4 if dt == F32 else 2)
        nw = (nbytes + 3) // 4
        nw = (nw + 7) // 8 * 8
        off = self._top(ph)
        assert off + nw <= self.n, ("SBUF arena overflow", ph, shape, off, nw, self.n)
        self.tops[ph] = off + nw
        self.hi = max(self.hi, off + nw)
        ap = self.t[:, off:off + nw]
        if dt != F32:
            ap = ap.bitcast(dt)
        ap = ap[:, 0:nel]
        if len(shape) == 3:
            ap = ap.rearrange("p (a b) -> p a b", a=shape[1])
        elif len(shape) == 4:
            ap = ap.rearrange("p (a b c) -> p a b c", a=shape[1], b=shape[2])
        if shape[0] != 128:
            ap = ap[0:shape[0]]
        return ap


def _bf16_bits(a):
    return np.ascontiguousarray(np.asarray(a, np.float32).astype(ml_dtypes.bfloat16)).view(np.uint16)


def _piece(arr):
    K, nc_ = arr.shape
    nkc = K // 128
    a = arr.reshape(nkc, 128, nc_).transpose(1, 0, 2).reshape(128, nkc * nc_)
    out = np.zeros((128, PW), np.float32)
    out[:, :nkc * nc_] = a
    return out, nkc, nc_


def _gtab(g, nkc):
    t = np.ones((128, 12), np.float32)
    if g is not None:
        t[:, :nkc] = np.asarray(g, np.float32).reshape(nkc, 128).T
    return t


def weight_pieces(inp):
    w_in = inp["w_in"][0]
    mix = inp["norm_mix_g"][0]
    qn = inp["mla_q_norm_g"][0]
    ffn = inp["norm_ffn_g"][0]
    w_uq = inp["mla_w_uq"][0]
    w_up = inp["ffn_w_up"][0]
    w_dn = inp["ffn_w_down"][0]
    perm = (np.arange(64) + 32) % 64
    rope = w_uq[:, :, 128:192]
    ropep = rope[:, :, perm]
    specs = [
        (w_in[:, 2560:3136], mix),
        (inp["mla_w_uk"][0].reshape(512, 1024), None),
        (inp["mla_w_uv"][0].reshape(512, 1024), None),
        (w_in[:, 0:1024], mix),
        (w_in[:, 1024:2048], mix),
        (w_in[:, 2048:2560], mix),
        (w_in[:, 3136:4160], mix),
        (w_in[:, 4160:5184], mix),
        (w_uq[:, :, 0:128].reshape(512, 1024), qn),
        (np.concatenate([rope, ropep, ropep, rope], axis=2).reshape(512, 2048), qn),
        (np.concatenate([inp["w_proj_a"][0][:, 0:512], inp["w_proj_b"][0][:, 0:512]], axis=1), None),
        (np.concatenate([inp["w_proj_a"][0][:, 512:1024], inp["w_proj_b"][0][:, 512:1024]], axis=1), None),
        (inp["w_out"][0], None),
    ]
    for t in range(6):
        cols = []
        for p in range(4 * t, min(4 * t + 4, NPAIR)):
            cols.append(w_up[:, p * 128:(p + 1) * 128])
            cols.append(w_up[:, DFF + p * 128:DFF + (p + 1) * 128])
        specs.append((np.concatenate(cols, axis=1), ffn))
    for half in range(2):
        for part in range(2):
            specs.append((w_dn[part * 1408:(part + 1) * 1408, half * 512:(half + 1) * 512], None))
    wraw = np.zeros((len(specs), 128, PW), np.float32)
    wgt = np.ones((128, len(specs), 12), np.float32)
    meta = []
    for i, (a, g) in enumerate(specs):
        wraw[i], nkc, nc_ = _piece(np.ascontiguousarray(a, dtype=np.float32))
        wgt[:, i, :] = _gtab(g, nkc)
        meta.append((nkc, nc_))
    return wraw, wgt, meta


PIECE_META = [(8, 576), (4, 1024), (4, 1024), (8, 1024), (8, 1024), (8, 512), (8, 1024), (8, 1024),
              (4, 1024), (4, 2048), (8, 1024), (8, 1024), (8, 1024),
              (8, 1024), (8, 1024), (8, 1024), (8, 1024), (8, 1024), (8, 512),
              (11, 512), (11, 512), (11, 512), (11, 512)]
NPW = len(PIECE_META)
P_KV, P_UK, P_UV, P_U, P_V, P_Q, P_G0, P_G1, P_UQN, P_UQR, P_AB0, P_AB1, P_O, P_UP0, P_D0 = \
    0, 1, 2, 3, 4, 5, 6, 7, 8, 9, 10, 11, 12, 13, 19
TILE_PIECES = list(range(3, NPW))


def _cs_tables(pos):
    half = ROPE // 2
    inv = (np.float32(10000.0) ** (-np.arange(half, dtype=np.float32) / np.float32(half))).astype(np.float32)
    ang = pos.astype(np.float32)[:, None] * inv[None, :]
    return np.cos(ang).astype(np.float32), np.sin(ang).astype(np.float32)


def _mask_rows(cfg, qstart, W, blocks, fake=False):
    out = np.zeros((8, cfg.NMM * 128), np.float32)
    if fake:
        return out
    for u, n in enumerate(blocks):
        kc = (n * 128 + np.arange(128)) // 64
        for j in range(W // 64):
            qc = qstart // 64 + j
            out[j, u * 128:(u + 1) * 128] = np.where(kc > qc, -BIG, 0.0)
    return out


def host_prep(inp, cfg):
    wraw, wgt, meta = weight_pieces(inp)
    assert meta == PIECE_META, meta
    xp = np.asarray(inp["x_prompt"], np.float32)
    xs = np.asarray(inp["x_sample"], np.float32)
    ckv_c = np.asarray(inp["cache_mla_ckv"], np.float32)[0]
    kr_c = np.asarray(inp["cache_mla_krope"], np.float32)[0]
    cst = np.asarray(inp["state_ffn_conv"], np.float32)[0]
    tl = cfg.tiles()
    shared = {}
    shared["wraw"] = wraw
    shared["wgt"] = wgt
    shared["ident"] = _bf16_bits(np.eye(128))
    shared["onesb"] = _bf16_bits(np.ones((128, 128)))
    ok = np.zeros((128, 128), np.float32)
    ok[0, :] = 1.0
    ok[32, :] = 1.0
    shared["onesk"] = _bf16_bits(ok)
    w_s = np.asarray(inp["gmlp_w_s"], np.float32)[0]
    shared["wst"] = np.ascontiguousarray(w_s.transpose(2, 0, 1))
    tri = (np.arange(128)[:, None] <= np.arange(128)[None, :]).astype(np.float32)
    shared["tri"] = np.ascontiguousarray(np.broadcast_to(tri[:, None, :], (128, 8, 128)))
    wss = np.zeros((128, 8, 128), np.float32)
    w64 = w_s[:, :64, :64].transpose(2, 0, 1)
    wss[0:64, :, 0:64] = w64
    wss[64:128, :, 64:128] = w64
    shared["wss"] = wss
    tri2 = np.zeros((128, 128), np.float32)
    tri2[0:64, 0:64] = tri[0:64, 0:64]
    tri2[64:128, 64:128] = tri[0:64, 0:64]
    shared["tri2"] = np.ascontiguousarray(np.broadcast_to(tri2[:, None, :], (128, 8, 128)))
    b_s = np.asarray(inp["gmlp_b_s"], np.float32)[0]
    bsr = np.zeros((128, 2, 8, 128), np.float32)
    bsr[0, 0] = b_s
    bsr[32, 0] = b_s
    bs64 = np.concatenate([b_s[:, :64], b_s[:, :64]], axis=1)
    bsr[0, 1] = bs64
    bsr[32, 1] = bs64
    shared["bsr"] = bsr.reshape(128, 2 * 8 * 128)
    bc = lambda v, n: np.ascontiguousarray(np.broadcast_to(np.asarray(v, np.float32).reshape(1, n), (128, n)))
    shared["lng"] = bc(inp["gmlp_ln_g"][0], GW)
    shared["lnb"] = bc(inp["gmlp_ln_b"][0], GW)
    shared["gfin"] = bc(inp["final_norm_g"], D)
    shared["gkv"] = bc(inp["mla_kv_norm_g"][0], KVL)
    cw = np.asarray(inp["ffn_conv_w"], np.float32)[0]
    cb = np.asarray(inp["ffn_conv_b"], np.float32)[0]
    cwt = np.zeros((128, NFC, 4), np.float32)
    for k in range(3):
        cwt[:, :, k] = cw[k].reshape(NFC, 128).T
    cwt[:, :, 3] = cb.reshape(NFC, 128).T
    shared["convw"] = cwt.reshape(128, NFC * 4)
    bp = np.zeros((8, 512), np.float32)
    for j in range(8):
        bp[j, j * 64:(j + 1) * 64] = 1.0
    shared["brow_p"] = _bf16_bits(bp)
    bsm = np.zeros((8, 512), np.float32)
    bsm[0, :] = 1.0
    shared["brow_s"] = _bf16_bits(bsm)
    pos_s = cfg.PAST + np.arange(cfg.DS)
    cs_, sn_ = _cs_tables(pos_s)
    cs_tok_s = np.tile(np.concatenate([cs_, sn_], axis=1), (cfg.NSB, 1))
    shared["cs_tok_s"] = np.ascontiguousarray(cs_tok_s, dtype=np.float32)
    c2 = np.concatenate([cs_, cs_], axis=1).T
    s2 = np.concatenate([-sn_, sn_], axis=1).T
    csq_s = np.stack([np.tile(c2, (1, cfg.NSB)), np.tile(s2, (1, cfg.NSB))], axis=0)
    shared["csq_s"] = np.ascontiguousarray(csq_s.transpose(1, 0, 2), dtype=np.float32)
    cs_p, sn_p = _cs_tables(np.arange(cfg.SEQ))
    shared["cs_tok"] = np.ascontiguousarray(np.concatenate([cs_p, sn_p], axis=1), dtype=np.float32)

    in_maps = []
    for c in range(8):
        b, r = c // 4, c % 4
        m = dict(shared)
        m["x_seq"] = np.ascontiguousarray(xp[b])
        xo = np.zeros((cfg.OWN, D), np.float32)
        pos_own = np.zeros((cfg.OWN,), np.int64)
        hs = np.ones((128, cfg.NJ), np.float32)
        mk = np.zeros((len(tl) + 1, 8, cfg.NMM * 128), np.float32)
        for ti, t in enumerate(tl):
            g = 4 * t["J"] + r
            if t["kind"] == "halo":
                q0 = g * cfg.B - 128
                fake = q0 < 0
                if fake:
                    hs[:, t["J"]] = 0.0
                else:
                    xo[t["tok0"]:t["tok0"] + 128] = xp[b, q0:q0 + 128]
                    pos_own[t["tok0"]:t["tok0"] + 128] = np.arange(q0, q0 + 128)
            else:
                q0 = g * cfg.B + t["i"] * cfg.W
                fake = False
                xo[t["tok0"]:t["tok0"] + cfg.W] = xp[b, q0:q0 + cfg.W]
                pos_own[t["tok0"]:t["tok0"] + cfg.W] = np.arange(q0, q0 + cfg.W)
            blocks = list(range(t["mlo"], t["nkb"]))
            mk[ti] = _mask_rows(cfg, max(q0, 0), t["W"], blocks, fake=fake)
        smk = np.zeros((8, cfg.NMM * 128), np.float32)
        for gidx in range(cfg.NSB):
            lo = 64 if gidx % 2 == 0 else 0
            smk[0, gidx * 128 + lo:gidx * 128 + lo + 64] = -BIG
        mk[len(tl)] = smk
        m["maskt"] = _bf16_bits(mk)
        m["x_own"] = xo
        m["hscale"] = hs
        co, so = _cs_tables(pos_own)
        c2o = np.concatenate([co, co], axis=1).T
        s2o = np.concatenate([-so, so], axis=1).T
        m["csq"] = np.ascontiguousarray(np.stack([c2o, s2o], axis=1), dtype=np.float32)
        sb0 = c * cfg.NSB
        m["x_smp"] = np.ascontiguousarray(xs[sb0:sb0 + cfg.NSB].reshape(cfg.WS, D))
        m["ckv_cache"] = np.ascontiguousarray(ckv_c[sb0:sb0 + cfg.NSB].reshape(cfg.NSB * cfg.PAST, KVL))
        m["kr_cache"] = np.ascontiguousarray(kr_c[sb0:sb0 + cfg.NSB].reshape(cfg.NSB * cfg.PAST, ROPE))
        st = cst[sb0:sb0 + cfg.NSB]
        m["conv_state"] = np.ascontiguousarray(st.reshape(cfg.NSB, 2, NFC, 128).transpose(3, 2, 0, 1)).reshape(128, NFC * cfg.NSB * 2)
        in_maps.append(m)
    return in_maps


class Prog:
    def __init__(self, cfg):
        self.cfg = cfg
        self.nc = bass.Bass("TRN2", target_bir_lowering=False)
        self.es = contextlib.ExitStack()
        self.S = Sched(self.nc)
        self.dsems = {}
        self.bank_rr = 0
        self.alt = 0
        self.wpos = 0
        self.wsched = []
        self.wloaded = 0

    def din(self, name, shape, dt=F32):
        return self.nc.dram_tensor(name, list(shape), dt, kind="ExternalInput").ap()

    def dout(self, name, shape, dt=F32):
        return self.nc.dram_tensor(name, list(shape), dt, kind="ExternalOutput").ap()

    def dint(self, name, shape, dt):
        return self.nc.dram_tensor(name, list(shape), dt, kind="Internal").ap()

    def dsem(self, key):
        if key not in self.dsems:
            self.dsems[key] = self.es.enter_context(self.nc.semaphore("d%d" % len(self.dsems)))
        return self.dsems[key]

    def dma(self, out, in_, reads, writes, semkey, slow=False):
        sem = self.dsem(semkey)
        if slow:
            fn = lambda e: e.dma_start(out=out, in_=in_, allow_slow_non_contiguous=True)
        else:
            fn = lambda e: e.dma_start(out=out, in_=in_)
        return self.S.op("sp", fn, reads=reads, writes=writes, dma_sem=sem)

    def mm(self, out, lhsT, rhs, start, stop, reads, writes):
        return self.S.op("pe", lambda e: e.matmul(out, lhsT=lhsT, rhs=rhs, start=start, stop=stop),
                         reads=reads, writes=writes)

    def tr(self, out, in_, reads, writes):
        ident = self.identb
        return self.S.op("pe", lambda e: e.transpose(out=out, in_=in_, identity=ident),
                         reads=list(reads) + ["ident"], writes=writes)

    def act(self, out, in_, func, reads, writes, scale=None, bias=None, accum=None):
        kw = {}
        if scale is not None:
            kw["scale"] = scale
        if bias is not None:
            kw["bias"] = bias
        if accum is not None:
            kw["accum_out"] = accum
        return self.S.op("act", lambda e: e.activation(out=out, in_=in_, func=func, **kw),
                         reads=reads, writes=writes)

    def ts(self, eng, out, in0, s1, s2, op0, op1, reads, writes):
        if op1 is None:
            fn = lambda e: e.tensor_scalar(out=out, in0=in0, scalar1=s1, scalar2=None, op0=op0)
        else:
            fn = lambda e: e.tensor_scalar(out=out, in0=in0, scalar1=s1, scalar2=s2, op0=op0, op1=op1)
        return self.S.op(eng, fn, reads=reads, writes=writes)

    def tt(self, eng, out, in0, in1, op, reads, writes):
        return self.S.op(eng, lambda e: e.tensor_tensor(out=out, in0=in0, in1=in1, op=op),
                         reads=reads, writes=writes)

    def stt(self, out, in0, scalar, in1, op0, op1, reads, writes):
        return self.S.op("dve", lambda e: e.scalar_tensor_tensor(out=out, in0=in0, scalar=scalar, in1=in1,
                                                                  op0=op0, op1=op1),
                         reads=reads, writes=writes)

    def cp(self, eng, out, in_, reads, writes):
        if eng == "act":
            return self.S.op("act", lambda e: e.copy(out=out, in_=in_), reads=reads, writes=writes)
        return self.S.op(eng, lambda e: e.tensor_copy(out=out, in_=in_), reads=reads, writes=writes)

    def memset(self, eng, ap, val, writes):
        return self.S.op(eng, lambda e: e.memset(ap, val), writes=writes)

    def evac_eng(self):
        self.alt ^= 1
        return "act" if self.alt else "dve"

    def bank(self):
        b = self.banks_free[self.bank_rr % len(self.banks_free)]
        self.bank_rr += 1
        return b

    def tbank(self):
        bk = self.bank()
        return self.F[bk][:, 0:256].bitcast(BF16), ("F", bk)

    def rstd(self, ssq, out, n, inv_n, key_in, key_out):
        tmp = self.sttmp[:, 0:n]
        self.ts("dve", tmp, ssq, inv_n, EPS, ALU.mult, ALU.add, reads=[key_in], writes=["sttmp"])
        nh = self.neghalf[:, 0:n]
        self.tt("pool", out, tmp, nh, ALU.pow, reads=["sttmp", "neghalf"], writes=[key_out])

    def w_issue(self, k):
        if k >= len(self.wsched) or k < self.wloaded:
            return
        assert k == self.wloaded
        pid = self.wsched[k]
        nkc, ncol = PIECE_META[pid]
        slot = k % 3
        dst = self.wslot[slot][:, 0:nkc * ncol]
        src = self.wstream[pid, :, 0:nkc * ncol]
        self.dma(dst, src, reads=[("wst", pid)], writes=[("ws", slot)], semkey=("ws", slot))
        self.wloaded = k + 1

    def w_get(self, pid):
        k = self.wpos
        assert self.wsched[k] == pid, (k, self.wsched[k], pid)
        for kk in range(self.wloaded, k + 3):
            self.w_issue(kk)
        self.wpos += 1
        nkc, ncol = PIECE_META[pid]
        slot = k % 3
        ap = self.wslot[slot][:, 0:nkc * ncol].rearrange("p (a b) -> p a b", a=nkc)
        return ap, ("ws", slot)

    def declare(self):
        cfg = self.cfg
        nc = self.nc
        es = self.es
        I = {}
        I["wraw"] = self.din("wraw", [NPW, 128, PW])
        I["wgt"] = self.din("wgt", [128, NPW, 12])
        for n in ("ident", "onesb", "onesk"):
            I[n] = self.din(n, [128, 128], mybir.dt.uint16)
        I["wst"] = self.din("wst", [128, 8, 128])
        I["tri"] = self.din("tri", [128, 8, 128])
        I["wss"] = self.din("wss", [128, 8, 128])
        I["tri2"] = self.din("tri2", [128, 8, 128])
        I["bsr"] = self.din("bsr", [128, 2048])
        I["lng"] = self.din("lng", [128, GW])
        I["lnb"] = self.din("lnb", [128, GW])
        I["gfin"] = self.din("gfin", [128, D])
        I["gkv"] = self.din("gkv", [128, KVL])
        I["convw"] = self.din("convw", [128, NFC * 4])
        I["brow_p"] = self.din("brow_p", [8, 512], mybir.dt.uint16)
        I["brow_s"] = self.din("brow_s", [8, 512], mybir.dt.uint16)
        I["cs_tok_s"] = self.din("cs_tok_s", [cfg.WS, 64])
        I["csq_s"] = self.din("csq_s", [64, 2, cfg.WS])
        I["cs_tok"] = self.din("cs_tok", [cfg.SEQ, 64])
        I["x_seq"] = self.din("x_seq", [cfg.SEQ, D])
        I["x_own"] = self.din("x_own", [cfg.OWN, D])
        I["hscale"] = self.din("hscale", [128, cfg.NJ])
        self.ntiles = len(cfg.tiles())
        I["maskt"] = self.din("maskt", [self.ntiles + 1, 8, cfg.NMM * 128], mybir.dt.uint16)
        I["csq"] = self.din("csq", [64, 2, cfg.OWN])
        I["x_smp"] = self.din("x_smp", [cfg.WS, D])
        I["ckv_cache"] = self.din("ckv_cache", [cfg.NSB * cfg.PAST, KVL])
        I["kr_cache"] = self.din("kr_cache", [cfg.NSB * cfg.PAST, ROPE])
        I["conv_state"] = self.din("conv_state", [128, NFC * cfg.NSB * 2])
        self.I = I
        O = {}
        O["y_own"] = self.dout("y_own", [cfg.NJ * cfg.B, D])
        O["ckv_seq"] = self.dout("ckv_seq", [cfg.SEQ, KVL])
        O["kr_seq"] = self.dout("kr_seq", [cfg.SEQ, ROPE])
        O["conv_last"] = self.dout("conv_last", [2, 2 * DFF])
        O["y_smp"] = self.dout("y_smp", [cfg.WS, D])
        O["ckv_smp"] = self.dout("ckv_smp", [cfg.WS, KVL])
        O["kr_smp"] = self.dout("kr_smp", [cfg.WS, ROPE])
        O["conv_smp"] = self.dout("conv_smp", [cfg.NSB, 2, 2 * DFF])
        O["gv_smp"] = self.dout("gv_smp", [cfg.WS, GW])
        self.O = O
        self.wstream = self.dint("wstream", [NPW, 128, PW], BF16)
        self.kvscr = self.dint("kvscr", [4, cfg.NGRP, 128, 4, KVC], BF16)

        NW = 50 * 1024
        big = es.enter_context(nc.sbuf_tensor("arena", [128, NW], F32))
        A = Arena(big, NW)
        self.A = A
        al = A.alloc
        self.identb = al("", [128, 128], BF16)
        self.onesb = al("", [128, 128], BF16)
        self.onesk = al("", [128, 128], BF16)
        self.ones32 = al("", [128, 128], F32)
        self.wstm = al("", [128, 8, 128], BF16)
        self.wssm = al("", [128, 8, 128], BF16)
        self.bsrb = al("", [128, 2, 1024], BF16)
        self.lng = al("", [128, GW], F32)
        self.lnb = al("", [128, GW], F32)
        self.gfin = al("", [128, D], F32)
        self.gkv = al("", [128, KVL], F32)
        self.convw = al("", [128, NFC, 4], F32)
        self.hscale = al("", [128, cfg.NJ], F32)
        self.neghalf = al("", [128, 8], F32)
        self.sttmp = al("", [128, 8], F32)
        self.stats = al("", [128, 96], F32)
        self.prevsave = al("", [128, NFC, cfg.NSB, 2], F32)
        self.wslot = [al("", [128, PW], BF16) for _ in range(3)]
        self.wgt = al("", [128, NPW, 12], F32)
        A.phase("wprep", "")
        self.wp_in = [al("wprep", [128, PW], F32) for _ in range(2)]
        self.wp_out = [al("wprep", [128, PW], BF16) for _ in range(2)]
        self.wp_f = [al("wprep", [128, 1024], F32) for _ in range(4)]
        A.phase("pre", "")
        G = 4
        self.XP = [al("pre", [128, G, D], F32) for _ in range(2)]
        self.p_xnb = [[al("pre", [128, D], BF16) for _ in range(G)] for _ in range(2)]
        self.p_xnT = al("pre", [128, 8, G * 128], BF16)
        self.p_ckv32 = [al("pre", [128, G, KVL], F32) for _ in range(2)]
        self.p_ckvb = [al("pre", [128, KVL], BF16) for _ in range(G)]
        self.p_krraw = al("pre", [128, G, ROPE], F32)
        self.p_kr32 = [al("pre", [128, G, ROPE], F32) for _ in range(2)]
        self.p_krb = al("pre", [128, G, ROPE], BF16)
        self.p_cs = [al("pre", [128, G, 64], F32) for _ in range(2)]
        self.p_t = [al("pre", [128, G, 32], F32) for _ in range(4)]
        self.p_ckvT = al("pre", [128, 4, G * 128], BF16)
        self.p_stg = [al("pre", [128, G, 4, KVC], BF16) for _ in range(1)]
        self.p_junk = al("pre", [128, D], BF16)
        self.p_st = al("pre", [128, 2, 32], F32)
        A.phase("main", "")
        W = max(cfg.W, cfg.WS)
        self.Wmax = W
        NS = W // 128
        self.X = al("main", [128, NS, D], F32)
        self.bufA = al("main", [128, 8, W], BF16)
        self.uT = al("main", [128, 8, W], BF16)
        self.gT = al("main", [128, 16, W], BF16)
        self.qnT = al("main", [128, 4, W], BF16)
        self.junk = al("main", [128, D], BF16)
        A.phase("front", "main")
        self.xnb = [al("front", [128, D], BF16) for _ in range(2)]
        self.vg = [al("front", [128, GW], F32) for _ in range(2)]
        self.vb = al("front", [128, NS, GW], BF16)
        self.qnb = [al("front", [128, QL], BF16) for _ in range(NS)]
        self.bst = al("front", [128, 2, 6], F32)
        A.phase("attn", "main")
        self.qnopeT = al("attn", [128, 8, W], BF16)
        self.QR = al("attn", [128, 8, W], BF16)
        self.attnT = al("attn", [128, 8, W], BF16)
        self.MK = al("attn", [128, cfg.NMM, 128], BF16)
        self.ropet = [al("attn", [128, W], F32) for _ in range(2)]
        self.csq = al("attn", [128, 2, W], F32)
        self.PT = [al("attn", [128, W], BF16) for _ in range(4)]
        self.NKV = 3
        self.KV = [al("attn", [128, 4, KVC], BF16) for _ in range(self.NKV)]
        self.rec = [al("attn", [128, W], F32) for _ in range(2)]
        self.sacc = [al("attn", [128, W], F32) for _ in range(2)]
        self.tmpm = self.sacc
        A.phase("ffn", "main")
        self.hnb = [al("ffn", [128, D], BF16) for _ in range(2)]
        self.actT = al("ffn", [128, NPAIR, W], BF16)
        self.U = [[al("ffn", [128, W + 2 * cfg.NSB], F32) for _ in range(2)] for _ in range(2)]
        self.cv = [[al("ffn", [128, W], F32) for _ in range(2)] for _ in range(2)]
        self.sg = [al("ffn", [128, W], F32) for _ in range(2)]
        self.Y = [al("ffn", [128, D], F32) for _ in range(2)]
        self.F = [es.enter_context(nc.psum_tensor("F%d" % i, [128, 512], F32)) for i in range(8)]
        self.banks_free = list(range(8))
        self.sems = {e: es.enter_context(nc.semaphore("s_" + e)) for e in ENGS}

    def setup(self):
        I = self.I
        self.S.ph = "setup"
        bf = lambda ap: ap.bitcast(BF16)
        self.dma(self.identb, bf(I["ident"]), [], ["ident"], "c0")
        self.dma(self.onesb, bf(I["onesb"]), [], ["onesb"], "c1")
        self.dma(self.onesk, bf(I["onesk"]), [], ["onesk"], "c2")
        self.dma(self.lng, I["lng"], [], ["lng"], "c3")
        self.dma(self.lnb, I["lnb"], [], ["lnb"], "c4")
        self.dma(self.gfin, I["gfin"], [], ["gfin"], "c5")
        self.dma(self.gkv, I["gkv"], [], ["gkv"], "c6")
        self.dma(self.convw.rearrange("p a b -> p (a b)"), I["convw"], [], ["convw"], "c7")
        self.dma(self.hscale, I["hscale"], [], ["hscale"], "c8")
        self.dma(self.wgt.rearrange("p a b -> p (a b)"), I["wgt"].rearrange("p a b -> p (a b)"), [], ["wgt"], "c9")
        self.memset("pool", self.neghalf, -0.5, ["neghalf"])
        self.memset("pool", self.ones32, 1.0, ["ones32"])
        self.memset("pool", self.prevsave.rearrange("p a b c -> p (a b c)"), 0.0, ["prevsave"])
        self.memset("pool", self.stats, 1.0, ["onescol"])
        f = self.wp_f
        flat = lambda ap: ap.rearrange("p a b -> p (a b)")
        self.dma(f[0], flat(I["wst"]), [], ["f0"], "f0")
        self.dma(f[1], flat(I["tri"]), [], ["f1"], "f1")
        self.dma(f[2], flat(I["wss"]), [], ["f2"], "f2")
        self.dma(f[3], flat(I["tri2"]), [], ["f3"], "f3")
        self.tt("dve", flat(self.wstm), f[0], f[1], ALU.mult, ["f0", "f1"], ["wstm"])
        self.tt("dve", flat(self.wssm), f[2], f[3], ALU.mult, ["f2", "f3"], ["wssm"])
        src = self.wp_in[0][:, 0:2048]
        tmpb = self.wp_out[0][:, 0:2048]
        bs = self.bsrb.rearrange("p a b -> p (a b)")
        self.dma(src, I["bsr"], [], ["bsrc"], "bsrc")
        self.memset("pool", bs, 0.0, ["bsrb"])
        self.cp("dve", bs[0:1, :], src[0:1, :], ["bsrc"], ["bsrb"])
        self.cp("dve", tmpb[32:33, :], src[32:33, :], ["bsrc"], ["btmp"])
        self.tt("dve", src[32:33, :], src[32:33, :], tmpb[32:33, :], ALU.subtract, ["bsrc", "btmp"], ["bsrc2"])
        self.cp("dve", bs[32:33, :], src[32:33, :], ["bsrc2"], ["bsrb"])
        self.S.barrier()
        self.S.ph = "wprep"
        engs = ["dve", "act"]
        k = 0
        for pi in range(NPW):
            nkc, ncol = PIECE_META[pi]
            b = pi % 2
            n = nkc * ncol
            self.dma(self.wp_in[b][:, 0:n], I["wraw"][pi, :, 0:n], [], [("wpi", b)], ("wpi", b))
            for kc in range(nkc):
                eng = engs[k % 2]
                k += 1
                o = self.wp_out[b][:, kc * ncol:(kc + 1) * ncol]
                i_ = self.wp_in[b][:, kc * ncol:(kc + 1) * ncol]
                sc = self.wgt[:, pi, kc:kc + 1]
                if eng == "act":
                    self.act(o, i_, AF.Copy, [("wpi", b), "wgt"], [("wpo", b)], scale=sc)
                else:
                    self.ts(eng, o, i_, sc, None, ALU.mult, None, [("wpi", b), "wgt"], [("wpo", b)])
            self.dma(self.wstream[pi, :, 0:n], self.wp_out[b][:, 0:n], [("wpo", b)], [("wst", pi)], ("wpo", b))
        self.S.barrier()

    def pre_setup(self):
        for i, pid in enumerate((P_KV, P_UK, P_UV)):
            nkc, ncol = PIECE_META[pid]
            self.dma(self.wslot[i][:, 0:nkc * ncol], self.wstream[pid, :, 0:nkc * ncol], [], [("pw", i)], ("pw", i))
        stg = self.p_stg[0]
        self.memset("pool", stg.rearrange("p a b c -> p (a b c)"), 0.0, ["stg_all"])
        for gi in range(4):
            self.memset("pool", stg[64:65, gi, :, 256:384], 1.0, ["stg_all"])
        self.Wkv = self.wslot[0][:, 0:8 * 576].rearrange("p (a b) -> p a b", a=8)
        self.Wuk = self.wslot[1][:, 0:4 * 1024].rearrange("p (a b) -> p a b", a=4)
        self.Wuv = self.wslot[2][:, 0:4 * 1024].rearrange("p (a b) -> p a b", a=4)
        self.pre_first = True

    def kv_stage_a(self, kind, G, par, x_src=None, cs_src=None, ckv_src=None, kr_src=None):
        st = self.p_st
        if kind == "x":
            self.dma(self.p_cs[par][:, 0:G, :], cs_src.rearrange("(g p) c -> p g c", p=128), [], [("pcs", par)], ("pcs", par))
            for gi in range(G):
                self.act(self.p_junk, self.XP[par][:, gi, :], AF.Square, [("XP", par)], ["pjunk", ("pssq", par)],
                         accum=st[:, par, gi:gi + 1])
            self.ts("dve", st[:, par, 4:4 + G], st[:, par, 0:G], 1.0 / D, EPS, ALU.mult, ALU.add,
                    [("pssq", par)], [("pms", par)])
            self.tt("pool", st[:, par, 8:8 + G], st[:, par, 4:4 + G], self.neghalf[:, 0:G], ALU.pow,
                    [("pms", par), "neghalf"], [("prs", par)])
            for gi in range(G):
                self.ts("dve", self.p_xnb[par][gi], self.XP[par][:, gi, :], st[:, par, 8 + gi:9 + gi], None, ALU.mult, None,
                        [("XP", par), ("prs", par)], [("pxnb", par, gi)])
        else:
            self.dma(self.p_ckv32[par][:, 0:G, :], ckv_src.rearrange("(g p) c -> p g c", p=128), [], [("pckv32", par)], ("pckv32", par))
            self.dma(self.p_kr32[par][:, 0:G, :], kr_src.rearrange("(g p) c -> p g c", p=128), [], [("pkr32", par)], ("pkr32", par))

    def kv_a_load(self, kind, G, par, x_src=None, cs_src=None, ckv_src=None, kr_src=None):
        if kind == "x":
            self.dma(self.XP[par][:, 0:G, :], x_src.rearrange("(g p) d -> p g d", p=128), [], [("XP", par)], ("XP", par))

    def kv_b(self, kind, G, par, blks, ckv_out=None, kr_out=None):
        if kind == "x":
            for gi in range(G):
                rows = slice(gi * 128, (gi + 1) * 128)
                for hf in range(2):
                    Tv, tk = self.tbank()
                    for j in range(4):
                        kc = 4 * hf + j
                        self.tr(Tv[:, j * 128:(j + 1) * 128],
                                self.p_xnb[par][gi][:, kc * 128:(kc + 1) * 128], [("pxnb", par, gi)], [tk])
                    self.cp(self.evac_eng(), self.p_xnT[:, 4 * hf:4 * hf + 4, rows],
                            Tv.rearrange("p (a b) -> p a b", a=4), [tk], [("pxnT", gi, hf)])

    def kv_c(self, kind, G, par, blks, ckv_out=None, kr_out=None):
        F = self.F
        st = self.p_st
        kr32 = self.p_kr32[par]
        if kind == "x":
            banks = []
            for gi in range(G):
                rows = slice(gi * 128, (gi + 1) * 128)
                ba = self.bank()
                for kc in range(8):
                    self.mm(F[ba][:, 0:512], self.p_xnT[:, kc, rows], self.Wkv[:, kc, 0:512], kc == 0, kc == 7,
                            [("pxnT", gi, 0), ("pxnT", gi, 1), ("pw", 0)], [("F", ba)])
                banks.append(ba)
            bb = self.bank()
            for gi in range(G):
                rows = slice(gi * 128, (gi + 1) * 128)
                for kc in range(8):
                    self.mm(F[bb][:, gi * 64:(gi + 1) * 64], self.p_xnT[:, kc, rows], self.Wkv[:, kc, 512:576],
                            kc == 0, kc == 7, [("pxnT", gi, 0), ("pxnT", gi, 1), ("pw", 0)], [("F", bb)])
            for gi in range(G):
                self.act(self.p_junk[:, 0:512], F[banks[gi]][:, 0:512], AF.Square, [("F", banks[gi])],
                         ["pjunk", ("pssc", par)], accum=st[:, par, 12 + gi:13 + gi])
            self.cp("act", self.p_krraw[:, 0:G, :], F[bb][:, 0:G * 64].rearrange("p (a b) -> p a b", a=G),
                    [("F", bb)], ["pkraw"])
            self.ts("dve", st[:, par, 16:16 + G], st[:, par, 12:12 + G], 1.0 / KVL, EPS, ALU.mult, ALU.add,
                    [("pssc", par)], [("pmc", par)])
            self.tt("pool", st[:, par, 20:20 + G], st[:, par, 16:16 + G], self.neghalf[:, 0:G], ALU.pow,
                    [("pmc", par), "neghalf"], [("prc", par)])
            for gi in range(G):
                self.stt(self.p_ckv32[par][:, gi, :], F[banks[gi]][:, 0:512], st[:, par, 20 + gi:21 + gi], self.gkv,
                         ALU.mult, ALU.mult, [("F", banks[gi]), ("prc", par), "gkv"], [("pckv32", par)])
            self.dma(ckv_out.rearrange("(g p) c -> p g c", p=128), self.p_ckv32[par][:, 0:G, :], [("pckv32", par)], [],
                     ("pckv32", par))
            raw, cs, t = self.p_krraw, self.p_cs[par], self.p_t
            rk = ["pkraw", ("pcs", par)]
            g_ = slice(0, G)
            self.tt("dve", t[0][:, g_, :], raw[:, g_, 0:32], cs[:, g_, 0:32], ALU.mult, rk, ["pt0"])
            self.tt("dve", t[1][:, g_, :], raw[:, g_, 32:64], cs[:, g_, 32:64], ALU.mult, rk, ["pt1"])
            self.tt("dve", t[2][:, g_, :], raw[:, g_, 0:32], cs[:, g_, 32:64], ALU.mult, rk, ["pt2"])
            self.tt("dve", t[3][:, g_, :], raw[:, g_, 32:64], cs[:, g_, 0:32], ALU.mult, rk, ["pt3"])
            self.tt("dve", kr32[:, g_, 0:32], t[0][:, g_, :], t[1][:, g_, :], ALU.subtract, ["pt0", "pt1"], [("pkr32", par)])
            self.tt("dve", kr32[:, g_, 32:64], t[2][:, g_, :], t[3][:, g_, :], ALU.add, ["pt2", "pt3"], [("pkr32", par)])
            self.dma(kr_out.rearrange("(g p) c -> p g c", p=128), kr32[:, 0:G, :], [("pkr32", par)], [], ("pkr32", par))

    def kv_d(self, kind, G, par, blks, ckv_out=None, kr_out=None):
        F = self.F
        stg = self.p_stg[0]
        sdep = ["stg_all"]
        kr32 = self.p_kr32[par]
        for gi in range(G):
            self.cp("act", self.p_ckvb[gi], self.p_ckv32[par][:, gi, :], [("pckv32", par)], [("pckvb", gi)])
        self.cp("dve", self.p_krb[:, 0:G, :], kr32[:, 0:G, :], [("pkr32", par)], ["pkrb"])
        for gi in range(G):
            rows = slice(gi * 128, (gi + 1) * 128)
            Tv, tk = self.tbank()
            for kc in range(4):
                self.tr(Tv[:, kc * 128:(kc + 1) * 128], self.p_ckvb[gi][:, kc * 128:(kc + 1) * 128],
                        [("pckvb", gi)], [tk])
            self.cp(self.evac_eng(), self.p_ckvT[:, 0:4, rows],
                    Tv.rearrange("p (a b) -> p a b", a=4), [tk], [("pckvT", gi)])
        Tv2, tk2 = self.tbank()
        for gi in range(G):
            self.tr(Tv2[0:64, gi * 128:(gi + 1) * 128], self.p_krb[:, gi, :], ["pkrb"], [tk2])
        for hp in range(4):
            self.cp("dve" if hp % 2 else "act", stg[0:64, 0:G, hp, 256:384],
                    Tv2[0:64, 0:G * 128].rearrange("p (a b) -> p a b", a=G),
                    [tk2] + sdep, [("stg", gi) for gi in range(G)])

    def kv_e1(self, kind, G, par, blks, ckv_out=None, kr_out=None):
        F = self.F
        stg = self.p_stg[0]
        sdep = ["stg_all"]
        GW_ = G * 128
        for h in range(NH):
            bk = self.bank()
            for kc in range(4):
                self.mm(F[bk][:, 0:GW_], self.Wuk[:, kc, h * 128:(h + 1) * 128], self.p_ckvT[:, kc, 0:GW_],
                        kc == 0, kc == 3, [("pckvT", gi) for gi in range(G)] + [("pw", 1)], [("F", bk)])
            self.cp("dve", stg[:, 0:G, h // 2, (h % 2) * 128:(h % 2 + 1) * 128],
                    F[bk][:, 0:GW_].rearrange("p (a b) -> p a b", a=G), [("F", bk)] + sdep,
                    [("stg", gi) for gi in range(G)])

    def kv_e2(self, kind, G, par, blks, ckv_out=None, kr_out=None):
        F = self.F
        stg = self.p_stg[0]
        sdep = ["stg_all"]
        for gi in range(G):
            rows = slice(gi * 128, (gi + 1) * 128)
            for half in range(2):
                bk = self.bank()
                for kc in range(4):
                    self.mm(F[bk][:, 0:512], self.p_ckvT[:, kc, rows], self.Wuv[:, kc, half * 512:(half + 1) * 512],
                            kc == 0, kc == 3, [("pckvT", gi), ("pw", 2)], [("F", bk)])
                self.cp("dve", stg[:, gi, 2 * half:2 * half + 2, 384:640],
                        F[bk][:, 0:512].rearrange("p (a b) -> p a b", a=2), [("F", bk)] + sdep, [("stg", gi)])
        for hp in range(4):
            self.dma(self.kvscr[hp, blks], stg[:, :, hp, :], [("stg", gi) for gi in range(4)], [("kvs", blks, hp)],
                     ("stgd", hp))

    def prepass(self):
        cfg = self.cfg
        self.S.ph = "prepass"
        I, O = self.I, self.O
        self.pre_setup()
        G = 4
        jobs = []
        for g0 in range(0, cfg.NBP, G):
            r = slice(g0 * 128, (g0 + G) * 128)
            jobs.append(dict(kind="x", G=G, blks=g0 // 4,
                             a=dict(x_src=I["x_seq"][r, :], cs_src=I["cs_tok"][r, :]),
                             r=dict(ckv_out=O["ckv_seq"][r, :], kr_out=O["kr_seq"][r, :])))
        Gs = cfg.NBS_NEW
        jobs.append(dict(kind="x", G=Gs, blks=cfg.GRP_NEW,
                         a=dict(x_src=I["x_smp"], cs_src=I["cs_tok_s"]),
                         r=dict(ckv_out=O["ckv_smp"], kr_out=O["kr_smp"])))
        for b in range(cfg.NSB):
            for g0 in range(0, cfg.NBC, 4):
                r = slice(b * cfg.PAST + g0 * 128, b * cfg.PAST + (g0 + 4) * 128)
                jobs.append(dict(kind="cache", G=4, blks=cfg.NGP + b * cfg.NGC + g0 // 4,
                                 a=dict(ckv_src=I["ckv_cache"][r, :], kr_src=I["kr_cache"][r, :]), r={}))
        n = len(jobs)

        def call(fn, i, stage_a=0):
            if i >= n:
                return
            jb = jobs[i]
            if stage_a == 1:
                self.kv_stage_a(jb["kind"], jb["G"], i % 2, **jb["a"])
            elif stage_a == 2:
                self.kv_a_load(jb["kind"], jb["G"], i % 2, **jb["a"])
            else:
                fn(jb["kind"], jb["G"], i % 2, jb["blks"], **jb["r"])

        call(None, 0, 2)
        call(None, 1, 2)
        call(None, 0, 1)
        call(None, 1, 1)
        call(None, 2, 2)
        call(self.kv_b, 0)
        call(self.kv_c, 0)
        call(self.kv_b, 1)
        for i in range(n):
            call(self.kv_d, i)
            call(None, i + 2, 1)
            call(None, i + 3, 2)
            call(self.kv_e1, i)
            call(self.kv_c, i + 1)
            call(self.kv_e2, i)
            call(self.kv_b, i + 2)
        self.S.barrier()

    def tile(self, W, x_src, csq_src, brow_src, mask_idx, groups, nb, hs_col, is_halo, sample,
             y_out=None, gv_out=None, conv_out=None):
        cfg = self.cfg
        S = self.S
        F = self.F
        I = self.I
        NS = W // 128
        wb = W // nb
        bufA, uT, gT, qnT, X = self.bufA, self.uT, self.gT, self.qnT, self.X
        st = self.stats
        S.soft_barrier()
        kind_ = ("smp" if sample else ("halo" if is_halo else "tile")) + str(mask_idx)
        S.ph = kind_ + ".front"
        for s in range(NS):
            self.dma(X[:, s, :], x_src[s * 128:(s + 1) * 128, :], [], [("X", s)], ("X", s))
        for s in range(NS):
            self.act(self.junk, X[:, s, :], AF.Square, [("X", s)], ["junk", ("ssq", s)], accum=st[:, s:s + 1])
            self.rstd(st[:, s:s + 1], st[:, 8 + s:9 + s], 1, 1.0 / D, ("ssq", s), ("rs", s))
            if s % 2 == 0:
                self.act(self.xnb[s % 2], X[:, s, :], AF.Copy, [("X", s), ("rs", s)], [("xnb", s % 2)], scale=st[:, 8 + s:9 + s])
            else:
                self.ts("dve", self.xnb[s % 2], X[:, s, :], st[:, 8 + s:9 + s], None, ALU.mult, None,
                        [("X", s), ("rs", s)], [("xnb", s % 2)])
            for hf in range(2):
                Tv, tk = self.tbank()
                for j in range(4):
                    kc = 4 * hf + j
                    self.tr(Tv[:, j * 128:(j + 1) * 128],
                            self.xnb[s % 2][:, kc * 128:(kc + 1) * 128], [("xnb", s % 2)], [tk])
                self.cp(self.evac_eng(), bufA[:, 4 * hf:4 * hf + 4, s * 128:(s + 1) * 128],
                        Tv.rearrange("p (a b) -> p a b", a=4), [tk], [("bA", s)])
        bA_all = [("bA", s) for s in range(NS)]
        Wu, wk = self.w_get(P_U)
        for j in range(8):
            bk = self.bank()
            for kc in range(8):
                self.mm(F[bk][:, 0:W], Wu[:, kc, j * 128:(j + 1) * 128], bufA[:, kc, 0:W], kc == 0, kc == 7,
                        bA_all + [wk], [("F", bk)])
            self.act(uT[:, j, 0:W], F[bk][:, 0:W], AF.Gelu_apprx_tanh, [("F", bk)], [("uT", j)])
        Wv, wk = self.w_get(P_V)
        for s in range(NS):
            vg = self.vg[s % 2]
            vk = ("vg", s % 2)
            for half in range(2):
                bk = self.bank()
                for kc in range(8):
                    self.mm(F[bk][:, 0:512], bufA[:, kc, s * 128:(s + 1) * 128], Wv[:, kc, half * 512:(half + 1) * 512],
                            kc == 0, kc == 7, [("bA", s), wk], [("F", bk)])
                self.act(vg[:, half * 512:(half + 1) * 512], F[bk][:, 0:512], AF.Gelu_apprx_tanh, [("F", bk)], [vk])
            for half in range(2):
                S.op("dve", lambda e, half=half, vg=vg: e.bn_stats(out=self.bst[:, half, :], in_=vg[:, half * 512:(half + 1) * 512]),
                     reads=[vk], writes=[("bst", half)])
            S.op("dve", lambda e: e.bn_aggr(out=st[:, 32:34], in_=self.bst.rearrange("p a b -> p (a b)")),
                 reads=[("bst", 0), ("bst", 1)], writes=["lnmv"])
            self.rstd(st[:, 33:34], st[:, 34:35], 1, 1.0, "lnmv", "lnrs")
            self.ts("dve", vg, vg, st[:, 32:33], st[:, 34:35], ALU.subtract, ALU.mult, [vk, "lnmv", "lnrs"], [vk])
            self.tt("dve", vg, vg, self.lng, ALU.mult, [vk, "lng"], [vk])
            self.tt("dve", vg, vg, self.lnb, ALU.add, [vk, "lnb"], [vk])
            if gv_out is not None:
                self.dma(gv_out[s * 128:(s + 1) * 128, :], vg, [vk], [], vk)
            self.cp("act", self.vb[:, s, :], vg, [vk], [("vb", s)])
        Wq, wk = self.w_get(P_Q)
        for s in range(NS):
            bk = self.bank()
            for kc in range(8):
                self.mm(F[bk][:, 0:512], bufA[:, kc, s * 128:(s + 1) * 128], Wq[:, kc, 0:512], kc == 0, kc == 7,
                        [("bA", s), wk], [("F", bk)])
            self.act(self.junk[:, 0:512], F[bk][:, 0:512], AF.Square, [("F", bk)], ["junk", ("qss", s)],
                     accum=st[:, 40 + s:41 + s])
            self.rstd(st[:, 40 + s:41 + s], st[:, 44 + s:45 + s], 1, 1.0 / QL, ("qss", s), ("qrs", s))
            self.act(self.qnb[s], F[bk][:, 0:512], AF.Copy, [("F", bk), ("qrs", s)], [("qnb", s)],
                     scale=st[:, 44 + s:45 + s])
        for gi_, pid in enumerate((P_G0, P_G1)):
            Wg, wk = self.w_get(pid)
            for jj in range(8):
                j = 8 * gi_ + jj
                bk = self.bank()
                for kc in range(8):
                    self.mm(F[bk][:, 0:W], Wg[:, kc, jj * 128:(jj + 1) * 128], bufA[:, kc, 0:W], kc == 0, kc == 7,
                            bA_all + [wk], [("F", bk)])
                self.act(gT[:, j, 0:W], F[bk][:, 0:W], AF.Sigmoid, [("F", bk)], [("gT", j)])
        wsm = self.wssm if sample else self.wstm
        bsel = 1 if sample else 0
        for g in range(8):
            bk = self.bank()
            for s in range(NS):
                self.mm(F[bk][:, s * 128:(s + 1) * 128], self.vb[:, s, g * 128:(g + 1) * 128], wsm[:, g, :], True, False,
                        [("vb", s), "wstm", "wssm"], [("F", bk)])
                self.mm(F[bk][:, s * 128:(s + 1) * 128], self.onesk, self.bsrb[:, bsel, g * 128:(g + 1) * 128], False, True,
                        ["onesk", "bsrb"], [("F", bk)])
            self.tt("dve", uT[:, g, 0:W], F[bk][:, 0:W], uT[:, g, 0:W], ALU.mult, [("F", bk), ("uT", g)], [("uT", g)])
        for s in range(NS):
            Tv, tk = self.tbank()
            for kc in range(4):
                self.tr(Tv[:, kc * 128:(kc + 1) * 128], self.qnb[s][:, kc * 128:(kc + 1) * 128],
                        [("qnb", s)], [tk])
            self.cp("dve", qnT[:, 0:4, s * 128:(s + 1) * 128], Tv.rearrange("p (a b) -> p a b", a=4),
                    [tk], [("qnT", s)])
        S.soft_barrier()
        S.ph = kind_ + ".qhead"
        QR, qnopeT, attnT, MK = self.QR, self.qnopeT, self.attnT, self.MK
        self.memset("pool", QR[64:128, :, :].rearrange("p a b -> p (a b)"), 0.0, ["QRhi"])
        for h in range(NH):
            self.dma(QR[96:104, h, 0:W], brow_src[:, 0:W].bitcast(BF16), ["QRhi"], [("QRb", h)], ("QRb", h))
        self.memset("pool", MK.rearrange("p a b -> p (a b)"), 0.0, ["MK0"])
        self.dma(MK[96:104, :, :].rearrange("p a b -> p (a b)"), I["maskt"][mask_idx].bitcast(BF16), ["MK0"], ["MK"], "MK")
        self.dma(self.csq[0:64, :, 0:W], csq_src, [], ["csq"], "csq")
        qnT_all = [("qnT", s) for s in range(NS)]
        Wn, wk = self.w_get(P_UQN)
        for h in range(NH):
            bk = self.bank()
            for kc in range(4):
                self.mm(F[bk][:, 0:W], Wn[:, kc, h * 128:(h + 1) * 128], qnT[:, kc, 0:W], kc == 0, kc == 3,
                        qnT_all + [wk], [("F", bk)])
            self.cp(self.evac_eng(), qnopeT[:, h, 0:W], F[bk][:, 0:W], [("F", bk)], [("qno", h)])
        Wr, wk = self.w_get(P_UQR)
        for h in range(NH):
            ba, bb = self.bank(), self.bank()
            for kc in range(4):
                self.mm(F[ba][:, 0:W], Wr[:, kc, h * 256:h * 256 + 128], qnT[:, kc, 0:W], kc == 0, kc == 3,
                        qnT_all + [wk], [("F", ba)])
            for kc in range(4):
                self.mm(F[bb][:, 0:W], Wr[:, kc, h * 256 + 128:h * 256 + 256], qnT[:, kc, 0:W], kc == 0, kc == 3,
                        qnT_all + [wk], [("F", bb)])
            r0, r1 = self.ropet[0], self.ropet[1]
            self.tt("dve", r0[0:64, 0:W], F[ba][0:64, 0:W], self.csq[0:64, 0, 0:W], ALU.mult, [("F", ba), "csq"], ["r0"])
            self.tt("dve", r1[0:64, 0:W], F[bb][0:64, 0:W], self.csq[0:64, 1, 0:W], ALU.mult, [("F", bb), "csq"], ["r1"])
            self.tt("pool", QR[0:64, h, 0:W], r0[0:64, 0:W], r1[0:64, 0:W], ALU.add, ["r0", "r1"], [("QRr", h)])
        if is_halo:
            self.memset("pool", attnT.rearrange("p a b -> p (a b)"), 0.0, [("at", h_) for h_ in range(NH)])
        S.ph = kind_ + ".attn"
        self.attention(groups)
        S.ph = kind_ + ".proj"
        for pi_, pid in enumerate((P_AB0, P_AB1)):
            Wab, wk = self.w_get(pid)
            for jj in range(4):
                j = 4 * pi_ + jj
                ba, bb = self.bank(), self.bank()
                for kc in range(8):
                    self.mm(F[ba][:, 0:W], Wab[:, kc, jj * 128:(jj + 1) * 128], uT[:, kc, 0:W], kc == 0, kc == 7,
                            [("uT", k_) for k_ in range(8)] + [wk], [("F", ba)])
                for kc in range(8):
                    self.mm(F[bb][:, 0:W], Wab[:, kc, 512 + jj * 128:512 + (jj + 1) * 128], attnT[:, kc, 0:W], kc == 0, kc == 7,
                            [("at", k_) for k_ in range(8)] + [wk], [("F", bb)])
                t0, t1 = self.tmpm[j % 2], self.rec[j % 2]
                self.tt("dve", t0[:, 0:W], F[ba][:, 0:W], gT[:, j, 0:W], ALU.mult, [("F", ba), ("gT", j)], [("sacc", j % 2)])
                self.tt("dve", t1[:, 0:W], F[bb][:, 0:W], gT[:, 8 + j, 0:W], ALU.mult, [("F", bb), ("gT", 8 + j)], [("rec", j % 2)])
                self.tt("pool", bufA[:, j, 0:W], t0[:, 0:W], t1[:, 0:W], ALU.add, [("sacc", j % 2), ("rec", j % 2)],
                        [("bA", s) for s in range(NS)])
        S.soft_barrier()
        S.ph = kind_ + ".ffn"
        Wo, wk = self.w_get(P_O)
        for s in range(NS):
            for half in range(2):
                bk = self.bank()
                for kc in range(8):
                    self.mm(F[bk][:, 0:512], bufA[:, kc, s * 128:(s + 1) * 128], Wo[:, kc, half * 512:(half + 1) * 512],
                            kc == 0, kc == 7, [("bA", s), wk], [("F", bk)])
                self.tt("dve", X[:, s, half * 512:(half + 1) * 512], F[bk][:, 0:512], X[:, s, half * 512:(half + 1) * 512],
                        ALU.add, [("F", bk), ("X", s)], [("X", s)])
        for s in range(NS):
            self.act(self.junk, X[:, s, :], AF.Square, [("X", s)], ["junk", ("hss", s)], accum=st[:, 48 + s:49 + s])
            self.rstd(st[:, 48 + s:49 + s], st[:, 52 + s:53 + s], 1, 1.0 / D, ("hss", s), ("hrs", s))
            if s % 2 == 0:
                self.act(self.hnb[s % 2], X[:, s, :], AF.Copy, [("X", s), ("hrs", s)], [("hnb", s % 2)], scale=st[:, 52 + s:53 + s])
            else:
                self.ts("dve", self.hnb[s % 2], X[:, s, :], st[:, 52 + s:53 + s], None, ALU.mult, None,
                        [("X", s), ("hrs", s)], [("hnb", s % 2)])
            for hf in range(2):
                Tv, tk = self.tbank()
                for j in range(4):
                    kc = 4 * hf + j
                    self.tr(Tv[:, j * 128:(j + 1) * 128],
                            self.hnb[s % 2][:, kc * 128:(kc + 1) * 128], [("hnb", s % 2)], [tk])
                self.cp(self.evac_eng(), bufA[:, 4 * hf:4 * hf + 4, s * 128:(s + 1) * 128],
                        Tv.rearrange("p (a b) -> p a b", a=4), [tk], [("bA", s)])
        def v3(ap, lo, n):
            return ap[:, 0:nb * (wb + 2)].rearrange("p (a b) -> p a b", a=nb)[:, :, lo:lo + n]
        pairs = [(t, q) for t in range(6) for q in range(4 if t < 5 else 2)]
        wst_ = {"t": -1, "W": None, "k": None}
        cw = self.convw

        def stage_x(idx):
            t, q = pairs[idx]
            p = 4 * t + q
            sl = p % 2
            if t != wst_["t"]:
                wst_["W"], wst_["k"] = self.w_get(P_UP0 + t)
                wst_["t"] = t
            Wup, wk = wst_["W"], wst_["k"]
            bv, bg = self.bank(), self.bank()
            for kc in range(8):
                self.mm(F[bv][:, 0:W], Wup[:, kc, q * 256:q * 256 + 128], bufA[:, kc, 0:W], kc == 0, kc == 7,
                        bA_all + [wk], [("F", bv)])
            for kc in range(8):
                self.mm(F[bg][:, 0:W], Wup[:, kc, q * 256 + 128:q * 256 + 256], bufA[:, kc, 0:W], kc == 0, kc == 7,
                        bA_all + [wk], [("F", bg)])
            for k_, (bk, chunk) in enumerate(((bv, p), (bg, NPAIR + p))):
                U = self.U[sl][k_]
                uk = ("U", sl, k_)
                self.cp("act", v3(U, 2, wb), F[bk][:, 0:W].rearrange("p (a b) -> p a b", a=nb), [("F", bk)], [uk])
                self.ts("pool", v3(U, 0, 2), self.prevsave[:, chunk, 0:nb, :], hs_col, None, ALU.mult, None,
                        [("prev", chunk), "hscale", "onescol"], [uk])
                self.cp("pool", self.prevsave[:, chunk, 0:nb, :], v3(U, wb, 2), [uk], [("prev", chunk)])

        def stage_y1(idx):
            t, q = pairs[idx]
            p = 4 * t + q
            sl = p % 2
            for k_, chunk in enumerate((p, NPAIR + p)):
                U = self.U[sl][k_]
                uk = ("U", sl, k_)
                c = self.cv[sl][k_][:, 0:W].rearrange("p (a b) -> p a b", a=nb)
                ck = ("cv", sl, k_)
                self.act(c, v3(U, 0, wb), AF.Identity, [uk, "convw"], [ck], scale=cw[:, chunk, 0:1],
                         bias=cw[:, chunk, 3:4])
                self.stt(c, v3(U, 1, wb), cw[:, chunk, 1:2], c, ALU.mult, ALU.add, [uk, ck, "convw"], [ck])
                self.stt(c, v3(U, 2, wb), cw[:, chunk, 2:3], c, ALU.mult, ALU.add, [uk, ck, "convw"], [ck])

        def stage_y2(idx):
            t, q = pairs[idx]
            p = 4 * t + q
            sl = p % 2
            self.act(self.sg[sl][:, 0:W], self.cv[sl][1][:, 0:W], AF.Silu, [("cv", sl, 1)], [("sg", sl)])
            self.tt("dve", self.actT[:, p, 0:W], self.sg[sl][:, 0:W], self.cv[sl][0][:, 0:W], ALU.mult,
                    [("sg", sl), ("cv", sl, 0)], [("aT", p)])

        npr = len(pairs)
        stage_x(0)
        for i in range(npr):
            if i + 1 < npr:
                stage_x(i + 1)
            if not is_halo:
                stage_y1(i)
                if i >= 1:
                    stage_y2(i - 1)
        if not is_halo:
            stage_y2(npr - 1)
        if is_halo:
            return
        aT_all = [("aT", p) for p in range(NPAIR)]
        for half in range(2):
            for part in range(2):
                Wd, wk = self.w_get(P_D0 + 2 * half + part)
                for s in range(NS):
                    for kc in range(11):
                        self.mm(F[s][:, 0:512], self.actT[:, part * 11 + kc, s * 128:(s + 1) * 128], Wd[:, kc, 0:512],
                                part == 0 and kc == 0, part == 1 and kc == 10, aT_all + [wk], [("F", s)])
            for s in range(NS):
                self.tt("dve", X[:, s, half * 512:(half + 1) * 512], F[s][:, 0:512], X[:, s, half * 512:(half + 1) * 512],
                        ALU.add, [("F", s), ("X", s)], [("X", s)])
        for s in range(NS):
            self.act(self.junk, X[:, s, :], AF.Square, [("X", s)], ["junk", ("yss", s)], accum=st[:, 56 + s:57 + s])
            self.rstd(st[:, 56 + s:57 + s], st[:, 60 + s:61 + s], 1, 1.0 / D, ("yss", s), ("yrs", s))
            self.stt(self.Y[s % 2], X[:, s, :], st[:, 60 + s:61 + s], self.gfin, ALU.mult, ALU.mult,
                     [("X", s), ("yrs", s), "gfin"], [("Y", s % 2)])
            self.dma(y_out[s * 128:(s + 1) * 128, :], self.Y[s % 2], [("Y", s % 2)], [], ("Y", s % 2))
        if conv_out is not None:
            for seg in range(nb):
                for t_ in range(2):
                    self.dma(conv_out[seg][t_].rearrange("(c p) -> p c", p=128), self.prevsave[:, :, seg, t_],
                             [("prev", c_) for c_ in range(NFC)], [], ("convo", seg, t_), slow=True)

    def attention(self, groups):
        F = self.F
        QR, qnopeT, attnT, MK = self.QR, self.qnopeT, self.attnT, self.MK
        kvctr = 0
        uctr = 0
        for (c0, ncol, kvgroups) in groups:
            cols = slice(c0, c0 + ncol)
            nblk = sum(len(js) for _, js in kvgroups)
            for hp in range(4):
                pend = []

                def flush(pend_item):
                    (slot, j, hh, bi, pt) = pend_item
                    kv = self.KV[slot][:, j, :]
                    self.mm(F[hh][:, 0:ncol], kv[:, 384 + hh * 128:384 + (hh + 1) * 128], self.PT[pt][:, 0:ncol],
                            bi == 0, bi == nblk - 1, [("kv", slot), ("PT", pt)], [("F", hh)])
                    sa = self.sacc[hh]
                    if bi == 0:
                        self.cp("dve", sa[:, 0:ncol], self.PT[pt][:, 0:ncol], [("PT", pt)], [("sacc", hh)])
                    else:
                        self.tt("dve", sa[:, 0:ncol], sa[:, 0:ncol], self.PT[pt][:, 0:ncol], ALU.add,
                                [("PT", pt), ("sacc", hh)], [("sacc", hh)])

                bi = -1
                for (grp, js) in kvgroups:
                    slot = kvctr % self.NKV
                    kvctr += 1
                    self.dma(self.KV[slot], self.kvscr[hp, grp], [("kvs", grp, hp)], [("kv", slot)], ("kv", slot))
                    for (j, mu) in js:
                        bi += 1
                        for hh in range(2):
                            h = 2 * hp + hh
                            sb_ = 4 + (uctr % 4)
                            pt = uctr % 4
                            uctr += 1
                            kv = self.KV[slot][:, j, :]
                            masked = mu is not None
                            self.mm(F[sb_][:, 0:ncol], kv[:, hh * 128:(hh + 1) * 128], qnopeT[:, h, cols], True, False,
                                    [("kv", slot), ("qno", h)], [("F", sb_)])
                            self.mm(F[sb_][:, 0:ncol], kv[:, 256:384], QR[:, h, cols], False, not masked,
                                    [("kv", slot), ("QRr", h), ("QRb", h), "QRhi"], [("F", sb_)])
                            if masked:
                                self.mm(F[sb_][:, 0:ncol], MK[:, mu, :], QR[:, h, cols], False, True,
                                        ["MK", ("QRb", h)], [("F", sb_)])
                            if len(pend) >= 2:
                                flush(pend.pop(0))
                            self.act(self.PT[pt][:, 0:ncol], F[sb_][:, 0:ncol], AF.Exp, [("F", sb_)], [("PT", pt)],
                                     scale=ATT_SCALE)
                            pend.append((slot, j, hh, bi, pt))
                while pend:
                    flush(pend.pop(0))
                for hh in range(2):
                    h = 2 * hp + hh
                    rc = self.rec[hh]
                    self.mm(F[2 + hh][:, 0:ncol], self.ones32, self.sacc[hh][:, 0:ncol], True, True,
                            ["ones32", ("sacc", hh)], [("F", 2 + hh)])
                    self.S.op("dve", lambda e, rc=rc, hh=hh, ncol=ncol: e.reciprocal(out=rc[:, 0:ncol], in_=F[2 + hh][:, 0:ncol]),
                              reads=[("F", 2 + hh)], writes=[("rec", hh)])
                    self.tt("dve", attnT[:, h, cols], F[hh][:, 0:ncol], rc[:, 0:ncol], ALU.mult,
                            [("F", hh), ("rec", hh)], [("at", h)])

    def build(self):
        cfg = self.cfg
        self.declare()
        I, O = self.I, self.O
        tl = cfg.tiles()
        base_pieces = [P_U, P_V, P_Q, P_G0, P_G1, P_UQN, P_UQR, P_AB0, P_AB1, P_O] + [P_UP0 + t for t in range(6)]
        dn = [P_D0 + i for i in range(4)]
        for t in tl:
            self.wsched += base_pieces + ([] if t["kind"] == "halo" else dn)
        self.wsched += base_pieces + dn
        self.S.alias_names = {"xnb", "vg", "vb", "qnb", "bst", "qno", "QRr", "QRb", "QRhi", "at", "MK", "MK0", "r0", "r1",
                              "csq", "PT", "kv", "rec", "tm", "sacc", "hnb", "aT", "U", "cv", "sg", "Y"}
        self.setup()
        self.prepass()
        ones_col = self.stats[:, 95:96]
        for ti, t in enumerate(tl):
            W = t["W"]
            blocks = list(range(t["nkb"]))
            mku = {n: n - t["mlo"] for n in range(t["mlo"], t["nkb"])}
            halo = t["kind"] == "halo"
            if halo:
                hs = ones_col
            elif t["i"] == 0:
                hs = self.hscale[:, t["J"]:t["J"] + 1]
            else:
                hs = ones_col
            y_out = None
            conv_out = None
            if not halo:
                r0 = t["J"] * cfg.B + t["i"] * cfg.W
                y_out = O["y_own"][r0:r0 + W, :]
                if ti == len(tl) - 1:
                    conv_out = [O["conv_last"]]
            kvg = [(g_, [(j_, mku.get(4 * g_ + j_)) for j_ in range(4) if 4 * g_ + j_ < t["nkb"]])
                   for g_ in range((t["nkb"] + 3) // 4)]
            grp = [(W - 2, 2, kvg)] if halo else [(0, W, kvg)]
            self.tile(W, I["x_own"][t["tok0"]:t["tok0"] + W, :], I["csq"][:, :, t["tok0"]:t["tok0"] + W],
                      I["brow_p"], ti, grp, 1, hs, halo, False, y_out=y_out, conv_out=conv_out)
        self.dma(self.prevsave.rearrange("p a b c -> p (a b c)"), I["conv_state"], [],
                 [("prev", c_) for c_ in range(NFC)], "prevs")
        groups = []
        for b in range(cfg.NSB):
            kvg = [(cfg.NGP + b * cfg.NGC + g_, [(j_, None) for j_ in range(4)]) for g_ in range(cfg.NGC)]
            kvg.append((cfg.GRP_NEW, [(b // 2, b)]))
            groups.append((b * cfg.DS, cfg.DS, kvg))
        self.tile(cfg.WS, I["x_smp"], I["csq_s"], I["brow_s"], len(tl), groups, cfg.NSB, ones_col, False, True,
                  y_out=O["y_smp"], gv_out=O["gv_smp"], conv_out=[O["conv_smp"][b] for b in range(cfg.NSB)])
        self.S.barrier()
        nw = self.S.emit(self.sems)
        self.stats_info = dict(ops=len(self.S.ops), waits=nw, sbuf_words=self.A.hi, dsems=len(self.dsems))
        return self.nc


_PROG_CACHE = {}


def run_cfg(inputs, cfg):
    in_maps = host_prep(inputs, cfg)
    key = (cfg.SEQ, cfg.W, cfg.NJ, cfg.PAST, cfg.NSB, cfg.DS)
    if key not in _PROG_CACHE:
        p = Prog(cfg)
        p.build()
        _PROG_CACHE[key] = p
    p = _PROG_CACHE[key]
    res = run_bass_kernel_spmd(p.nc, in_maps, core_ids=list(range(8)))
    R = res.results
    nb_, SEQ = 2, cfg.SEQ
    y_p = np.zeros((nb_, SEQ, D), np.float32)
    ckv_p = np.zeros((1, nb_, SEQ, KVL), np.float32)
    kr_p = np.zeros((1, nb_, SEQ, ROPE), np.float32)
    cv_p = np.zeros((1, nb_, 2, 2 * DFF), np.float32)
    nsb = 8 * cfg.NSB
    y_s = np.zeros((nsb, cfg.DS, D), np.float32)
    ckv_s = np.zeros((1, nsb, cfg.DS, KVL), np.float32)
    kr_s = np.zeros((1, nsb, cfg.DS, ROPE), np.float32)
    cv_s = np.zeros((1, nsb, 2, 2 * DFF), np.float32)
    gv_s = np.zeros((1, nsb, cfg.DS, GW), np.float32)
    for c in range(8):
        b, r = c // 4, c % 4
        o = R[c]
        yo = np.asarray(o["y_own"], np.float32)
        for J in range(cfg.NJ):
            g = 4 * J + r
            y_p[b, g * cfg.B:(g + 1) * cfg.B] = yo[J * cfg.B:(J + 1) * cfg.B]
        if r == 0:
            ckv_p[0, b] = np.asarray(o["ckv_seq"], np.float32)
            kr_p[0, b] = np.asarray(o["kr_seq"], np.float32)
        if r == 3:
            cv_p[0, b] = np.asarray(o["conv_last"], np.float32)
        sl = slice(c * cfg.NSB, (c + 1) * cfg.NSB)
        y_s[sl] = np.asarray(o["y_smp"], np.float32).reshape(cfg.NSB, cfg.DS, D)
        ckv_s[0, sl] = np.asarray(o["ckv_smp"], np.float32).reshape(cfg.NSB, cfg.DS, KVL)
        kr_s[0, sl] = np.asarray(o["kr_smp"], np.float32).reshape(cfg.NSB, cfg.DS, ROPE)
        cv_s[0, sl] = np.asarray(o["conv_smp"], np.float32)
        gv_s[0, sl] = np.asarray(o["gv_smp"], np.float32).reshape(cfg.NSB, cfg.DS, GW)
    return (y_p, y_s, ckv_p, kr_p, cv_p, ckv_s, kr_s, cv_s, gv_s)


def kernel(**inputs):
    inputs = {k: np.asarray(v) for k, v in inputs.items()}
    cfg = Cfg(SEQ=inputs["x_prompt"].shape[1], W=512, NJ=inputs["x_prompt"].shape[1] // 4096,
              PAST=inputs["cache_mla_ckv"].shape[2], NSB=inputs["x_sample"].shape[0] // 8,
              DS=inputs["x_sample"].shape[1])
    return run_cfg(inputs, cfg)
```

```python
import contextlib
import math
import numpy as np
import ml_dtypes
import concourse.bass as bass
import concourse.mybir as mybir
from concourse.bass_utils import run_bass_kernel_spmd

F32 = mybir.dt.float32
BF16 = mybir.dt.bfloat16
AF = mybir.ActivationFunctionType
ALU = mybir.AluOpType

D = 1024
GW = 1024
QL = 512
KVL = 512
ROPE = 64
NOPE = 128
NH = 8
DFF = 2816
NFC = 44
NPAIR = 22
EPS = 1e-6
ATT_SCALE = (NOPE + ROPE) ** -0.5
BIG = 30000.0
PW = 8192
KVC = 640


class Cfg:
    def __init__(s, SEQ=16384, W=512, NJ=4, PAST=4096, NSB=4, DS=64, NBATCH=2, DEC_BATCH=32):
        s.SEQ, s.W, s.NJ, s.PAST, s.NSB, s.DS = SEQ, W, NJ, PAST, NSB, DS
        s.NBATCH, s.DEC_BATCH = NBATCH, DEC_BATCH
        s.B = 2 * W
        s.B8 = s.B // 128
        s.W8 = W // 128
        assert 4 * NJ * s.B == SEQ
        s.NBP = SEQ // 128
        s.NBC = PAST // 128
        s.WS = NSB * DS
        assert s.WS % 128 == 0
        s.NBS_NEW = s.WS // 128
        s.NBLK = s.NBP + NSB * s.NBC + s.NBS_NEW
        s.OWN = NJ * (128 + s.B)
        s.NMM = 3 * s.B8 + s.W8
        assert s.NBP % 4 == 0 and s.NBC % 4 == 0 and s.NBS_NEW <= 4
        s.NGP = s.NBP // 4
        s.NGC = s.NBC // 4
        s.GRP_NEW = s.NGP + NSB * s.NGC
        s.NGRP = s.GRP_NEW + 1

    def tiles(s):
        out = []
        for J in range(s.NJ):
            base = J * (128 + s.B)
            nk = s.B8 * (4 * J + 3)
            out.append(dict(kind="halo", J=J, i=-1, W=128, tok0=base, nkb=nk,
                            mlo=max(0, s.B8 * 4 * J - 1)))
            for i in range(2):
                nk2 = nk + s.W8 * (i + 1)
                out.append(dict(kind="tile", J=J, i=i, W=s.W, tok0=base + 128 + i * s.W, nkb=nk2,
                                mlo=nk2 - s.NMM))
        return out


class Op:
    __slots__ = ("id", "eng", "fn", "deps", "sem", "inc", "has_dep", "cnt", "waits", "ph")


ENGS = ("pe", "act", "dve", "pool", "sp")


class Sched:
    def __init__(self, nc):
        self.nc = nc
        self.ops = []
        self.last_w = {}
        self.readers = {}
        self.floor = None
        self.last_eng = {}
        self.dma_since = []
        self.ph = "init"
        self.scopes = False
        self.soft = None
        self.alias_names = set()
        self.last_dma = {}

    def op(self, eng, fn, reads=(), writes=(), dma_sem=None, extra=()):
        o = Op()
        o.id = len(self.ops)
        o.eng = eng
        o.fn = fn
        o.sem = dma_sem
        o.has_dep = False
        o.ph = self.ph
        deps = {}
        is_dma = dma_sem is not None

        def add(d, raw):
            if d is None:
                return
            p = self.ops[d]
            if (not raw) and (not is_dma) and p.sem is None and p.eng == eng and eng == "pe":
                return
            deps[d] = True

        for b in reads:
            for d in self.last_w.get(b, {}).values():
                add(d, True)
        for b in writes:
            for d in self.last_w.get(b, {}).values():
                add(d, False)
            for d in self.readers.get(b, {}).values():
                add(d, False)
        for d in extra:
            deps[d] = True
        if self.soft is not None:
            for b in writes:
                nm = b if isinstance(b, str) else b[0]
                if nm in self.alias_names:
                    for d in self.soft:
                        deps[d] = True
                    break
        if self.floor is not None:
            deps[self.floor] = True
        deps.pop(o.id, None)
        o.deps = list(deps)
        who = dma_sem if is_dma else eng
        for b in reads:
            self.readers.setdefault(b, {})[who] = o.id
        for b in writes:
            self.last_w.setdefault(b, {})[who] = o.id
        self.ops.append(o)
        self.last_eng[eng] = o.id
        if is_dma:
            self.dma_since.append(o.id)
            self.last_dma[dma_sem] = o.id
        return o.id

    def soft_barrier(self):
        self.soft = list(self.last_eng.values()) + list(self.last_dma.values())

    def barrier(self):
        extra = list(self.last_eng.values()) + list(self.dma_since)
        self.floor = None
        bid = self.op("sp", lambda e: e.nop(), extra=extra)
        self.floor = bid
        self.dma_since = []
        self.last_w = {}
        self.readers = {}
        self.soft = None
        return bid

    def emit(self, sems):
        ops = self.ops
        for o in ops:
            for d in o.deps:
                ops[d].has_dep = True
        cnt = {}
        waited = {e: {} for e in ENGS}
        per_eng = {e: [] for e in ENGS}
        n_wait = 0
        for o in ops:
            need = {}
            for d in o.deps:
                p = ops[d]
                is_dma = p.sem is not None
                if (not is_dma) and p.eng == "pe" and o.eng == "pe":
                    continue
                key = p.sem if is_dma else sems[p.eng]
                v = p.cnt
                if need.get(key, 0) < v:
                    need[key] = v
            o.waits = []
            for key, v in need.items():
                if waited[o.eng].get(key, 0) >= v:
                    continue
                o.waits.append((key, v))
                waited[o.eng][key] = v
                n_wait += 1
            if o.sem is not None:
                c = cnt.get(o.sem, 0) + 16
                cnt[o.sem] = c
                o.cnt = c
                o.inc = (o.sem, 16)
            elif o.has_dep:
                key = sems[o.eng]
                c = cnt.get(key, 0) + 1
                cnt[key] = c
                o.cnt = c
                o.inc = (key, 1)
            else:
                o.cnt = None
                o.inc = None
            per_eng[o.eng].append(o)

        def run1(h, o):
            for key, v in o.waits:
                h.wait_ge(key, v)
            ins = o.fn(h)
            if o.inc is not None:
                ins.then_inc(o.inc[0], o.inc[1])

        def run(h, lst):
            if not self.scopes:
                for o in lst:
                    run1(h, o)
                return
            i = 0
            while i < len(lst):
                j = i
                while j < len(lst) and lst[j].ph == lst[i].ph:
                    j += 1
                with self.nc.named_scope(lst[i].ph):
                    for o in lst[i:j]:
                        run1(h, o)
                i = j

        with self.nc.Block() as block:
            @block.tensor
            def _(e):
                run(e, per_eng["pe"])

            @block.scalar
            def _(e):
                run(e, per_eng["act"])

            @block.vector
            def _(e):
                run(e, per_eng["dve"])

            @block.gpsimd
            def _(e):
                run(e, per_eng["pool"])

            @block.sync
            def _(e):
                run(e, per_eng["sp"])
        return n_wait


class Arena:
    def __init__(self, tile, nwords):
        self.t = tile
        self.n = nwords
        self.tops = {"": 0}
        self.parent = {}
        self.hi = 0

    def phase(self, name, parent=""):
        self.parent[name] = parent
        self.tops[name] = None

    def _top(self, ph):
        if self.tops[ph] is None:
            self.tops[ph] = self._top(self.parent[ph])
        return self.tops[ph]

    def alloc(self, ph, shape, dt):
        nel = 1
        for d in shape[1:]:
            nel *= d
        nbytes = nel * (4 if dt == F32 else 2)
        nw = (nbytes + 3) // 4
        nw = (nw + 7) // 8 * 8
        off = self._top(ph)
        assert off + nw <= self.n, ("SBUF arena overflow", ph, shape, off, nw, self.n)
        self.tops[ph] = off + nw
        self.hi = max(self.hi, off + nw)
        ap = self.t[:, off:off + nw]
        if dt != F32:
            ap = ap.bitcast(dt)
        ap = ap[:, 0:nel]
        if len(shape) == 3:
            ap = ap.rearrange("p (a b) -> p a b", a=shape[1])
        elif len(shape) == 4:
            ap = ap.rearrange("p (a b c) -> p a b c", a=shape[1], b=shape[2])
        if shape[0] != 128:
            ap = ap[0:shape[0]]
        return ap


def _bf16_bits(a):
    return np.ascontiguousarray(np.asarray(a, np.float32).astype(ml_dtypes.bfloat16)).view(np.uint16)


def _piece(arr):
    K, nc_ = arr.shape
    nkc = K // 128
    a = arr.reshape(nkc, 128, nc_).transpose(1, 0, 2).reshape(128, nkc * nc_)
    out = np.zeros((128, PW), np.float32)
    out[:, :nkc * nc_] = a
    return out, nkc, nc_


def _gtab(g, nkc):
    t = np.ones((128, 12), np.float32)
    if g is not None:
        t[:, :nkc] = np.asarray(g, np.float32).reshape(nkc, 128).T
    return t


def weight_pieces(inp):
    w_in = inp["w_in"][0]
    mix = inp["norm_mix_g"][0]
    qn = inp["mla_q_norm_g"][0]
    ffn = inp["norm_ffn_g"][0]
    w_uq = inp["mla_w_uq"][0]
    w_up = inp["ffn_w_up"][0]
    w_dn = inp["ffn_w_down"][0]
    perm = (np.arange(64) + 32) % 64
    rope = w_uq[:, :, 128:192]
    ropep = rope[:, :, perm]
    specs = [
        (w_in[:, 2560:3136], mix),
        (inp["mla_w_uk"][0].reshape(512, 1024), None),
        (inp["mla_w_uv"][0].reshape(512, 1024), None),
        (w_in[:, 0:1024], mix),
        (w_in[:, 1024:2048], mix),
        (w_in[:, 2048:2560], mix),
        (w_in[:, 3136:4160], mix),
        (w_in[:, 4160:5184], mix),
        (w_uq[:, :, 0:128].reshape(512, 1024), qn),
        (np.concatenate([rope, ropep, ropep, rope], axis=2).reshape(512, 2048), qn),
        (np.concatenate([inp["w_proj_a"][0][:, 0:512], inp["w_proj_b"][0][:, 0:512]], axis=1), None),
        (np.concatenate([inp["w_proj_a"][0][:, 512:1024], inp["w_proj_b"][0][:, 512:1024]], axis=1), None),
        (inp["w_out"][0], None),
    ]
    for t in range(6):
        cols = []
        for p in range(4 * t, min(4 * t + 4, NPAIR)):
            cols.append(w_up[:, p * 128:(p + 1) * 128])
            cols.append(w_up[:, DFF + p * 128:DFF + (p + 1) * 128])
        specs.append((np.concatenate(cols, axis=1), ffn))
    for half in range(2):
        for part in range(2):
            specs.append((w_dn[part * 1408:(part + 1) * 1408, half * 512:(half + 1) * 512], None))
    wraw = np.zeros((len(specs), 128, PW), np.float32)
    wgt = np.ones((128, len(specs), 12), np.float32)
    meta = []
    for i, (a, g) in enumerate(specs):
        wraw[i], nkc, nc_ = _piece(np.ascontiguousarray(a, dtype=np.float32))
        wgt[:, i, :] = _gtab(g, nkc)
        meta.append((nkc, nc_))
    return wraw, wgt, meta


PIECE_META = [(8, 576), (4, 1024), (4, 1024), (8, 1024), (8, 1024), (8, 512), (8, 1024), (8, 1024),
              (4, 1024), (4, 2048), (8, 1024), (8, 1024), (8, 1024),
              (8, 1024), (8, 1024), (8, 1024), (8, 1024), (8, 1024), (8, 512),
              (11, 512), (11, 512), (11, 512), (11, 512)]
NPW = len(PIECE_META)
P_KV, P_UK, P_UV, P_U, P_V, P_Q, P_G0, P_G1, P_UQN, P_UQR, P_AB0, P_AB1, P_O, P_UP0, P_D0 = \
    0, 1, 2, 3, 4, 5, 6, 7, 8, 9, 10, 11, 12, 13, 19
TILE_PIECES = list(range(3, NPW))


def _cs_tables(pos):
    half = ROPE // 2
    inv = (np.float32(10000.0) ** (-np.arange(half, dtype=np.float32) / np.float32(half))).astype(np.float32)
    ang = pos.astype(np.float32)[:, None] * inv[None, :]
    return np.cos(ang).astype(np.float32), np.sin(ang).astype(np.float32)


def _mask_rows(cfg, qstart, W, blocks, fake=False):
    out = np.zeros((8, cfg.NMM * 128), np.float32)
    if fake:
        return out
    for u, n in enumerate(blocks):
        kc = (n * 128 + np.arange(128)) // 64
        for j in range(W // 64):
            qc = qstart // 64 + j
            out[j, u * 128:(u + 1) * 128] = np.where(kc > qc, -BIG, 0.0)
    return out


def host_prep(inp, cfg):
    wraw, wgt, meta = weight_pieces(inp)
    assert meta == PIECE_META, meta
    xp = np.asarray(inp["x_prompt"], np.float32)
    xs = np.asarray(inp["x_sample"], np.float32)
    ckv_c = np.asarray(inp["cache_mla_ckv"], np.float32)[0]
    kr_c = np.asarray(inp["cache_mla_krope"], np.float32)[0]
    cst = np.asarray(inp["state_ffn_conv"], np.float32)[0]
    tl = cfg.tiles()
    shared = {}
    shared["wraw"] = wraw
    shared["wgt"] = wgt
    shared["ident"] = _bf16_bits(np.eye(128))
    shared["onesb"] = _bf16_bits(np.ones((128, 128)))
    ok = np.zeros((128, 128), np.float32)
    ok[0, :] = 1.0
    ok[32, :] = 1.0
    shared["onesk"] = _bf16_bits(ok)
    w_s = np.asarray(inp["gmlp_w_s"], np.float32)[0]
    shared["wst"] = np.ascontiguousarray(w_s.transpose(2, 0, 1))
    tri = (np.arange(128)[:, None] <= np.arange(128)[None, :]).astype(np.float32)
    shared["tri"] = np.ascontiguousarray(np.broadcast_to(tri[:, None, :], (128, 8, 128)))
    wss = np.zeros((128, 8, 128), np.float32)
    w64 = w_s[:, :64, :64].transpose(2, 0, 1)
    wss[0:64, :, 0:64] = w64
    wss[64:128, :, 64:128] = w64
    shared["wss"] = wss
    tri2 = np.zeros((128, 128), np.float32)
    tri2[0:64, 0:64] = tri[0:64, 0:64]
    tri2[64:128, 64:128] = tri[0:64, 0:64]
    shared["tri2"] = np.ascontiguousarray(np.broadcast_to(tri2[:, None, :], (128, 8, 128)))
    b_s = np.asarray(inp["gmlp_b_s"], np.float32)[0]
    bsr = np.zeros((128, 2, 8, 128), np.float32)
    bsr[0, 0] = b_s
    bsr[32, 0] = b_s
    bs64 = np.concatenate([b_s[:, :64], b_s[:, :64]], axis=1)
    bsr[0, 1] = bs64
    bsr[32, 1] = bs64
    shared["bsr"] = bsr.reshape(128, 2 * 8 * 128)
    bc = lambda v, n: np.ascontiguousarray(np.broadcast_to(np.asarray(v, np.float32).reshape(1, n), (128, n)))
    shared["lng"] = bc(inp["gmlp_ln_g"][0], GW)
    shared["lnb"] = bc(inp["gmlp_ln_b"][0], GW)
    shared["gfin"] = bc(inp["final_norm_g"], D)
    shared["gkv"] = bc(inp["mla_kv_norm_g"][0], KVL)
    cw = np.asarray(inp["ffn_conv_w"], np.float32)[0]
    cb = np.asarray(inp["ffn_conv_b"], np.float32)[0]
    cwt = np.zeros((128, NFC, 4), np.float32)
    for k in range(3):
        cwt[:, :, k] = cw[k].reshape(NFC, 128).T
    cwt[:, :, 3] = cb.reshape(NFC, 128).T
    shared["convw"] = cwt.reshape(128, NFC * 4)
    bp = np.zeros((8, 512), np.float32)
    for j in range(8):
        bp[j, j * 64:(j + 1) * 64] = 1.0
    shared["brow_p"] = _bf16_bits(bp)
    bsm = np.zeros((8, 512), np.float32)
    bsm[0, :] = 1.0
    shared["brow_s"] = _bf16_bits(bsm)
    pos_s = cfg.PAST + np.arange(cfg.DS)
    cs_, sn_ = _cs_tables(pos_s)
    cs_tok_s = np.tile(np.concatenate([cs_, sn_], axis=1), (cfg.NSB, 1))
    shared["cs_tok_s"] = np.ascontiguousarray(cs_tok_s, dtype=np.float32)
    c2 = np.concatenate([cs_, cs_], axis=1).T
    s2 = np.concatenate([-sn_, sn_], axis=1).T
    csq_s = np.stack([np.tile(c2, (1, cfg.NSB)), np.tile(s2, (1, cfg.NSB))], axis=0)
    shared["csq_s"] = np.ascontiguousarray(csq_s.transpose(1, 0, 2), dtype=np.float32)
    cs_p, sn_p = _cs_tables(np.arange(cfg.SEQ))
    shared["cs_tok"] = np.ascontiguousarray(np.concatenate([cs_p, sn_p], axis=1), dtype=np.float32)

    in_maps = []
    for c in range(8):
        b, r = c // 4, c % 4
        m = dict(shared)
        m["x_seq"] = np.ascontiguousarray(xp[b])
        xo = np.zeros((cfg.OWN, D), np.float32)
        pos_own = np.zeros((cfg.OWN,), np.int64)
        hs = np.ones((128, cfg.NJ), np.float32)
        mk = np.zeros((len(tl) + 1, 8, cfg.NMM * 128), np.float32)
        for ti, t in enumerate(tl):
            g = 4 * t["J"] + r
            if t["kind"] == "halo":
                q0 = g * cfg.B - 128
                fake = q0 < 0
                if fake:
                    hs[:, t["J"]] = 0.0
                else:
                    xo[t["tok0"]:t["tok0"] + 128] = xp[b, q0:q0 + 128]
                    pos_own[t["tok0"]:t["tok0"] + 128] = np.arange(q0, q0 + 128)
            else:
                q0 = g * cfg.B + t["i"] * cfg.W
                fake = False
                xo[t["tok0"]:t["tok0"] + cfg.W] = xp[b, q0:q0 + cfg.W]
                pos_own[t["tok0"]:t["tok0"] + cfg.W] = np.arange(q0, q0 + cfg.W)
            blocks = list(range(t["mlo"], t["nkb"]))
            mk[ti] = _mask_rows(cfg, max(q0, 0), t["W"], blocks, fake=fake)
        smk = np.zeros((8, cfg.NMM * 128), np.float32)
        for gidx in range(cfg.NSB):
            lo = 64 if gidx % 2 == 0 else 0
            smk[0, gidx * 128 + lo:gidx * 128 + lo + 64] = -BIG
        mk[len(tl)] = smk
        m["maskt"] = _bf16_bits(mk)
        m["x_own"] = xo
        m["hscale"] = hs
        co, so = _cs_tables(pos_own)
        c2o = np.concatenate([co, co], axis=1).T
        s2o = np.concatenate([-so, so], axis=1).T
        m["csq"] = np.ascontiguousarray(np.stack([c2o, s2o], axis=1), dtype=np.float32)
        sb0 = c * cfg.NSB
        m["x_smp"] = np.ascontiguousarray(xs[sb0:sb0 + cfg.NSB].reshape(cfg.WS, D))
        m["ckv_cache"] = np.ascontiguousarray(ckv_c[sb0:sb0 + cfg.NSB].reshape(cfg.NSB * cfg.PAST, KVL))
        m["kr_cache"] = np.ascontiguousarray(kr_c[sb0:sb0 + cfg.NSB].reshape(cfg.NSB * cfg.PAST, ROPE))
        st = cst[sb0:sb0 + cfg.NSB]
        m["conv_state"] = np.ascontiguousarray(st.reshape(cfg.NSB, 2, NFC, 128).transpose(3, 2, 0, 1)).reshape(128, NFC * cfg.NSB * 2)
        in_maps.append(m)
    return in_maps


class Prog:
    def __init__(self, cfg):
        self.cfg = cfg
        self.nc = bass.Bass("TRN2", target_bir_lowering=False)
        self.es = contextlib.ExitStack()
        self.S = Sched(self.nc)
        self.dsems = {}
        self.bank_rr = 0
        self.alt = 0
        self.wpos = 0
        self.wsched = []
        self.wloaded = 0

    def din(self, name, shape, dt=F32):
        return self.nc.dram_tensor(name, list(shape), dt, kind="ExternalInput").ap()

    def dout(self, name, shape, dt=F32):
        return self.nc.dram_tensor(name, list(shape), dt, kind="ExternalOutput").ap()

    def dint(self, name, shape, dt):
        return self.nc.dram_tensor(name, list(shape), dt, kind="Internal").ap()

    def dsem(self, key):
        if key not in self.dsems:
            self.dsems[key] = self.es.enter_context(self.nc.semaphore("d%d" % len(self.dsems)))
        return self.dsems[key]

    def dma(self, out, in_, reads, writes, semkey, slow=False):
        sem = self.dsem(semkey)
        if slow:
            fn = lambda e: e.dma_start(out=out, in_=in_, allow_slow_non_contiguous=True)
        else:
            fn = lambda e: e.dma_start(out=out, in_=in_)
        return self.S.op("sp", fn, reads=reads, writes=writes, dma_sem=sem)

    def mm(self, out, lhsT, rhs, start, stop, reads, writes):
        return self.S.op("pe", lambda e: e.matmul(out, lhsT=lhsT, rhs=rhs, start=start, stop=stop),
                         reads=reads, writes=writes)

    def tr(self, out, in_, reads, writes):
        ident = self.identb
        return self.S.op("pe", lambda e: e.transpose(out=out, in_=in_, identity=ident),
                         reads=list(reads) + ["ident"], writes=writes)

    def act(self, out, in_, func, reads, writes, scale=None, bias=None, accum=None):
        kw = {}
        if scale is not None:
            kw["scale"] = scale
        if bias is not None:
            kw["bias"] = bias
        if accum is not None:
            kw["accum_out"] = accum
        return self.S.op("act", lambda e: e.activation(out=out, in_=in_, func=func, **kw),
                         reads=reads, writes=writes)

    def ts(self, eng, out, in0, s1, s2, op0, op1, reads, writes):
        if op1 is None:
            fn = lambda e: e.tensor_scalar(out=out, in0=in0, scalar1=s1, scalar2=None, op0=op0)
        else:
            fn = lambda e: e.tensor_scalar(out=out, in0=in0, scalar1=s1, scalar2=s2, op0=op0, op1=op1)
        return self.S.op(eng, fn, reads=reads, writes=writes)

    def tt(self, eng, out, in0, in1, op, reads, writes):
        return self.S.op(eng, lambda e: e.tensor_tensor(out=out, in0=in0, in1=in1, op=op),
                         reads=reads, writes=writes)

    def stt(self, out, in0, scalar, in1, op0, op1, reads, writes):
        return self.S.op("dve", lambda e: e.scalar_tensor_tensor(out=out, in0=in0, scalar=scalar, in1=in1,
                                                                  op0=op0, op1=op1),
                         reads=reads, writes=writes)

    def cp(self, eng, out, in_, reads, writes):
        if eng == "act":
            return self.S.op("act", lambda e: e.copy(out=out, in_=in_), reads=reads, writes=writes)
        return self.S.op(eng, lambda e: e.tensor_copy(out=out, in_=in_), reads=reads, writes=writes)

    def memset(self, eng, ap, val, writes):
        return self.S.op(eng, lambda e: e.memset(ap, val), writes=writes)

    def evac_eng(self):
        self.alt ^= 1
        return "act" if self.alt else "dve"

    def bank(self):
        b = self.banks_free[self.bank_rr % len(self.banks_free)]
        self.bank_rr += 1
        return b

    def tbank(self):
        bk = self.bank()
        return self.F[bk][:, 0:256].bitcast(BF16), ("F", bk)

    def rstd(self, ssq, out, n, inv_n, key_in, key_out):
        tmp = self.sttmp[:, 0:n]
        self.ts("dve", tmp, ssq, inv_n, EPS, ALU.mult, ALU.add, reads=[key_in], writes=["sttmp"])
        nh = self.neghalf[:, 0:n]
        self.tt("pool", out, tmp, nh, ALU.pow, reads=["sttmp", "neghalf"], writes=[key_out])

    def w_issue(self, k):
        if k >= len(self.wsched) or k < self.wloaded:
            return
        assert k == self.wloaded
        pid = self.wsched[k]
        nkc, ncol = PIECE_META[pid]
        slot = k % 3
        dst = self.wslot[slot][:, 0:nkc * ncol]
        src = self.wstream[pid, :, 0:nkc * ncol]
        self.dma(dst, src, reads=[("wst", pid)], writes=[("ws", slot)], semkey=("ws", slot))
        self.wloaded = k + 1

    def w_get(self, pid):
        k = self.wpos
        assert self.wsched[k] == pid, (k, self.wsched[k], pid)
        for kk in range(self.wloaded, k + 3):
            self.w_issue(kk)
        self.wpos += 1
        nkc, ncol = PIECE_META[pid]
        slot = k % 3
        ap = self.wslot[slot][:, 0:nkc * ncol].rearrange("p (a b) -> p a b", a=nkc)
        return ap, ("ws", slot)

    def declare(self):
        cfg = self.cfg
        nc = self.nc
        es = self.es
        I = {}
        I["wraw"] = self.din("wraw", [NPW, 128, PW])
        I["wgt"] = self.din("wgt", [128, NPW, 12])
        for n in ("ident", "onesb", "onesk"):
            I[n] = self.din(n, [128, 128], mybir.dt.uint16)
        I["wst"] = self.din("wst", [128, 8, 128])
        I["tri"] = self.din("tri", [128, 8, 128])
        I["wss"] = self.din("wss", [128, 8, 128])
        I["tri2"] = self.din("tri2", [128, 8, 128])
        I["bsr"] = self.din("bsr", [128, 2048])
        I["lng"] = self.din("lng", [128, GW])
        I["lnb"] = self.din("lnb", [128, GW])
        I["gfin"] = self.din("gfin", [128, D])
        I["gkv"] = self.din("gkv", [128, KVL])
        I["convw"] = self.din("convw", [128, NFC * 4])
        I["brow_p"] = self.din("brow_p", [8, 512], mybir.dt.uint16)
        I["brow_s"] = self.din("brow_s", [8, 512], mybir.dt.uint16)
        I["cs_tok_s"] = self.din("cs_tok_s", [cfg.WS, 64])
        I["csq_s"] = self.din("csq_s", [64, 2, cfg.WS])
        I["cs_tok"] = self.din("cs_tok", [cfg.SEQ, 64])
        I["x_seq"] = self.din("x_seq", [cfg.SEQ, D])
        I["x_own"] = self.din("x_own", [cfg.OWN, D])
        I["hscale"] = self.din("hscale", [128, cfg.NJ])
        self.ntiles = len(cfg.tiles())
        I["maskt"] = self.din("maskt", [self.ntiles + 1, 8, cfg.NMM * 128], mybir.dt.uint16)
        I["csq"] = self.din("csq", [64, 2, cfg.OWN])
        I["x_smp"] = self.din("x_smp", [cfg.WS, D])
        I["ckv_cache"] = self.din("ckv_cache", [cfg.NSB * cfg.PAST, KVL])
        I["kr_cache"] = self.din("kr_cache", [cfg.NSB * cfg.PAST, ROPE])
        I["conv_state"] = self.din("conv_state", [128, NFC * cfg.NSB * 2])
        self.I = I
        O = {}
        O["y_own"] = self.dout("y_own", [cfg.NJ * cfg.B, D])
        O["ckv_seq"] = self.dout("ckv_seq", [cfg.SEQ, KVL])
        O["kr_seq"] = self.dout("kr_seq", [cfg.SEQ, ROPE])
        O["conv_last"] = self.dout("conv_last", [2, 2 * DFF])
        O["y_smp"] = self.dout("y_smp", [cfg.WS, D])
        O["ckv_smp"] = self.dout("ckv_smp", [cfg.WS, KVL])
        O["kr_smp"] = self.dout("kr_smp", [cfg.WS, ROPE])
        O["conv_smp"] = self.dout("conv_smp", [cfg.NSB, 2, 2 * DFF])
        O["gv_smp"] = self.dout("gv_smp", [cfg.WS, GW])
        self.O = O
        self.wstream = self.dint("wstream", [NPW, 128, PW], BF16)
        self.kvscr = self.dint("kvscr", [4, cfg.NGRP, 128, 4, KVC], BF16)

        NW = 50 * 1024
        big = es.enter_context(nc.sbuf_tensor("arena", [128, NW], F32))
        A = Arena(big, NW)
        self.A = A
        al = A.alloc
        self.identb = al("", [128, 128], BF16)
        self.onesb = al("", [128, 128], BF16)
        self.onesk = al("", [128, 128], BF16)
        self.ones32 = al("", [128, 128], F32)
        self.wstm = al("", [128, 8, 128], BF16)
        self.wssm = al("", [128, 8, 128], BF16)
        self.bsrb = al("", [128, 2, 1024], BF16)
        self.lng = al("", [128, GW], F32)
        self.lnb = al("", [128, GW], F32)
        self.gfin = al("", [128, D], F32)
        self.gkv = al("", [128, KVL], F32)
        self.convw = al("", [128, NFC, 4], F32)
        self.hscale = al("", [128, cfg.NJ], F32)
        self.neghalf = al("", [128, 8], F32)
        self.sttmp = al("", [128, 8], F32)
        self.stats = al("", [128, 96], F32)
        self.prevsave = al("", [128, NFC, cfg.NSB, 2], F32)
        self.wslot = [al("", [128, PW], BF16) for _ in range(3)]
        self.wgt = al("", [128, NPW, 12], F32)
        A.phase("wprep", "")
        self.wp_in = [al("wprep", [128, PW], F32) for _ in range(2)]
        self.wp_out = [al("wprep", [128, PW], BF16) for _ in range(2)]
        self.wp_f = [al("wprep", [128, 1024], F32) for _ in range(4)]
        A.phase("pre", "")
        G = 4
        self.XP = [al("pre", [128, G, D], F32) for _ in range(2)]
        self.p_xnb = [[al("pre", [128, D], BF16) for _ in range(G)] for _ in range(2)]
        self.p_xnT = al("pre", [128, 8, G * 128], BF16)
        self.p_ckv32 = [al("pre", [128, G, KVL], F32) for _ in range(2)]
        self.p_ckvb = [al("pre", [128, KVL], BF16) for _ in range(G)]
        self.p_krraw = al("pre", [128, G, ROPE], F32)
        self.p_kr32 = [al("pre", [128, G, ROPE], F32) for _ in range(2)]
        self.p_krb = al("pre", [128, G, ROPE], BF16)
        self.p_cs = [al("pre", [128, G, 64], F32) for _ in range(2)]
        self.p_t = [al("pre", [128, G, 32], F32) for _ in range(4)]
        self.p_ckvT = al("pre", [128, 4, G * 128], BF16)
        self.p_stg = [al("pre", [128, G, 4, KVC], BF16) for _ in range(1)]
        self.p_junk = al("pre", [128, D], BF16)
        self.p_st = al("pre", [128, 2, 32], F32)
        A.phase("main", "")
        W = max(cfg.W, cfg.WS)
        self.Wmax = W
        NS = W // 128
        self.X = al("main", [128, NS, D], F32)
        self.bufA = al("main", [128, 8, W], BF16)
        self.uT = al("main", [128, 8, W], BF16)
        self.gT = al("main", [128, 16, W], BF16)
        self.qnT = al("main", [128, 4, W], BF16)
        self.junk = al("main", [128, D], BF16)
        A.phase("front", "main")
        self.xnb = [al("front", [128, D], BF16) for _ in range(2)]
        self.vg = [al("front", [128, GW], F32) for _ in range(2)]
        self.vb = al("front", [128, NS, GW], BF16)
        self.qnb = [al("front", [128, QL], BF16) for _ in range(NS)]
        self.bst = al("front", [128, 2, 6], F32)
        A.phase("attn", "main")
        self.qnopeT = al("attn", [128, 8, W], BF16)
        self.QR = al("attn", [128, 8, W], BF16)
        self.attnT = al("attn", [128, 8, W], BF16)
        self.MK = al("attn", [128, cfg.NMM, 128], BF16)
        self.ropet = [al("attn", [128, W], F32) for _ in range(2)]
        self.csq = al("attn", [128, 2, W], F32)
        self.PT = [al("attn", [128, W], BF16) for _ in range(4)]
        self.NKV = 3
        self.KV = [al("attn", [128, 4, KVC], BF16) for _ in range(self.NKV)]
        self.rec = [al("attn", [128, W], F32) for _ in range(2)]
        self.sacc = [al("attn", [128, W], F32) for _ in range(2)]
        self.tmpm = self.sacc
        A.phase("ffn", "main")
        self.hnb = [al("ffn", [128, D], BF16) for _ in range(2)]
        self.actT = al("ffn", [128, NPAIR, W], BF16)
        self.U = [[al("ffn", [128, W + 2 * cfg.NSB], F32) for _ in range(2)] for _ in range(2)]
        self.cv = [[al("ffn", [128, W], F32) for _ in range(2)] for _ in range(2)]
        self.sg = [al("ffn", [128, W], F32) for _ in range(2)]
        self.Y = [al("ffn", [128, D], F32) for _ in range(2)]
        self.F = [es.enter_context(nc.psum_tensor("F%d" % i, [128, 512], F32)) for i in range(8)]
        self.banks_free = list(range(8))
        self.sems = {e: es.enter_context(nc.semaphore("s_" + e)) for e in ENGS}

    def setup(self):
        I = self.I
        self.S.ph = "setup"
        bf = lambda ap: ap.bitcast(BF16)
        self.dma(self.identb, bf(I["ident"]), [], ["ident"], "c0")
        self.dma(self.onesb, bf(I["onesb"]), [], ["onesb"], "c1")
        self.dma(self.onesk, bf(I["onesk"]), [], ["onesk"], "c2")
        self.dma(self.lng, I["lng"], [], ["lng"], "c3")
        self.dma(self.lnb, I["lnb"], [], ["lnb"], "c4")
        self.dma(self.gfin, I["gfin"], [], ["gfin"], "c5")
        self.dma(self.gkv, I["gkv"], [], ["gkv"], "c6")
        self.dma(self.convw.rearrange("p a b -> p (a b)"), I["convw"], [], ["convw"], "c7")
        self.dma(self.hscale, I["hscale"], [], ["hscale"], "c8")
        self.dma(self.wgt.rearrange("p a b -> p (a b)"), I["wgt"].rearrange("p a b -> p (a b)"), [], ["wgt"], "c9")
        self.memset("pool", self.neghalf, -0.5, ["neghalf"])
        self.memset("pool", self.ones32, 1.0, ["ones32"])
        self.memset("pool", self.prevsave.rearrange("p a b c -> p (a b c)"), 0.0, ["prevsave"])
        self.memset("pool", self.stats, 1.0, ["onescol"])
        f = self.wp_f
        flat = lambda ap: ap.rearrange("p a b -> p (a b)")
        self.dma(f[0], flat(I["wst"]), [], ["f0"], "f0")
        self.dma(f[1], flat(I["tri"]), [], ["f1"], "f1")
        self.dma(f[2], flat(I["wss"]), [], ["f2"], "f2")
        self.dma(f[3], flat(I["tri2"]), [], ["f3"], "f3")
        self.tt("dve", flat(self.wstm), f[0], f[1], ALU.mult, ["f0", "f1"], ["wstm"])
        self.tt("dve", flat(self.wssm), f[2], f[3], ALU.mult, ["f2", "f3"], ["wssm"])
        src = self.wp_in[0][:, 0:2048]
        tmpb = self.wp_out[0][:, 0:2048]
        bs = self.bsrb.rearrange("p a b -> p (a b)")
        self.dma(src, I["bsr"], [], ["bsrc"], "bsrc")
        self.memset("pool", bs, 0.0, ["bsrb"])
        self.cp("dve", bs[0:1, :], src[0:1, :], ["bsrc"], ["bsrb"])
        self.cp("dve", tmpb[32:33, :], src[32:33, :], ["bsrc"], ["btmp"])
        self.tt("dve", src[32:33, :], src[32:33, :], tmpb[32:33, :], ALU.subtract, ["bsrc", "btmp"], ["bsrc2"])
        self.cp("dve", bs[32:33, :], src[32:33, :], ["bsrc2"], ["bsrb"])
        self.S.barrier()
        self.S.ph = "wprep"
        engs = ["dve", "act"]
        k = 0
        for pi in range(NPW):
            nkc, ncol = PIECE_META[pi]
            b = pi % 2
            n = nkc * ncol
            self.dma(self.wp_in[b][:, 0:n], I["wraw"][pi, :, 0:n], [], [("wpi", b)], ("wpi", b))
            for kc in range(nkc):
                eng = engs[k % 2]
                k += 1
                o = self.wp_out[b][:, kc * ncol:(kc + 1) * ncol]
                i_ = self.wp_in[b][:, kc * ncol:(kc + 1) * ncol]
                sc = self.wgt[:, pi, kc:kc + 1]
                if eng == "act":
                    self.act(o, i_, AF.Copy, [("wpi", b), "wgt"], [("wpo", b)], scale=sc)
                else:
                    self.ts(eng, o, i_, sc, None, ALU.mult, None, [("wpi", b), "wgt"], [("wpo", b)])
            self.dma(self.wstream[pi, :, 0:n], self.wp_out[b][:, 0:n], [("wpo", b)], [("wst", pi)], ("wpo", b))
        self.S.barrier()

    def pre_setup(self):
        for i, pid in enumerate((P_KV, P_UK, P_UV)):
            nkc, ncol = PIECE_META[pid]
            self.dma(self.wslot[i][:, 0:nkc * ncol], self.wstream[pid, :, 0:nkc * ncol], [], [("pw", i)], ("pw", i))
        stg = self.p_stg[0]
        self.memset("pool", stg.rearrange("p a b c -> p (a b c)"), 0.0, ["stg_all"])
        for gi in range(4):
            self.memset("pool", stg[64:65, gi, :, 256:384], 1.0, ["stg_all"])
        self.Wkv = self.wslot[0][:, 0:8 * 576].rearrange("p (a b) -> p a b", a=8)
        self.Wuk = self.wslot[1][:, 0:4 * 1024].rearrange("p (a b) -> p a b", a=4)
        self.Wuv = self.wslot[2][:, 0:4 * 1024].rearrange("p (a b) -> p a b", a=4)
        self.pre_first = True

    def kv_stage_a(self, kind, G, par, x_src=None, cs_src=None, ckv_src=None, kr_src=None):
        st = self.p_st
        if kind == "x":
            self.dma(self.p_cs[par][:, 0:G, :], cs_src.rearrange("(g p) c -> p g c", p=128), [], [("pcs", par)], ("pcs", par))
            for gi in range(G):
                self.act(self.p_junk, self.XP[par][:, gi, :], AF.Square, [("XP", par)], ["pjunk", ("pssq", par)],
                         accum=st[:, par, gi:gi + 1])
            self.ts("dve", st[:, par, 4:4 + G], st[:, par, 0:G], 1.0 / D, EPS, ALU.mult, ALU.add,
                    [("pssq", par)], [("pms", par)])
            self.tt("pool", st[:, par, 8:8 + G], st[:, par, 4:4 + G], self.neghalf[:, 0:G], ALU.pow,
                    [("pms", par), "neghalf"], [("prs", par)])
            for gi in range(G):
                self.ts("dve", self.p_xnb[par][gi], self.XP[par][:, gi, :], st[:, par, 8 + gi:9 + gi], None, ALU.mult, None,
                        [("XP", par), ("prs", par)], [("pxnb", par, gi)])
        else:
            self.dma(self.p_ckv32[par][:, 0:G, :], ckv_src.rearrange("(g p) c -> p g c", p=128), [], [("pckv32", par)], ("pckv32", par))
            self.dma(self.p_kr32[par][:, 0:G, :], kr_src.rearrange("(g p) c -> p g c", p=128), [], [("pkr32", par)], ("pkr32", par))

    def kv_a_load(self, kind, G, par, x_src=None, cs_src=None, ckv_src=None, kr_src=None):
        if kind == "x":
            self.dma(self.XP[par][:, 0:G, :], x_src.rearrange("(g p) d -> p g d", p=128), [], [("XP", par)], ("XP", par))

    def kv_b(self, kind, G, par, blks, ckv_out=None, kr_out=None):
        if kind == "x":
            for gi in range(G):
                rows = slice(gi * 128, (gi + 1) * 128)
                for hf in range(2):
                    Tv, tk = self.tbank()
                    for j in range(4):
                        kc = 4 * hf + j
                        self.tr(Tv[:, j * 128:(j + 1) * 128],
                                self.p_xnb[par][gi][:, kc * 128:(kc + 1) * 128], [("pxnb", par, gi)], [tk])
                    self.cp(self.evac_eng(), self.p_xnT[:, 4 * hf:4 * hf + 4, rows],
                            Tv.rearrange("p (a b) -> p a b", a=4), [tk], [("pxnT", gi, hf)])

    def kv_c(self, kind, G, par, blks, ckv_out=None, kr_out=None):
        F = self.F
        st = self.p_st
        kr32 = self.p_kr32[par]
        if kind == "x":
            banks = []
            for gi in range(G):
                rows = slice(gi * 128, (gi + 1) * 128)
                ba = self.bank()
                for kc in range(8):
                    self.mm(F[ba][:, 0:512], self.p_xnT[:, kc, rows], self.Wkv[:, kc, 0:512], kc == 0, kc == 7,
                            [("pxnT", gi, 0), ("pxnT", gi, 1), ("pw", 0)], [("F", ba)])
                banks.append(ba)
            bb = self.bank()
            for gi in range(G):
                rows = slice(gi * 128, (gi + 1) * 128)
                for kc in range(8):
                    self.mm(F[bb][:, gi * 64:(gi + 1) * 64], self.p_xnT[:, kc, rows], self.Wkv[:, kc, 512:576],
                            kc == 0, kc == 7, [("pxnT", gi, 0), ("pxnT", gi, 1), ("pw", 0)], [("F", bb)])
            for gi in range(G):
                self.act(self.p_junk[:, 0:512], F[banks[gi]][:, 0:512], AF.Square, [("F", banks[gi])],
                         ["pjunk", ("pssc", par)], accum=st[:, par, 12 + gi:13 + gi])
            self.cp("act", self.p_krraw[:, 0:G, :], F[bb][:, 0:G * 64].rearrange("p (a b) -> p a b", a=G),
                    [("F", bb)], ["pkraw"])
            self.ts("dve", st[:, par, 16:16 + G], st[:, par, 12:12 + G], 1.0 / KVL, EPS, ALU.mult, ALU.add,
                    [("pssc", par)], [("pmc", par)])
            self.tt("pool", st[:, par, 20:20 + G], st[:, par, 16:16 + G], self.neghalf[:, 0:G], ALU.pow,
                    [("pmc", par), "neghalf"], [("prc", par)])
            for gi in range(G):
                self.stt(self.p_ckv32[par][:, gi, :], F[banks[gi]][:, 0:512], st[:, par, 20 + gi:21 + gi], self.gkv,
                         ALU.mult, ALU.mult, [("F", banks[gi]), ("prc", par), "gkv"], [("pckv32", par)])
            self.dma(ckv_out.rearrange("(g p) c -> p g c", p=128), self.p_ckv32[par][:, 0:G, :], [("pckv32", par)], [],
                     ("pckv32", par))
            raw, cs, t = self.p_krraw, self.p_cs[par], self.p_t
            rk = ["pkraw", ("pcs", par)]
            g_ = slice(0, G)
            self.tt("dve", t[0][:, g_, :], raw[:, g_, 0:32], cs[:, g_, 0:32], ALU.mult, rk, ["pt0"])
            self.tt("dve", t[1][:, g_, :], raw[:, g_, 32:64], cs[:, g_, 32:64], ALU.mult, rk, ["pt1"])
            self.tt("dve", t[2][:, g_, :], raw[:, g_, 0:32], cs[:, g_, 32:64], ALU.mult, rk, ["pt2"])
            self.tt("dve", t[3][:, g_, :], raw[:, g_, 32:64], cs[:, g_, 0:32], ALU.mult, rk, ["pt3"])
            self.tt("dve", kr32[:, g_, 0:32], t[0][:, g_, :], t[1][:, g_, :], ALU.subtract, ["pt0", "pt1"], [("pkr32", par)])
            self.tt("dve", kr32[:, g_, 32:64], t[2][:, g_, :], t[3][:, g_, :], ALU.add, ["pt2", "pt3"], [("pkr32", par)])
            self.dma(kr_out.rearrange("(g p) c -> p g c", p=128), kr32[:, 0:G, :], [("pkr32", par)], [], ("pkr32", par))

    def kv_d(self, kind, G, par, blks, ckv_out=None, kr_out=None):
        F = self.F
        stg = self.p_stg[0]
        sdep = ["stg_all"]
        kr32 = self.p_kr32[par]
        for gi in range(G):
            self.cp("act", self.p_ckvb[gi], self.p_ckv32[par][:, gi, :], [("pckv32", par)], [("pckvb", gi)])
        self.cp("dve", self.p_krb[:, 0:G, :], kr32[:, 0:G, :], [("pkr32", par)], ["pkrb"])
        for gi in range(G):
            rows = slice(gi * 128, (gi + 1) * 128)
            Tv, tk = self.tbank()
            for kc in range(4):
                self.tr(Tv[:, kc * 128:(kc + 1) * 128], self.p_ckvb[gi][:, kc * 128:(kc + 1) * 128],
                        [("pckvb", gi)], [tk])
            self.cp(self.evac_eng(), self.p_ckvT[:, 0:4, rows],
                    Tv.rearrange("p (a b) -> p a b", a=4), [tk], [("pckvT", gi)])
        Tv2, tk2 = self.tbank()
        for gi in range(G):
            self.tr(Tv2[0:64, gi * 128:(gi + 1) * 128], self.p_krb[:, gi, :], ["pkrb"], [tk2])
        for hp in range(4):
            self.cp("dve" if hp % 2 else "act", stg[0:64, 0:G, hp, 256:384],
                    Tv2[0:64, 0:G * 128].rearrange("p (a b) -> p a b", a=G),
                    [tk2] + sdep, [("stg", gi) for gi in range(G)])

    def kv_e1(self, kind, G, par, blks, ckv_out=None, kr_out=None):
        F = self.F
        stg = self.p_stg[0]
        sdep = ["stg_all"]
        GW_ = G * 128
        for h in range(NH):
            bk = self.bank()
            for kc in range(4):
                self.mm(F[bk][:, 0:GW_], self.Wuk[:, kc, h * 128:(h + 1) * 128], self.p_ckvT[:, kc, 0:GW_],
                        kc == 0, kc == 3, [("pckvT", gi) for gi in range(G)] + [("pw", 1)], [("F", bk)])
            self.cp("dve", stg[:, 0:G, h // 2, (h % 2) * 128:(h % 2 + 1) * 128],
                    F[bk][:, 0:GW_].rearrange("p (a b) -> p a b", a=G), [("F", bk)] + sdep,
                    [("stg", gi) for gi in range(G)])

    def kv_e2(self, kind, G, par, blks, ckv_out=None, kr_out=None):
        F = self.F
        stg = self.p_stg[0]
        sdep = ["stg_all"]
        for gi in range(G):
            rows = slice(gi * 128, (gi + 1) * 128)
            for half in range(2):
                bk = self.bank()
                for kc in range(4):
                    self.mm(F[bk][:, 0:512], self.p_ckvT[:, kc, rows], self.Wuv[:, kc, half * 512:(half + 1) * 512],
                            kc == 0, kc == 3, [("pckvT", gi), ("pw", 2)], [("F", bk)])
                self.cp("dve", stg[:, gi, 2 * half:2 * half + 2, 384:640],
                        F[bk][:, 0:512].rearrange("p (a b) -> p a b", a=2), [("F", bk)] + sdep, [("stg", gi)])
        for hp in range(4):
            self.dma(self.kvscr[hp, blks], stg[:, :, hp, :], [("stg", gi) for gi in range(4)], [("kvs", blks, hp)],
                     ("stgd", hp))

    def prepass(self):
        cfg = self.cfg
        self.S.ph = "prepass"
        I, O = self.I, self.O
        self.pre_setup()
        G = 4
        jobs = []
        for g0 in range(0, cfg.NBP, G):
            r = slice(g0 * 128, (g0 + G) * 128)
            jobs.append(dict(kind="x", G=G, blks=g0 // 4,
                             a=dict(x_src=I["x_seq"][r, :], cs_src=I["cs_tok"][r, :]),
                             r=dict(ckv_out=O["ckv_seq"][r, :], kr_out=O["kr_seq"][r, :])))
        Gs = cfg.NBS_NEW
        jobs.append(dict(kind="x", G=Gs, blks=cfg.GRP_NEW,
                         a=dict(x_src=I["x_smp"], cs_src=I["cs_tok_s"]),
                         r=dict(ckv_out=O["ckv_smp"], kr_out=O["kr_smp"])))
        for b in range(cfg.NSB):
            for g0 in range(0, cfg.NBC, 4):
                r = slice(b * cfg.PAST + g0 * 128, b * cfg.PAST + (g0 + 4) * 128)
                jobs.append(dict(kind="cache", G=4, blks=cfg.NGP + b * cfg.NGC + g0 // 4,
                                 a=dict(ckv_src=I["ckv_cache"][r, :], kr_src=I["kr_cache"][r, :]), r={}))
        n = len(jobs)

        def call(fn, i, stage_a=0):
            if i >= n:
                return
            jb = jobs[i]
            if stage_a == 1:
                self.kv_stage_a(jb["kind"], jb["G"], i % 2, **jb["a"])
            elif stage_a == 2:
                self.kv_a_load(jb["kind"], jb["G"], i % 2, **jb["a"])
            else:
                fn(jb["kind"], jb["G"], i % 2, jb["blks"], **jb["r"])

        call(None, 0, 2)
        call(None, 1, 2)
        call(None, 0, 1)
        call(None, 1, 1)
        call(None, 2, 2)
        call(self.kv_b, 0)
        call(self.kv_c, 0)
        call(self.kv_b, 1)
        for i in range(n):
            call(self.kv_d, i)
            call(None, i + 2, 1)
            call(None, i + 3, 2)
            call(self.kv_e1, i)
            call(self.kv_c, i + 1)
            call(self.kv_e2, i)
            call(self.kv_b, i + 2)
        self.S.barrier()

    def tile(self, W, x_src, csq_src, brow_src, mask_idx, groups, nb, hs_col, is_halo, sample,
             y_out=None, gv_out=None, conv_out=None):
        cfg = self.cfg
        S = self.S
        F = self.F
        I = self.I
        NS = W // 128
        wb = W // nb
        bufA, uT, gT, qnT, X = self.bufA, self.uT, self.gT, self.qnT, self.X
        st = self.stats
        S.soft_barrier()
        kind_ = ("smp" if sample else ("halo" if is_halo else "tile")) + str(mask_idx)
        S.ph = kind_ + ".front"
        for s in range(NS):
            self.dma(X[:, s, :], x_src[s * 128:(s + 1) * 128, :], [], [("X", s)], ("X", s))
        for s in range(NS):
            self.act(self.junk, X[:, s, :], AF.Square, [("X", s)], ["junk", ("ssq", s)], accum=st[:, s:s + 1])
            self.rstd(st[:, s:s + 1], st[:, 8 + s:9 + s], 1, 1.0 / D, ("ssq", s), ("rs", s))
            if s % 2 == 0:
                self.act(self.xnb[s % 2], X[:, s, :], AF.Copy, [("X", s), ("rs", s)], [("xnb", s % 2)], scale=st[:, 8 + s:9 + s])
            else:
                self.ts("dve", self.xnb[s % 2], X[:, s, :], st[:, 8 + s:9 + s], None, ALU.mult, None,
                        [("X", s), ("rs", s)], [("xnb", s % 2)])
            for hf in range(2):
                Tv, tk = self.tbank()
                for j in range(4):
                    kc = 4 * hf + j
                    self.tr(Tv[:, j * 128:(j + 1) * 128],
                            self.xnb[s % 2][:, kc * 128:(kc + 1) * 128], [("xnb", s % 2)], [tk])
                self.cp(self.evac_eng(), bufA[:, 4 * hf:4 * hf + 4, s * 128:(s + 1) * 128],
                        Tv.rearrange("p (a b) -> p a b", a=4), [tk], [("bA", s)])
        bA_all = [("bA", s) for s in range(NS)]
        Wu, wk = self.w_get(P_U)
        for j in range(8):
            bk = self.bank()
            for kc in range(8):
                self.mm(F[bk][:, 0:W], Wu[:, kc, j * 128:(j + 1) * 128], bufA[:, kc, 0:W], kc == 0, kc == 7,
                        bA_all + [wk], [("F", bk)])
            self.act(uT[:, j, 0:W], F[bk][:, 0:W], AF.Gelu_apprx_tanh, [("F", bk)], [("uT", j)])
        Wv, wk = self.w_get(P_V)
        for s in range(NS):
            vg = self.vg[s % 2]
            vk = ("vg", s % 2)
            for half in range(2):
                bk = self.bank()
                for kc in range(8):
                    self.mm(F[bk][:, 0:512], bufA[:, kc, s * 128:(s + 1) * 128], Wv[:, kc, half * 512:(half + 1) * 512],
                            kc == 0, kc == 7, [("bA", s), wk], [("F", bk)])
                self.act(vg[:, half * 512:(half + 1) * 512], F[bk][:, 0:512], AF.Gelu_apprx_tanh, [("F", bk)], [vk])
            for half in range(2):
                S.op("dve", lambda e, half=half, vg=vg: e.bn_stats(out=self.bst[:, half, :], in_=vg[:, half * 512:(half + 1) * 512]),
                     reads=[vk], writes=[("bst", half)])
            S.op("dve", lambda e: e.bn_aggr(out=st[:, 32:34], in_=self.bst.rearrange("p a b -> p (a b)")),
                 reads=[("bst", 0), ("bst", 1)], writes=["lnmv"])
            self.rstd(st[:, 33:34], st[:, 34:35], 1, 1.0, "lnmv", "lnrs")
            self.ts("dve", vg, vg, st[:, 32:33], st[:, 34:35], ALU.subtract, ALU.mult, [vk, "lnmv", "lnrs"], [vk])
            self.tt("dve", vg, vg, self.lng, ALU.mult, [vk, "lng"], [vk])
            self.tt("dve", vg, vg, self.lnb, ALU.add, [vk, "lnb"], [vk])
            if gv_out is not None:
                self.dma(gv_out[s * 128:(s + 1) * 128, :], vg, [vk], [], vk)
            self.cp("act", self.vb[:, s, :], vg, [vk], [("vb", s)])
        Wq, wk = self.w_get(P_Q)
        for s in range(NS):
            bk = self.bank()
            for kc in range(8):
                self.mm(F[bk][:, 0:512], bufA[:, kc, s * 128:(s + 1) * 128], Wq[:, kc, 0:512], kc == 0, kc == 7,
                        [("bA", s), wk], [("F", bk)])
            self.act(self.junk[:, 0:512], F[bk][:, 0:512], AF.Square, [("F", bk)], ["junk", ("qss", s)],
                     accum=st[:, 40 + s:41 + s])
            self.rstd(st[:, 40 + s:41 + s], st[:, 44 + s:45 + s], 1, 1.0 / QL, ("qss", s), ("qrs", s))
            self.act(self.qnb[s], F[bk][:, 0:512], AF.Copy, [("F", bk), ("qrs", s)], [("qnb", s)],
                     scale=st[:, 44 + s:45 + s])
        for gi_, pid in enumerate((P_G0, P_G1)):
            Wg, wk = self.w_get(pid)
            for jj in range(8):
                j = 8 * gi_ + jj
                bk = self.bank()
                for kc in range(8):
                    self.mm(F[bk][:, 0:W], Wg[:, kc, jj * 128:(jj + 1) * 128], bufA[:, kc, 0:W], kc == 0, kc == 7,
                            bA_all + [wk], [("F", bk)])
                self.act(gT[:, j, 0:W], F[bk][:, 0:W], AF.Sigmoid, [("F", bk)], [("gT", j)])
        wsm = self.wssm if sample else self.wstm
        bsel = 1 if sample else 0
        for g in range(8):
            bk = self.bank()
            for s in range(NS):
                self.mm(F[bk][:, s * 128:(s + 1) * 128], self.vb[:, s, g * 128:(g + 1) * 128], wsm[:, g, :], True, False,
                        [("vb", s), "wstm", "wssm"], [("F", bk)])
                self.mm(F[bk][:, s * 128:(s + 1) * 128], self.onesk, self.bsrb[:, bsel, g * 128:(g + 1) * 128], False, True,
                        ["onesk", "bsrb"], [("F", bk)])
            self.tt("dve", uT[:, g, 0:W], F[bk][:, 0:W], uT[:, g, 0:W], ALU.mult, [("F", bk), ("uT", g)], [("uT", g)])
        for s in range(NS):
            Tv, tk = self.tbank()
            for kc in range(4):
                self.tr(Tv[:, kc * 128:(kc + 1) * 128], self.qnb[s][:, kc * 128:(kc + 1) * 128],
                        [("qnb", s)], [tk])
            self.cp("dve", qnT[:, 0:4, s * 128:(s + 1) * 128], Tv.rearrange("p (a b) -> p a b", a=4),
                    [tk], [("qnT", s)])
        S.soft_barrier()
        S.ph = kind_ + ".qhead"
        QR, qnopeT, attnT, MK = self.QR, self.qnopeT, self.attnT, self.MK
        self.memset("pool", QR[64:128, :, :].rearrange("p a b -> p (a b)"), 0.0, ["QRhi"])
        for h in range(NH):
            self.dma(QR[96:104, h, 0:W], brow_src[:, 0:W].bitcast(BF16), ["QRhi"], [("QRb", h)], ("QRb", h))
        self.memset("pool", MK.rearrange("p a b -> p (a b)"), 0.0, ["MK0"])
        self.dma(MK[96:104, :, :].rearrange("p a b -> p (a b)"), I["maskt"][mask_idx].bitcast(BF16), ["MK0"], ["MK"], "MK")
        self.dma(self.csq[0:64, :, 0:W], csq_src, [], ["csq"], "csq")
        qnT_all = [("qnT", s) for s in range(NS)]
        Wn, wk = self.w_get(P_UQN)
        for h in range(NH):
            bk = self.bank()
            for kc in range(4):
                self.mm(F[bk][:, 0:W], Wn[:, kc, h * 128:(h + 1) * 128], qnT[:, kc, 0:W], kc == 0, kc == 3,
                        qnT_all + [wk], [("F", bk)])
            self.cp(self.evac_eng(), qnopeT[:, h, 0:W], F[bk][:, 0:W], [("F", bk)], [("qno", h)])
        Wr, wk = self.w_get(P_UQR)
        for h in range(NH):
            ba, bb = self.bank(), self.bank()
            for kc in range(4):
                self.mm(F[ba][:, 0:W], Wr[:, kc, h * 256:h * 256 + 128], qnT[:, kc, 0:W], kc == 0, kc == 3,
                        qnT_all + [wk], [("F", ba)])
            for kc in range(4):
                self.mm(F[bb][:, 0:W], Wr[:, kc, h * 256 + 128:h * 256 + 256], qnT[:, kc, 0:W], kc == 0, kc == 3,
                        qnT_all + [wk], [("F", bb)])
            r0, r1 = self.ropet[0], self.ropet[1]
            self.tt("dve", r0[0:64, 0:W], F[ba][0:64, 0:W], self.csq[0:64, 0, 0:W], ALU.mult, [("F", ba), "csq"], ["r0"])
            self.tt("dve", r1[0:64, 0:W], F[bb][0:64, 0:W], self.csq[0:64, 1, 0:W], ALU.mult, [("F", bb), "csq"], ["r1"])
            self.tt("pool", QR[0:64, h, 0:W], r0[0:64, 0:W], r1[0:64, 0:W], ALU.add, ["r0", "r1"], [("QRr", h)])
        if is_halo:
            self.memset("pool", attnT.rearrange("p a b -> p (a b)"), 0.0, [("at", h_) for h_ in range(NH)])
        S.ph = kind_ + ".attn"
        self.attention(groups)
        S.ph = kind_ + ".proj"
        for pi_, pid in enumerate((P_AB0, P_AB1)):
            Wab, wk = self.w_get(pid)
            for jj in range(4):
                j = 4 * pi_ + jj
                ba, bb = self.bank(), self.bank()
                for kc in range(8):
                    self.mm(F[ba][:, 0:W], Wab[:, kc, jj * 128:(jj + 1) * 128], uT[:, kc, 0:W], kc == 0, kc == 7,
                            [("uT", k_) for k_ in range(8)] + [wk], [("F", ba)])
                for kc in range(8):
                    self.mm(F[bb][:, 0:W], Wab[:, kc, 512 + jj * 128:512 + (jj + 1) * 128], attnT[:, kc, 0:W], kc == 0, kc == 7,
                            [("at", k_) for k_ in range(8)] + [wk], [("F", bb)])
                t0, t1 = self.tmpm[j % 2], self.rec[j % 2]
                self.tt("dve", t0[:, 0:W], F[ba][:, 0:W], gT[:, j, 0:W], ALU.mult, [("F", ba), ("gT", j)], [("sacc", j % 2)])
                self.tt("dve", t1[:, 0:W], F[bb][:, 0:W], gT[:, 8 + j, 0:W], ALU.mult, [("F", bb), ("gT", 8 + j)], [("rec", j % 2)])
                self.tt("pool", bufA[:, j, 0:W], t0[:, 0:W], t1[:, 0:W], ALU.add, [("sacc", j % 2), ("rec", j % 2)],
                        [("bA", s) for s in range(NS)])
        S.soft_barrier()
        S.ph = kind_ + ".ffn"
        Wo, wk = self.w_get(P_O)
        for s in range(NS):
            for half in range(2):
                bk = self.bank()
                for kc in range(8):
                    self.mm(F[bk][:, 0:512], bufA[:, kc, s * 128:(s + 1) * 128], Wo[:, kc, half * 512:(half + 1) * 512],
                            kc == 0, kc == 7, [("bA", s), wk], [("F", bk)])
                self.tt("dve", X[:, s, half * 512:(half + 1) * 512], F[bk][:, 0:512], X[:, s, half * 512:(half + 1) * 512],
                        ALU.add, [("F", bk), ("X", s)], [("X", s)])
        for s in range(NS):
            self.act(self.junk, X[:, s, :], AF.Square, [("X", s)], ["junk", ("hss", s)], accum=st[:, 48 + s:49 + s])
            self.rstd(st[:, 48 + s:49 + s], st[:, 52 + s:53 + s], 1, 1.0 / D, ("hss", s), ("hrs", s))
            if s % 2 == 0:
                self.act(self.hnb[s % 2], X[:, s, :], AF.Copy, [("X", s), ("hrs", s)], [("hnb", s % 2)], scale=st[:, 52 + s:53 + s])
            else:
                self.ts("dve", self.hnb[s % 2], X[:, s, :], st[:, 52 + s:53 + s], None, ALU.mult, None,
                        [("X", s), ("hrs", s)], [("hnb", s % 2)])
            for hf in range(2):
                Tv, tk = self.tbank()
                for j in range(4):
                    kc = 4 * hf + j
                    self.tr(Tv[:, j * 128:(j + 1) * 128],
                            self.hnb[s % 2][:, kc * 128:(kc + 1) * 128], [("hnb", s % 2)], [tk])
                self.cp(self.evac_eng(), bufA[:, 4 * hf:4 * hf + 4, s * 128:(s + 1) * 128],
                        Tv.rearrange("p (a b) -> p a b", a=4), [tk], [("bA", s)])
        def v3(ap, lo, n):
            return ap[:, 0:nb * (wb + 2)].rearrange("p (a b) -> p a b", a=nb)[:, :, lo:lo + n]
        pairs = [(t, q) for t in range(6) for q in range(4 if t < 5 else 2)]
        wst_ = {"t": -1, "W": None, "k": None}
        cw = self.convw

        def stage_x(idx):
            t, q = pairs[idx]
            p = 4 * t + q
            sl = p % 2
            if t != wst_["t"]:
                wst_["W"], wst_["k"] = self.w_get(P_UP0 + t)
                wst_["t"] = t
            Wup, wk = wst_["W"], wst_["k"]
            bv, bg = self.bank(), self.bank()
            for kc in range(8):
                self.mm(F[bv][:, 0:W], Wup[:, kc, q * 256:q * 256 + 128], bufA[:, kc, 0:W], kc == 0, kc == 7,
                        bA_all + [wk], [("F", bv)])
            for kc in range(8):
                self.mm(F[bg][:, 0:W], Wup[:, kc, q * 256 + 128:q * 256 + 256], bufA[:, kc, 0:W], kc == 0, kc == 7,
                        bA_all + [wk], [("F", bg)])
            for k_, (bk, chunk) in enumerate(((bv, p), (bg, NPAIR + p))):
                U = self.U[sl][k_]
                uk = ("U", sl, k_)
                self.cp("act", v3(U, 2, wb), F[bk][:, 0:W].rearrange("p (a b) -> p a b", a=nb), [("F", bk)], [uk])
                self.ts("pool", v3(U, 0, 2), self.prevsave[:, chunk, 0:nb, :], hs_col, None, ALU.mult, None,
                        [("prev", chunk), "hscale", "onescol"], [uk])
                self.cp("pool", self.prevsave[:, chunk, 0:nb, :], v3(U, wb, 2), [uk], [("prev", chunk)])

        def stage_y1(idx):
            t, q = pairs[idx]
            p = 4 * t + q
            sl = p % 2
            for k_, chunk in enumerate((p, NPAIR + p)):
                U = self.U[sl][k_]
                uk = ("U", sl, k_)
                c = self.cv[sl][k_][:, 0:W].rearrange("p (a b) -> p a b", a=nb)
                ck = ("cv", sl, k_)
                self.act(c, v3(U, 0, wb), AF.Identity, [uk, "convw"], [ck], scale=cw[:, chunk, 0:1],
                         bias=cw[:, chunk, 3:4])
                self.stt(c, v3(U, 1, wb), cw[:, chunk, 1:2], c, ALU.mult, ALU.add, [uk, ck, "convw"], [ck])
                self.stt(c, v3(U, 2, wb), cw[:, chunk, 2:3], c, ALU.mult, ALU.add, [uk, ck, "convw"], [ck])

        def stage_y2(idx):
            t, q = pairs[idx]
            p = 4 * t + q
            sl = p % 2
            self.act(self.sg[sl][:, 0:W], self.cv[sl][1][:, 0:W], AF.Silu, [("cv", sl, 1)], [("sg", sl)])
            self.tt("dve", self.actT[:, p, 0:W], self.sg[sl][:, 0:W], self.cv[sl][0][:, 0:W], ALU.mult,
                    [("sg", sl), ("cv", sl, 0)], [("aT", p)])

        npr = len(pairs)
        stage_x(0)
        for i in range(npr):
            if i + 1 < npr:
                stage_x(i + 1)
            if not is_halo:
                stage_y1(i)
                if i >= 1:
                    stage_y2(i - 1)
        if not is_halo:
            stage_y2(npr - 1)
        if is_halo:
            return
        aT_all = [("aT", p) for p in range(NPAIR)]
        for half in range(2):
            for part in range(2):
                Wd, wk = self.w_get(P_D0 + 2 * half + part)
                for s in range(NS):
                    for kc in range(11):
                        self.mm(F[s][:, 0:512], self.actT[:, part * 11 + kc, s * 128:(s + 1) * 128], Wd[:, kc, 0:512],
                                part == 0 and kc == 0, part == 1 and kc == 10, [("aT", part * 11 + kc), wk], [("F", s)])
            for s in range(NS):
                self.tt("dve", X[:, s, half * 512:(half + 1) * 512], F[s][:, 0:512], X[:, s, half * 512:(half + 1) * 512],
                        ALU.add, [("F", s), ("X", s)], [("X", s)])
        for s in range(NS):
            self.act(self.junk, X[:, s, :], AF.Square, [("X", s)], ["junk", ("yss", s)], accum=st[:, 56 + s:57 + s])
            self.rstd(st[:, 56 + s:57 + s], st[:, 60 + s:61 + s], 1, 1.0 / D, ("yss", s), ("yrs", s))
            self.stt(self.Y[s % 2], X[:, s, :], st[:, 60 + s:61 + s], self.gfin, ALU.mult, ALU.mult,
                     [("X", s), ("yrs", s), "gfin"], [("Y", s % 2)])
            self.dma(y_out[s * 128:(s + 1) * 128, :], self.Y[s % 2], [("Y", s % 2)], [], ("Y", s % 2))
        if conv_out is not None:
            for seg in range(nb):
                for t_ in range(2):
                    self.dma(conv_out[seg][t_].rearrange("(c p) -> p c", p=128), self.prevsave[:, :, seg, t_],
                             [("prev", c_) for c_ in range(NFC)], [], ("convo", seg, t_), slow=True)

    def attention(self, groups):
        F = self.F
        QR, qnopeT, attnT, MK = self.QR, self.qnopeT, self.attnT, self.MK
        kvctr = 0
        uctr = 0
        for (c0, ncol, kvgroups) in groups:
            cols = slice(c0, c0 + ncol)
            nblk = sum(len(js) for _, js in kvgroups)
            for hp in range(4):
                pend = []

                def flush(pend_item):
                    (slot, j, hh, bi, pt) = pend_item
                    kv = self.KV[slot][:, j, :]
                    self.mm(F[hh][:, 0:ncol], kv[:, 384 + hh * 128:384 + (hh + 1) * 128], self.PT[pt][:, 0:ncol],
                            bi == 0, bi == nblk - 1, [("kv", slot), ("PT", pt)], [("F", hh)])
                    sa = self.sacc[hh]
                    if bi == 0:
                        self.cp("dve", sa[:, 0:ncol], self.PT[pt][:, 0:ncol], [("PT", pt)], [("sacc", hh)])
                    else:
                        self.tt("dve", sa[:, 0:ncol], sa[:, 0:ncol], self.PT[pt][:, 0:ncol], ALU.add,
                                [("PT", pt), ("sacc", hh)], [("sacc", hh)])

                bi = -1
                for (grp, js) in kvgroups:
                    slot = kvctr % self.NKV
                    kvctr += 1
                    self.dma(self.KV[slot], self.kvscr[hp, grp], [("kvs", grp, hp)], [("kv", slot)], ("kv", slot))
                    for (j, mu) in js:
                        bi += 1
                        for hh in range(2):
                            h = 2 * hp + hh
                            sb_ = 4 + (uctr % 4)
                            pt = uctr % 4
                            uctr += 1
                            kv = self.KV[slot][:, j, :]
                            masked = mu is not None
                            self.mm(F[sb_][:, 0:ncol], kv[:, hh * 128:(hh + 1) * 128], qnopeT[:, h, cols], True, False,
                                    [("kv", slot), ("qno", h)], [("F", sb_)])
                            self.mm(F[sb_][:, 0:ncol], kv[:, 256:384], QR[:, h, cols], False, not masked,
                                    [("kv", slot), ("QRr", h), ("QRb", h), "QRhi"], [("F", sb_)])
                            if masked:
                                self.mm(F[sb_][:, 0:ncol], MK[:, mu, :], QR[:, h, cols], False, True,
                                        ["MK", ("QRb", h)], [("F", sb_)])
                            if len(pend) >= 2:
                                flush(pend.pop(0))
                            self.act(self.PT[pt][:, 0:ncol], F[sb_][:, 0:ncol], AF.Exp, [("F", sb_)], [("PT", pt)],
                                     scale=ATT_SCALE)
                            pend.append((slot, j, hh, bi, pt))
                while pend:
                    flush(pend.pop(0))
                for hh in range(2):
                    h = 2 * hp + hh
                    rc = self.rec[hh]
                    self.mm(F[2 + hh][:, 0:ncol], self.ones32, self.sacc[hh][:, 0:ncol], True, True,
                            ["ones32", ("sacc", hh)], [("F", 2 + hh)])
                    self.S.op("dve", lambda e, rc=rc, hh=hh, ncol=ncol: e.reciprocal(out=rc[:, 0:ncol], in_=F[2 + hh][:, 0:ncol]),
                              reads=[("F", 2 + hh)], writes=[("rec", hh)])
                    self.tt("dve", attnT[:, h, cols], F[hh][:, 0:ncol], rc[:, 0:ncol], ALU.mult,
                            [("F", hh), ("rec", hh)], [("at", h)])

    def build(self):
        cfg = self.cfg
        self.declare()
        I, O = self.I, self.O
        tl = cfg.tiles()
        base_pieces = [P_U, P_V, P_Q, P_G0, P_G1, P_UQN, P_UQR, P_AB0, P_AB1, P_O] + [P_UP0 + t for t in range(6)]
        dn = [P_D0 + i for i in range(4)]
        for t in tl:
            self.wsched += base_pieces + ([] if t["kind"] == "halo" else dn)
        self.wsched += base_pieces + dn
        self.S.alias_names = {"xnb", "vg", "vb", "qnb", "bst", "qno", "QRr", "QRb", "QRhi", "at", "MK", "MK0", "r0", "r1",
                              "csq", "PT", "kv", "rec", "tm", "sacc", "hnb", "aT", "U", "cv", "sg", "Y"}
        self.setup()
        self.prepass()
        ones_col = self.stats[:, 95:96]
        for ti, t in enumerate(tl):
            W = t["W"]
            blocks = list(range(t["nkb"]))
            mku = {n: n - t["mlo"] for n in range(t["mlo"], t["nkb"])}
            halo = t["kind"] == "halo"
            if halo:
                hs = ones_col
            elif t["i"] == 0:
                hs = self.hscale[:, t["J"]:t["J"] + 1]
            else:
                hs = ones_col
            y_out = None
            conv_out = None
            if not halo:
                r0 = t["J"] * cfg.B + t["i"] * cfg.W
                y_out = O["y_own"][r0:r0 + W, :]
                if ti == len(tl) - 1:
                    conv_out = [O["conv_last"]]
            kvg = [(g_, [(j_, mku.get(4 * g_ + j_)) for j_ in range(4) if 4 * g_ + j_ < t["nkb"]])
                   for g_ in range((t["nkb"] + 3) // 4)]
            grp = [(W - 2, 2, kvg)] if halo else [(0, W, kvg)]
            self.tile(W, I["x_own"][t["tok0"]:t["tok0"] + W, :], I["csq"][:, :, t["tok0"]:t["tok0"] + W],
                      I["brow_p"], ti, grp, 1, hs, halo, False, y_out=y_out, conv_out=conv_out)
        self.dma(self.prevsave.rearrange("p a b c -> p (a b c)"), I["conv_state"], [],
                 [("prev", c_) for c_ in range(NFC)], "prevs")
        groups = []
        for b in range(cfg.NSB):
            kvg = [(cfg.NGP + b * cfg.NGC + g_, [(j_, None) for j_ in range(4)]) for g_ in range(cfg.NGC)]
            kvg.append((cfg.GRP_NEW, [(b // 2, b)]))
            groups.append((b * cfg.DS, cfg.DS, kvg))
        self.tile(cfg.WS, I["x_smp"], I["csq_s"], I["brow_s"], len(tl), groups, cfg.NSB, ones_col, False, True,
                  y_out=O["y_smp"], gv_out=O["gv_smp"], conv_out=[O["conv_smp"][b] for b in range(cfg.NSB)])
        self.S.barrier()
        nw = self.S.emit(self.sems)
        self.stats_info = dict(ops=len(self.S.ops), waits=nw, sbuf_words=self.A.hi, dsems=len(self.dsems))
        return self.nc


_PROG_CACHE = {}


def run_cfg(inputs, cfg):
    in_maps = host_prep(inputs, cfg)
    key = (cfg.SEQ, cfg.W, cfg.NJ, cfg.PAST, cfg.NSB, cfg.DS)
    if key not in _PROG_CACHE:
        p = Prog(cfg)
        p.build()
        _PROG_CACHE[key] = p
    p = _PROG_CACHE[key]
    res = run_bass_kernel_spmd(p.nc, in_maps, core_ids=list(range(8)))
    R = res.results
    nb_, SEQ = 2, cfg.SEQ
    y_p = np.zeros((nb_, SEQ, D), np.float32)
    ckv_p = np.zeros((1, nb_, SEQ, KVL), np.float32)
    kr_p = np.zeros((1, nb_, SEQ, ROPE), np.float32)
    cv_p = np.zeros((1, nb_, 2, 2 * DFF), np.float32)
    nsb = 8 * cfg.NSB
    y_s = np.zeros((nsb, cfg.DS, D), np.float32)
    ckv_s = np.zeros((1, nsb, cfg.DS, KVL), np.float32)
    kr_s = np.zeros((1, nsb, cfg.DS, ROPE), np.float32)
    cv_s = np.zeros((1, nsb, 2, 2 * DFF), np.float32)
    gv_s = np.zeros((1, nsb, cfg.DS, GW), np.float32)
    for c in range(8):
        b, r = c // 4, c % 4
        o = R[c]
        yo = np.asarray(o["y_own"], np.float32)
        for J in range(cfg.NJ):
            g = 4 * J + r
            y_p[b, g * cfg.B:(g + 1) * cfg.B] = yo[J * cfg.B:(J + 1) * cfg.B]
        if r == 0:
            ckv_p[0, b] = np.asarray(o["ckv_seq"], np.float32)
            kr_p[0, b] = np.asarray(o["kr_seq"], np.float32)
        if r == 3:
            cv_p[0, b] = np.asarray(o["conv_last"], np.float32)
        sl = slice(c * cfg.NSB, (c + 1) * cfg.NSB)
        y_s[sl] = np.asarray(o["y_smp"], np.float32).reshape(cfg.NSB, cfg.DS, D)
        ckv_s[0, sl] = np.asarray(o["ckv_smp"], np.float32).reshape(cfg.NSB, cfg.DS, KVL)
        kr_s[0, sl] = np.asarray(o["kr_smp"], np.float32).reshape(cfg.NSB, cfg.DS, ROPE)
        cv_s[0, sl] = np.asarray(o["conv_smp"], np.float32)
        gv_s[0, sl] = np.asarray(o["gv_smp"], np.float32).reshape(cfg.NSB, cfg.DS, GW)
    return (y_p, y_s, ckv_p, kr_p, cv_p, ckv_s, kr_s, cv_s, gv_s)


def kernel(**inputs):
    inputs = {k: np.asarray(v) for k, v in inputs.items()}
    cfg = Cfg(SEQ=inputs["x_prompt"].shape[1], W=512, NJ=inputs["x_prompt"].shape[1] // 4096,
              PAST=inputs["cache_mla_ckv"].shape[2], NSB=inputs["x_sample"].shape[0] // 8,
              DS=inputs["x_sample"].shape[1])
    return run_cfg(inputs, cfg)
```
